# Optimizing a Trainium2 kernel written in Bass

```python
import math
import jax, jax.numpy as jnp
from jax import lax
import numpy as np

D_MODEL = 1024
BATCH = 8
SEQ = 2048
DEPTH = 2
DEC_BATCH = 128
DEC_SEQ = 8
PAST_LEN = 16384
PAGE_SIZE = 128

D_INNER = 2 * D_MODEL
SSD_HEAD_DIM = 64
SSD_HEADS = D_INNER // SSD_HEAD_DIM
SSD_GROUPS = 8
SSD_HPG = SSD_HEADS // SSD_GROUPS
D_STATE = 128
CONV_W = 4
CONV_DIM = D_INNER + 2 * SSD_GROUPS * D_STATE
SSD_IN_DIM = D_INNER + CONV_DIM + SSD_HEADS
CHUNK = 128
POOL_WINDOWS = (2, 4, 8, 16)
N_POOL_GROUPS = len(POOL_WINDOWS)
D_POOL = D_MODEL
POOL_GROUP_DIM = D_POOL // N_POOL_GROUPS
MAX_WIN = max(POOL_WINDOWS)
D_FF = 2816
N_SSD_LAYERS = (DEPTH + 1) // 2
N_POOL_LAYERS = DEPTH // 2
EPS = 1e-6

kernel_name = 'ssd_pool_macaron_decode'


def rmsnorm(x, g):
    xf = x.astype(jnp.float32)
    y = xf * lax.rsqrt(jnp.mean(xf * xf, axis=-1, keepdims=True) + EPS)
    return (y * g.astype(jnp.float32)).astype(x.dtype)


def swiglu(h, w_gate, w_up, w_down):
    return (jax.nn.silu(h @ w_gate) * (h @ w_up)) @ w_down


def causal_dwconv(u, buf, w, b):
    l = u.shape[1]
    ext = jnp.concatenate([buf.astype(u.dtype), u], axis=1)
    y = b
    for k in range(CONV_W):
        y = y + ext[:, k:k + l] * w[k]
    return y, ext[:, l:]


def ssd_scan(x, dt, A, Bm, Cm, h0):
    f32 = jnp.float32
    bsz, l = x.shape[:2]
    q = min(CHUNK, l)
    nc = -(-l // q)
    pad = nc * q - l
    if pad:
        padw = lambda a: jnp.pad(a, [(0, 0), (0, pad)] + [(0, 0)] * (a.ndim - 2))
        x, dt, Bm, Cm = padw(x), padw(dt), padw(Bm), padw(Cm)
    xc = x.reshape(bsz, nc, q, SSD_GROUPS, SSD_HPG, SSD_HEAD_DIM).astype(f32)
    dtc = dt.reshape(bsz, nc, q, SSD_GROUPS, SSD_HPG).astype(f32)
    Bc = Bm.reshape(bsz, nc, q, SSD_GROUPS, D_STATE).astype(f32)
    Cc = Cm.reshape(bsz, nc, q, SSD_GROUPS, D_STATE).astype(f32)
    a_cum = jnp.cumsum(dtc * A.reshape(SSD_GROUPS, SSD_HPG), axis=2)
    seg = a_cum[:, :, :, None] - a_cum[:, :, None]
    tril = jnp.tril(jnp.ones((q, q), dtype=bool))[:, :, None, None]
    decay = jnp.exp(jnp.where(tril, seg, -jnp.inf))
    xdt = xc * dtc[..., None]
    cb = jnp.einsum('bclgn,bcsgn->bclsg', Cc, Bc)
    y_diag = jnp.einsum('bclsgr,bcsgrp->bclgrp', cb[..., None] * decay, xdt)
    decay_end = jnp.exp(a_cum[:, :, -1:] - a_cum)
    chunk_states = jnp.einsum('bcsgn,bcsgr,bcsgrp->bcgrpn', Bc, decay_end, xdt)
    chunk_decay = jnp.exp(a_cum[:, :, -1])

    def step(h, inp):
        st, dec = inp
        return h * dec[..., None, None] + st, h

    h_init = h0.reshape(bsz, SSD_GROUPS, SSD_HPG, SSD_HEAD_DIM, D_STATE).astype(f32)
    h_final, h_starts = lax.scan(step, h_init,
                                 (jnp.moveaxis(chunk_states, 1, 0), jnp.moveaxis(chunk_decay, 1, 0)))
    h_starts = jnp.moveaxis(h_starts, 0, 1)
    y_off = jnp.einsum('bclgn,bclgr,bcgrpn->bclgrp', Cc, jnp.exp(a_cum), h_starts)
    y = (y_diag + y_off).reshape(bsz, nc * q, SSD_HEADS, SSD_HEAD_DIM)[:, :l]
    return y, h_final.reshape(bsz, SSD_HEADS, SSD_HEAD_DIM, D_STATE)


def ssd_mixer(h, ssm0, conv0, w_in, conv_w, conv_b, dt_bias, a_log, d_skip, norm_g, w_out):
    f32 = jnp.float32
    bsz, l, _ = h.shape
    proj = h @ w_in
    z, xbc, dt_raw = jnp.split(proj, [D_INNER, D_INNER + CONV_DIM], axis=-1)
    xbc, conv_new = causal_dwconv(xbc, conv0, conv_w, conv_b)
    xbc = jax.nn.silu(xbc)
    xs, Bm, Cm = jnp.split(xbc, [D_INNER, D_INNER + SSD_GROUPS * D_STATE], axis=-1)
    xs = xs.reshape(bsz, l, SSD_HEADS, SSD_HEAD_DIM)
    Bm = Bm.reshape(bsz, l, SSD_GROUPS, D_STATE)
    Cm = Cm.reshape(bsz, l, SSD_GROUPS, D_STATE)
    dt = jax.nn.softplus((dt_raw + dt_bias).astype(f32))
    A = -jnp.exp(a_log.astype(f32))
    y, ssm_new = ssd_scan(xs, dt, A, Bm, Cm, ssm0)
    y = y + d_skip.astype(f32)[:, None] * xs.astype(f32)
    y = y.reshape(bsz, l, D_INNER) * jax.nn.silu(z.astype(f32))
    yg = y.reshape(bsz, l, SSD_GROUPS, D_INNER // SSD_GROUPS)
    yg = yg * lax.rsqrt(jnp.mean(yg * yg, axis=-1, keepdims=True) + EPS)
    y = (yg.reshape(bsz, l, D_INNER) * norm_g.astype(f32)).astype(h.dtype)
    return y @ w_out, ssm_new.astype(ssm0.dtype), conv_new.astype(conv0.dtype)


def pool_mixer(h, buf, pos0, w_in, w_group, scale, w_out):
    f32 = jnp.float32
    bsz, l, _ = h.shape
    u = h @ w_in
    ext = jnp.concatenate([buf.astype(u.dtype), u], axis=1)
    cs = jnp.cumsum(ext.astype(f32), axis=1)
    cs = jnp.concatenate([jnp.zeros((bsz, 1, D_POOL), f32), cs], axis=1)
    end = cs[:, MAX_WIN:MAX_WIN + l]
    pos = (pos0 + jnp.arange(l)).astype(f32)
    outs = []
    for k, w in enumerate(POOL_WINDOWS):
        sl = slice(k * POOL_GROUP_DIM, (k + 1) * POOL_GROUP_DIM)
        count = jnp.minimum(jnp.float32(w), pos + 1.0)[None, :, None]
        mean = (end[..., sl] - cs[:, MAX_WIN - w:MAX_WIN - w + l, sl]) / count
        outs.append(mean - u[..., sl].astype(f32))
    mixed = jnp.stack(outs, axis=2).astype(u.dtype)
    mixed = jnp.einsum('blgc,gcd->blgd', mixed, w_group).reshape(bsz, l, D_POOL)
    return (mixed * scale) @ w_out, ext[:, l:].astype(buf.dtype)


def trunk(x, ssm_st, conv_st, pool_st, pos0,
          ffn_norm, ffn_w_gate, ffn_w_up, ffn_w_down, mix_norm,
          ssd_w_in, ssd_conv_w, ssd_conv_b, ssd_dt_bias, ssd_a_log, ssd_d, ssd_norm, ssd_w_out,
          pool_w_in, pool_w_group, pool_scale, pool_w_out, final_norm):
    new_ssm, new_conv, new_pool = [], [], []
    for i in range(DEPTH):
        x = x + 0.5 * swiglu(rmsnorm(x, ffn_norm[i, 0]), ffn_w_gate[i, 0], ffn_w_up[i, 0], ffn_w_down[i, 0])
        h = rmsnorm(x, mix_norm[i])
        j = i // 2
        if i % 2 == 0:
            m, s_new, c_new = ssd_mixer(h, ssm_st[j], conv_st[j], ssd_w_in[j], ssd_conv_w[j], ssd_conv_b[j],
                                        ssd_dt_bias[j], ssd_a_log[j], ssd_d[j], ssd_norm[j], ssd_w_out[j])
            new_ssm.append(s_new)
            new_conv.append(c_new)
        else:
            m, p_new = pool_mixer(h, pool_st[j], pos0, pool_w_in[j], pool_w_group[j], pool_scale[j], pool_w_out[j])
            new_pool.append(p_new)
        x = x + m
        x = x + 0.5 * swiglu(rmsnorm(x, ffn_norm[i, 1]), ffn_w_gate[i, 1], ffn_w_up[i, 1], ffn_w_down[i, 1])
    return rmsnorm(x, final_norm), jnp.stack(new_ssm), jnp.stack(new_conv), jnp.stack(new_pool)


def setup_inputs(seed: int = 0) -> dict:
    key = jax.random.key(seed)
    ks = jax.random.split(key, 24)
    f32 = jnp.float32

    def nrm(k, shape, scale):
        return jax.random.normal(k, shape, f32) * scale

    dt0 = jnp.exp(jax.random.uniform(ks[13], (N_SSD_LAYERS, SSD_HEADS), f32, math.log(1e-3), math.log(0.1)))
    return {
        'x_prompt': nrm(ks[0], (BATCH, SEQ, D_MODEL), 1.0),
        'x_sample': nrm(ks[1], (DEC_BATCH, DEC_SEQ, D_MODEL), 1.0),
        'state_ssm': nrm(ks[2], (N_SSD_LAYERS, DEC_BATCH, SSD_HEADS, SSD_HEAD_DIM, D_STATE), 0.1),
        'state_conv': nrm(ks[3], (N_SSD_LAYERS, DEC_BATCH, CONV_W - 1, CONV_DIM), 1.0),
        'state_pool': nrm(ks[4], (N_POOL_LAYERS, DEC_BATCH, MAX_WIN - 1, D_POOL), 1.0),
        'ffn_norm': 1.0 + nrm(ks[5], (DEPTH, 2, D_MODEL), 0.02),
        'ffn_w_gate': nrm(ks[6], (DEPTH, 2, D_MODEL, D_FF), D_MODEL ** -0.5),
        'ffn_w_up': nrm(ks[7], (DEPTH, 2, D_MODEL, D_FF), D_MODEL ** -0.5),
        'ffn_w_down': nrm(ks[8], (DEPTH, 2, D_FF, D_MODEL), D_FF ** -0.5),
        'mix_norm': 1.0 + nrm(ks[9], (DEPTH, D_MODEL), 0.02),
        'ssd_w_in': nrm(ks[10], (N_SSD_LAYERS, D_MODEL, SSD_IN_DIM), D_MODEL ** -0.5),
        'ssd_conv_w': nrm(ks[11], (N_SSD_LAYERS, CONV_W, CONV_DIM), CONV_W ** -0.5),
        'ssd_conv_b': nrm(ks[12], (N_SSD_LAYERS, CONV_DIM), 0.02),
        'ssd_dt_bias': dt0 + jnp.log(-jnp.expm1(-dt0)),
        'ssd_a_log': jnp.log(jax.random.uniform(ks[14], (N_SSD_LAYERS, SSD_HEADS), f32, 1.0, 16.0)),
        'ssd_d': 1.0 + nrm(ks[15], (N_SSD_LAYERS, SSD_HEADS), 0.1),
        'ssd_norm': 1.0 + nrm(ks[16], (N_SSD_LAYERS, D_INNER), 0.02),
        'ssd_w_out': nrm(ks[17], (N_SSD_LAYERS, D_INNER, D_MODEL), D_INNER ** -0.5),
        'pool_w_in': nrm(ks[18], (N_POOL_LAYERS, D_MODEL, D_POOL), D_MODEL ** -0.5),
        'pool_w_group': nrm(ks[19], (N_POOL_LAYERS, N_POOL_GROUPS, POOL_GROUP_DIM, POOL_GROUP_DIM), POOL_GROUP_DIM ** -0.5),
        'pool_scale': 1.0 + nrm(ks[20], (N_POOL_LAYERS, D_POOL), 0.1),
        'pool_w_out': nrm(ks[21], (N_POOL_LAYERS, D_POOL, D_MODEL), D_POOL ** -0.5),
        'final_norm': 1.0 + nrm(ks[22], (D_MODEL,), 0.02),
    }


def reference(x_prompt, x_sample, state_ssm, state_conv, state_pool,
              ffn_norm, ffn_w_gate, ffn_w_up, ffn_w_down, mix_norm,
              ssd_w_in, ssd_conv_w, ssd_conv_b, ssd_dt_bias, ssd_a_log, ssd_d, ssd_norm, ssd_w_out,
              pool_w_in, pool_w_group, pool_scale, pool_w_out, final_norm):
    bp = x_prompt.shape[0]
    ssm0 = jnp.zeros((N_SSD_LAYERS, bp, SSD_HEADS, SSD_HEAD_DIM, D_STATE), state_ssm.dtype)
    conv0 = jnp.zeros((N_SSD_LAYERS, bp, CONV_W - 1, CONV_DIM), state_conv.dtype)
    pool0 = jnp.zeros((N_POOL_LAYERS, bp, MAX_WIN - 1, D_POOL), state_pool.dtype)
    y_prompt, ssm_p, conv_p, pool_p = trunk(
        x_prompt, ssm0, conv0, pool0, 0,
        ffn_norm, ffn_w_gate, ffn_w_up, ffn_w_down, mix_norm,
        ssd_w_in, ssd_conv_w, ssd_conv_b, ssd_dt_bias, ssd_a_log, ssd_d, ssd_norm, ssd_w_out,
        pool_w_in, pool_w_group, pool_scale, pool_w_out, final_norm)
    y_sample, ssm_s, conv_s, pool_s = trunk(
        x_sample, state_ssm, state_conv, state_pool, PAST_LEN,
        ffn_norm, ffn_w_gate, ffn_w_up, ffn_w_down, mix_norm,
        ssd_w_in, ssd_conv_w, ssd_conv_b, ssd_dt_bias, ssd_a_log, ssd_d, ssd_norm, ssd_w_out,
        pool_w_in, pool_w_group, pool_scale, pool_w_out, final_norm)
    return (y_prompt, y_sample, ssm_p, conv_p, pool_p, ssm_s, conv_s, pool_s)
```

```python
import numpy as np
import concourse.bass as bass
import concourse.mybir as mybir
from concourse.bass_utils import run_bass_kernel_spmd

F32 = mybir.dt.float32
BF16 = mybir.dt.bfloat16
AF = mybir.ActivationFunctionType
ALU = mybir.AluOpType

NCORES = 8
import os as _os
NTOK = 2176
BLOCKS = [(0, 512), (512, 512), (1024, 512), (1536, 512), (2048, 128)]
D_FF = 2816
EPS = 1e-6
POOL_WINDOWS = (2, 4, 8, 16)

C_FN, C_MN, C_FIN, C_CW, C_CB, C_DD, C_SN, C_PS, NCOL = 0, 32, 48, 56, 184, 216, 232, 248, 256
K_UP, K_US, K_NEGP, K_NEGS, K_ONES, K_BD, K_SEQM, K_INVC, K_ID, NCONST = 0, 128, 256, 384, 512, 640, 768, 784, 848, 976


class Op:
    __slots__ = ("eng", "fn", "waits", "signaled", "sigval", "pos", "clock", "is_dma", "dsem", "dval")


class Sched:
    ENGS = ("pe", "act", "dve", "pool", "sp")

    def __init__(self, nc):
        self.nc = nc
        self.ops = []
        self.streams = {e: [] for e in self.ENGS}
        self.clock = {e: {} for e in self.ENGS}
        self.dma_waited = {e: {} for e in self.ENGS}
        self.last_writer = {}
        self.readers = {}
        self.dsem_total = {}
        self.pending = {e: [] for e in self.ENGS}

    def add(self, eng, fn, reads=(), writes=(), dsem=None):
        op = Op()
        op.eng = eng; op.fn = fn; op.waits = []; op.signaled = False; op.sigval = None
        op.is_dma = dsem is not None; op.dsem = dsem; op.dval = None
        op.pos = len(self.streams[eng])
        deps = []
        for k in reads:
            lw = self.last_writer.get(k)
            if lw is not None:
                deps.append(lw)
            if k in ("pt", "psn") or (isinstance(k, tuple) and k[0] in ("ps", "pg", "pu", "pd")):
                deps.extend(r for r in self.readers.get(k, ()) if r.eng != eng)
        for k in writes:
            lw = self.last_writer.get(k)
            if lw is not None:
                deps.append(lw)
            deps.extend(self.readers.get(k, ()))
        forced = self.pending[eng]
        if forced:
            deps.extend(forced)
            self.pending[eng] = []
        clk = self.clock[eng]
        seen = set()
        deps.sort(key=lambda d_: -d_.pos)
        for d in deps:
            if id(d) in seen or d is op:
                continue
            seen.add(id(d))
            if d.is_dma:
                tot = self.dsem_total[d.dsem]
                if self.dma_waited[eng].get(d.dsem, 0) < tot:
                    op.waits.append(("dma", d.dsem, tot))
                    self.dma_waited[eng][d.dsem] = tot
            else:
                if d.eng == eng and (eng == "pe" or op.pos - d.pos > 6):
                    continue
                if clk.get(d.eng, -1) >= d.pos:
                    continue
                op.waits.append(("eng", d))
                d.signaled = True
                for e2, p2 in d.clock.items():
                    if clk.get(e2, -1) < p2:
                        clk[e2] = p2
                if clk.get(d.eng, -1) < d.pos:
                    clk[d.eng] = d.pos
        if op.is_dma:
            self.dsem_total[dsem] = self.dsem_total.get(dsem, 0) + 16
            op.dval = self.dsem_total[dsem]
            op.clock = None
        else:
            op.clock = dict(clk)
        for k in reads:
            self.readers.setdefault(k, []).append(op)
        for k in writes:
            self.last_writer[k] = op
            self.readers[k] = []
        self.ops.append(op)
        self.streams[eng].append(op)
        return op

    def barrier(self):
        lasts = []
        for e in self.ENGS:
            st = [o for o in self.streams[e][-1:] if not o.is_dma]
            for o in reversed(self.streams[e]):
                if not o.is_dma:
                    lasts.append(o)
                    break
        dmas = {}
        for o in self.ops:
            if o.is_dma:
                dmas[o.dsem] = o
        for e in self.ENGS:
            self.pending[e] = [o for o in lasts if o.eng != e] + list(dmas.values())

    def emit(self, out_dsems=()):
        nc = self.nc
        engobj = {"pe": nc.tensor, "act": nc.scalar, "dve": nc.vector, "pool": nc.gpsimd, "sp": nc.sync}
        esem = {e: nc.alloc_semaphore("es_" + e) for e in self.ENGS}
        dsems = {}
        for op in self.ops:
            if op.is_dma and op.dsem not in dsems:
                dsems[op.dsem] = nc.alloc_semaphore("ds_%d" % len(dsems))
        for e in self.ENGS:
            n = 0
            for op in self.streams[e]:
                if op.signaled:
                    n += 1
                    op.sigval = n
        for op in self.ops:
            eo = engobj[op.eng]
            for w in op.waits:
                if w[0] == "dma":
                    eo.wait_ge(dsems[w[1]], w[2])
                else:
                    eo.wait_ge(esem[w[1].eng], w[1].sigval)
            inst = op.fn(eo)
            if op.is_dma:
                inst.then_inc(dsems[op.dsem], 16)
            elif op.signaled:
                inst.then_inc(esem[op.eng], 1)
        for ds in out_dsems:
            if ds in dsems:
                nc.sync.wait_ge(dsems[ds], self.dsem_total[ds])


class Arena:
    def __init__(self, flat, nwords):
        self.flat = flat
        self.n = nwords
        self.off = 0

    def _take(self, words):
        words = (words + 7) // 8 * 8
        o = self.off
        self.off += words
        assert self.off <= self.n, ("arena overflow", self.off, self.n)
        return o

    @staticmethod
    def _shape(v, shape):
        if len(shape) == 1:
            return v
        if len(shape) == 2:
            return v.rearrange("p (a b) -> p a b", a=shape[0])
        if len(shape) == 3:
            return v.rearrange("p (a b c) -> p a b c", a=shape[0], b=shape[1])
        raise ValueError(shape)

    def f32(self, *shape):
        n = int(np.prod(shape))
        o = self._take(n)
        return self._shape(self.flat[:, o:o + n], shape)

    def bf16(self, *shape):
        n = int(np.prod(shape))
        assert n % 2 == 0
        o = self._take(n // 2)
        return self._shape(self.flat[:, o:o + n // 2].bitcast(BF16), shape)


def bcast_mid(ap2d, rep):
    a = ap2d.ap
    assert len(a) == 2
    return bass.AP(ap2d.tensor, ap2d.offset, [list(a[0]), [0, rep], list(a[1])])


def bcast_last(ap2d, rep):
    a = ap2d.ap
    assert len(a) == 2
    return bass.AP(ap2d.tensor, ap2d.offset, [list(a[0]), list(a[1]), [0, rep]])


def bcast_col(ap_col, rep):
    a = ap_col.ap
    return bass.AP(ap_col.tensor, ap_col.offset, [list(a[0]), [0, rep]])


def build_program(do_ffn=True, do_ssd=True, do_pool=True):
    nc = bass.Bass("TRN2", target_bir_lowering=False)

    def din(name, shape):
        return nc.dram_tensor(name, list(shape), F32, kind="ExternalInput").ap()

    def dout(name, shape):
        return nc.dram_tensor(name, list(shape), F32, kind="ExternalOutput").ap()

    xT = din("xT", [1024, NTOK])
    ssmT = din("ssmT", [8, 128, 16 * 256])
    convT = din("convT", [128, 32 * 48])
    poolT = din("poolT", [1024, 16, 15])
    cols_d = din("cols", [128, NCOL])
    rows_d = din("rows", [128, 64])
    const_d = din("consts", [128, NCONST])
    w_gate = din("ffn_w_gate", [2, 2, 1024, D_FF])
    w_up = din("ffn_w_up", [2, 2, 1024, D_FF])
    w_down = din("ffn_w_down", [2, 2, D_FF, 1024])
    ssd_w_in = din("ssd_w_in", [1024, 6176])
    ssd_w_out = din("ssd_w_out", [2048, 1024])
    pool_w_in = din("pool_w_in", [1024, 1024])
    pool_w_group = din("pool_w_group", [4, 256, 256])
    pool_w_out = din("pool_w_out", [1024, 1024])

    yT = dout("yT", [1024, NTOK])
    ssm_p_out = dout("ssm_p_out", [128, 2048])
    conv_p_out = dout("conv_p_out", [128, 32 * 3])
    pool_p_out = dout("pool_p_out", [1024, 15])
    ssm_s_out = dout("ssm_s_out", [8, 128, 16 * 256])
    conv_s_out = dout("conv_s_out", [128, 32 * 48])
    pool_s_out = dout("pool_s_out", [1024, 16, 15])

    def sb(name, shape, dt):
        return nc.alloc_sbuf_tensor(name, list(shape), dt).ap()

    x = sb("x", [128, 8, NTOK], F32)
    wA = sb("wA", [128, 8, 1024], BF16)
    wBf = sb("wB", [128, 12288], BF16)
    cols = sb("colsb", [128, NCOL], F32)
    rows = sb("rowsb", [128, 64], F32)
    consts = sb("constsb", [128, NCONST], F32)
    ident_bf = sb("ident_bf", [128, 128], BF16)
    ones_bf = sb("ones_bf", [128, 128], BF16)
    a_row = sb("a_row", [128, 32], F32)
    wdt = sb("wdt", [128, 8, 32], BF16)
    wgrp = sb("wgrp", [128, 8, 256], BF16)
    UW = 22944
    un = sb("union", [128, UW], F32)

    PS = [nc.alloc_psum_tensor("ps%d" % i, [128, 512], F32).ap() for i in range(7)]
    PT = nc.alloc_psum_tensor("pst", [128, 1024], BF16).ap()

    S = Sched(nc)
    out_dsems = []

    S.add("sp", lambda e: e.dma_start(out=cols, in_=cols_d), writes=["cols"], dsem="cols")
    S.add("sp", lambda e: e.dma_start(out=rows, in_=rows_d), writes=["rows"], dsem="rows")
    S.add("sp", lambda e: e.dma_start(out=consts, in_=const_d), writes=["consts"], dsem="consts")
    xTv = xT.rearrange("(k p) t -> p k t", p=128)
    for k in range(8):
        S.add("sp", lambda e, k=k: e.dma_start(out=x[:, k, :], in_=xTv[:, k, :]), writes=[("x", k, b) for b in range(5)], dsem="xin")
    S.add("dve", lambda e: e.tensor_copy(out=ident_bf, in_=consts[:, K_ID:K_ID + 128]), reads=["consts"], writes=["ident"])
    S.add("dve", lambda e: e.tensor_copy(out=ones_bf, in_=consts[:, K_ONES:K_ONES + 128]), reads=["consts"], writes=["ones"])

    def col(off, k):
        return cols[:, off + k:off + k + 1]

    nrm_ctr = [0]

    def rmsnorm(ar_sq, ar_rstd, goff, bi, out_fn, out_keys_fn, ps_norm, ps_keys=("psn",), span=None):
        c0, n = BLOCKS[bi] if span is None else span
        for k in range(8):
            i = 0 if ar_sq[0] is ar_sq[1] else nrm_ctr[0] % 2
            nrm_ctr[0] += 1
            sq = ar_sq[i]
            S.add("act", lambda e, k=k, sq=sq: e.activation(out=sq[:, 0:n], in_=x[:, k, c0:c0 + n], func=AF.Square),
                  reads=[("x", k, bi)], writes=[("sq", i)])
            S.add("pe", lambda e, k=k, sq=sq: e.matmul(ps_norm[:, 0:n], lhsT=ones_bf, rhs=sq[:, 0:n], start=(k == 0), stop=(k == 7)),
                  reads=[("sq", i), "ones"], writes=list(ps_keys))
        S.add("act", lambda e: e.activation(out=ar_rstd[:, 0:n], in_=ps_norm[:, 0:n], func=AF.Ln, bias=EPS, scale=1.0 / 1024.0),
              reads=list(ps_keys), writes=["rstd"])
        S.add("act", lambda e: e.activation(out=ar_rstd[:, 0:n], in_=ar_rstd[:, 0:n], func=AF.Exp, scale=-0.5), reads=["rstd"], writes=["rstd"])
        for k in range(8):
            S.add("dve", lambda e, k=k: e.scalar_tensor_tensor(out=out_fn(k), in0=x[:, k, c0:c0 + n], scalar=col(goff, k),
                                                               in1=ar_rstd[:, 0:n], op0=ALU.mult, op1=ALU.mult),
                  reads=[("x", k, bi), "rstd", "cols"], writes=out_keys_fn(k))

    SPLITS = [(0, 4), (4, 4), (8, 3)]

    def ffn_phase(i, j):
        ar = Arena(un, UW)
        xn = ar.bf16(8, NTOK)
        h = ar.bf16(8, NTOK)
        sq = [ar.bf16(512), ar.bf16(512)]
        rstd = ar.f32(512)
        sg = [ar.bf16(512), ar.bf16(512)]
        goff = C_FN + (i * 2 + j) * 8
        for bi in range(5):
            c0, n = BLOCKS[bi]
            rmsnorm(sq, rstd, goff, bi, lambda k, c0=c0, n=n: xn[:, k, c0:c0 + n], lambda k, bi=bi: [("xn", k, bi)], PS[6])
        wgd = w_gate[i, j]
        wud = w_up[i, j]
        wdd = w_down[i, j]
        wslots = [wBf[:, s * 2048:(s + 1) * 2048].rearrange("p (k n) -> p k n", k=8) for s in range(4)]
        ev = 0

        def load_gu(cc):
            sg_slot = (cc % 2) * 2
            su_slot = sg_slot + 1
            wg_s = wslots[sg_slot]
            wu_s = wslots[su_slot]
            S.add("pool", lambda e: e.dma_start(out=wg_s, in_=wgd[:, cc * 256:(cc + 1) * 256].rearrange("(k p) n -> p k n", p=128)),
                  writes=[("wB", sg_slot)], dsem=("wB", sg_slot))
            S.add("pool", lambda e: e.dma_start(out=wu_s, in_=wud[:, cc * 256:(cc + 1) * 256].rearrange("(k p) n -> p k n", p=128)),
                  writes=[("wB", su_slot)], dsem=("wB", su_slot))

        load_gu(0)
        for (cc0, ncc) in SPLITS:
            nh = ncc * 2
            r0 = cc0 * 256
            S.add("pool", lambda e, r0=r0, nh=nh: e.dma_start(out=wA[:, 0:nh, :], in_=wdd[r0:r0 + nh * 128, :].rearrange("(k p) n -> p k n", p=128)),
                  writes=["wA"], dsem="wA")
            for cc in range(cc0, cc0 + ncc):
                sg_slot = (cc % 2) * 2
                su_slot = sg_slot + 1
                wg_s = wslots[sg_slot]
                wu_s = wslots[su_slot]
                if cc + 1 < 11:
                    load_gu(cc + 1)
                for sub in range(2):
                    hc = (cc - cc0) * 2 + sub
                    for bi in range(5):
                        c0, n = BLOCKS[bi]
                        pg = PS[ev % 2]
                        pu = PS[2 + ev % 2]
                        sgt = sg[ev % 2]
                        evi = ev % 2
                        ev += 1
                        for k in range(8):
                            S.add("pe", lambda e, k=k, pg=pg, wg_s=wg_s, sub=sub, c0=c0, n=n: e.matmul(
                                pg[:, 0:n], lhsT=wg_s[:, k, sub * 128:(sub + 1) * 128], rhs=xn[:, k, c0:c0 + n], start=(k == 0), stop=(k == 7)),
                                reads=[("wB", sg_slot), ("xn", k, bi)], writes=[("pg", evi)])
                        for k in range(8):
                            S.add("pe", lambda e, k=k, pu=pu, wu_s=wu_s, sub=sub, c0=c0, n=n: e.matmul(
                                pu[:, 0:n], lhsT=wu_s[:, k, sub * 128:(sub + 1) * 128], rhs=xn[:, k, c0:c0 + n], start=(k == 0), stop=(k == 7)),
                                reads=[("wB", su_slot), ("xn", k, bi)], writes=[("pu", evi)])
                        S.add("act", lambda e, pg=pg, sgt=sgt, n=n: e.activation(out=sgt[:, 0:n], in_=pg[:, 0:n], func=AF.Silu),
                              reads=[("pg", evi)], writes=[("sg", evi)])
                        S.add("dve", lambda e, pu=pu, sgt=sgt, hc=hc, c0=c0, n=n: e.tensor_tensor(
                            out=h[:, hc, c0:c0 + n], in0=sgt[:, 0:n], in1=pu[:, 0:n], op=ALU.mult),
                            reads=[("sg", evi), ("pu", evi)], writes=[("h", hc, bi)])
            dctr = 0
            for bi in range(5):
                c0, n = BLOCKS[bi]
                for m in range(8):
                    pd = PS[4 + dctr % 2]
                    di = dctr % 2
                    dctr += 1
                    for kk in range(nh):
                        S.add("pe", lambda e, kk=kk, pd=pd, m=m, c0=c0, n=n, nh=nh: e.matmul(
                            pd[:, 0:n], lhsT=wA[:, kk, m * 128:(m + 1) * 128], rhs=h[:, kk, c0:c0 + n], start=(kk == 0), stop=(kk == nh - 1)),
                            reads=["wA", ("h", kk, bi)], writes=[("pd", di)])
                    S.add("dve", lambda e, pd=pd, m=m, c0=c0, n=n: e.scalar_tensor_tensor(
                        out=x[:, m, c0:c0 + n], in0=pd[:, 0:n], scalar=0.5, in1=x[:, m, c0:c0 + n], op0=ALU.mult, op1=ALU.add),
                        reads=[("pd", di), ("x", m, bi)], writes=[("x", m, bi)])

    def pool_phase():
        ar = Arena(un, UW)
        xn = ar.bf16(8, NTOK)
        mixed = ar.bf16(8, NTOK)
        sq = [ar.bf16(512), ar.bf16(512)]
        rstd = ar.f32(512)
        ext_p = [ar.f32(528) for _ in range(2)]
        ext_s = [ar.f32(16, 23) for _ in range(2)]
        tp = [ar.f32(528) for _ in range(2)]
        ts_ = [ar.f32(16, 23) for _ in range(2)]
        fix = ar.f32(16)
        goff = C_MN + 8
        S.add("pool", lambda e: e.dma_start(out=wA, in_=pool_w_in.rearrange("(k p) n -> p k n", p=128)), writes=["wA"], dsem="wA")
        wo = wBf[:, 0:8192].rearrange("p (k n) -> p k n", k=8)
        S.add("pool", lambda e: e.dma_start(out=wo, in_=pool_w_out.rearrange("(k p) n -> p k n", p=128)),
              writes=[("wB", s_) for s_ in range(4)], dsem=("wB", 0))
        S.add("pool", lambda e: e.dma_start(out=wgrp.rearrange("p (g kk) n -> p g kk n", g=4), in_=pool_w_group.rearrange("g (kk p) n -> p g kk n", p=128)),
              writes=["wgrp"], dsem="wgrp")
        for bi in range(5):
            c0, n = BLOCKS[bi]
            rmsnorm(sq, rstd, goff, bi, lambda k, c0=c0, n=n: xn[:, k, c0:c0 + n], lambda k, bi=bi: [("xn", k, bi)], PS[6])
        ev = 0
        ectr = 0
        for m in range(8):
            widx = m // 2
            w = POOL_WINDOWS[widx]
            prev = None
            for bi in range(5):
                c0, n = BLOCKS[bi]
                pp = PS[ev % 2]
                pi = ev % 2
                ev += 1
                for k in range(8):
                    S.add("pe", lambda e, k=k, pp=pp, m=m, c0=c0, n=n: e.matmul(
                        pp[:, 0:n], lhsT=wA[:, k, m * 128:(m + 1) * 128], rhs=xn[:, k, c0:c0 + n], start=(k == 0), stop=(k == 7)),
                        reads=["wA", ("xn", k, bi)], writes=[("pg", pi)])
                es = ectr % 2
                ectr += 1
                if bi < 4:
                    ep = ext_p[es]
                    ekey = ("extp", es)
                    if bi == 0:
                        S.add("pool", lambda e, ep=ep: e.memset(ep[:, 0:15], 0.0), writes=[ekey])
                    else:
                        S.add("pool", lambda e, ep=ep, prev=prev: e.tensor_copy(out=ep[:, 0:15], in_=prev[:, 512:527]), reads=[("extp", 1 - es)], writes=[ekey])
                    S.add("act", lambda e, pp=pp, ep=ep: e.activation(out=ep[:, 15:527], in_=pp[:, 0:512], func=AF.Copy),
                          reads=[("pg", pi)], writes=[ekey])
                    prev = ep
                    if bi == 3:
                        S.add("sp", lambda e, m=m, ep=ep: e.dma_start(out=pool_p_out[m * 128:(m + 1) * 128, :], in_=ep[:, 512:527]),
                              reads=[ekey], dsem="pool_p_out")
                    src = ep
                    skey = ekey
                    hi = 527
                    sl = lambda a, lo_, hi_: a[:, lo_:hi_]
                    tmps = tp
                    tkey = "tp"
                else:
                    esm = ext_s[es]
                    ekey = ("exts", es)
                    for bq in range(2):
                        S.add("sp", lambda e, m=m, esm=esm, bq=bq: e.dma_start(out=esm[:, bq * 8:(bq + 1) * 8, 0:15], in_=poolT[m * 128:(m + 1) * 128, bq * 8:(bq + 1) * 8, :]),
                              writes=[ekey], dsem=("exts", es))
                    S.add("act", lambda e, pp=pp, esm=esm: e.activation(out=esm[:, :, 15:23], in_=pp[:, 0:128].rearrange("p (b t) -> p b t", b=16), func=AF.Copy),
                          reads=[("pg", pi)], writes=[ekey])
                    for bq in range(2):
                        S.add("sp", lambda e, m=m, esm=esm, bq=bq: e.dma_start(out=pool_s_out[m * 128:(m + 1) * 128, bq * 8:(bq + 1) * 8, :], in_=esm[:, bq * 8:(bq + 1) * 8, 8:23]),
                              reads=[ekey], dsem="pool_s_out")
                    src = esm
                    skey = ekey
                    hi = 23
                    sl = lambda a, lo_, hi_: a[:, :, lo_:hi_]
                    tmps = ts_
                    tkey = "ts"
                base = src
                lo = 0
                step = 1
                lvl = 0
                while step < w:
                    dst = tmps[lvl % 2]
                    nlo = lo + step
                    S.add("pool", lambda e, dst=dst, src=src, nlo=nlo, step=step, hi=hi, sl=sl: e.tensor_tensor(
                        out=sl(dst, nlo, hi), in0=sl(src, nlo, hi), in1=sl(src, nlo - step, hi - step), op=ALU.add),
                        reads=[skey], writes=[(tkey, lvl % 2)])
                    src = dst
                    skey = (tkey, lvl % 2)
                    lo = nlo
                    step *= 2
                    lvl += 1
                if bi < 4:
                    S.add("dve", lambda e, src=src, base=base, m=m, w=w, c0=c0: e.scalar_tensor_tensor(
                        out=mixed[:, m, c0:c0 + 512], in0=src[:, 15:527], scalar=1.0 / w, in1=base[:, 15:527], op0=ALU.mult, op1=ALU.subtract),
                        reads=[skey, ekey], writes=[("mixed", m, bi)])
                    if bi == 0:
                        S.add("dve", lambda e, src=src, widx=widx: e.tensor_tensor(
                            out=fix, in0=src[:, 15:31], in1=consts[:, K_INVC + widx * 16:K_INVC + widx * 16 + 16], op=ALU.mult),
                            reads=[skey, "consts"], writes=["fix"])
                        S.add("dve", lambda e, base=base, m=m: e.tensor_tensor(out=mixed[:, m, 0:16], in0=fix, in1=base[:, 15:31], op=ALU.subtract),
                              reads=["fix", ekey], writes=[("mixed", m, 0)])
                else:
                    S.add("dve", lambda e, src=src, base=base, m=m, w=w: e.scalar_tensor_tensor(
                        out=mixed[:, m, 2048:2176].rearrange("p (b t) -> p b t", b=16), in0=src[:, :, 15:23], scalar=1.0 / w, in1=base[:, :, 15:23],
                        op0=ALU.mult, op1=ALU.subtract),
                        reads=[skey, ekey], writes=[("mixed", m, 4)])
        wg4 = wgrp.rearrange("p (g kk) n -> p g kk n", g=4)
        mg = xn
        ev = 0
        for m in range(8):
            g, mm = m // 2, m % 2
            for bi in range(5):
                c0, n = BLOCKS[bi]
                pp = PS[ev % 2]
                pi = ev % 2
                ev += 1
                for kk in range(2):
                    S.add("pe", lambda e, kk=kk, pp=pp, g=g, mm=mm, c0=c0, n=n: e.matmul(
                        pp[:, 0:n], lhsT=wg4[:, g, kk, mm * 128:(mm + 1) * 128], rhs=mixed[:, 2 * g + kk, c0:c0 + n], start=(kk == 0), stop=(kk == 1)),
                        reads=["wgrp", ("mixed", 2 * g + kk, bi)], writes=[("pg", pi)])
                S.add("act", lambda e, pp=pp, m=m, c0=c0, n=n: e.activation(out=mg[:, m, c0:c0 + n], in_=pp[:, 0:n], func=AF.Identity, scale=col(C_PS, m)),
                      reads=[("pg", pi), "cols"] + [("xn", k, bi) for k in range(8)], writes=[("mg", m, bi), ("xn", m, bi)])
        ev = 0
        for bi in range(5):
            c0, n = BLOCKS[bi]
            for m in range(8):
                pd = PS[4 + ev % 2]
                di = ev % 2
                ev += 1
                for k in range(8):
                    S.add("pe", lambda e, k=k, pd=pd, m=m, c0=c0, n=n: e.matmul(
                        pd[:, 0:n], lhsT=wo[:, k, m * 128:(m + 1) * 128], rhs=mg[:, k, c0:c0 + n], start=(k == 0), stop=(k == 7)),
                        reads=[("wB", 0), ("mg", k, bi)], writes=[("pd", di)])
                S.add("dve", lambda e, pd=pd, m=m, c0=c0, n=n: e.tensor_tensor(out=x[:, m, c0:c0 + n], in0=x[:, m, c0:c0 + n], in1=pd[:, 0:n], op=ALU.add),
                      reads=[("pd", di), ("x", m, bi)], writes=[("x", m, bi)])

    def ssd_phase():
        SBLK = [(i * 256, 256, i // 2) for i in range(8)] + [(2048, 128, 4)]
        SM = PS[2]

        def pk(i):
            return ("ps", i)

        def common(ar):
            d = {}
            _sq = ar.bf16(256)
            d["sq"] = [_sq, _sq]
            d["rstd"] = ar.f32(256)
            for nm in ("dt", "a", "acum", "dend", "cd", "f2"):
                d[nm] = ar.f32(2, 32)
            d["dtx"] = d["dt"]
            d["seg"] = [ar.f32(4, 128), ar.f32(4, 128)]
            d["E"] = [ar.f32(4, 128), ar.f32(4, 128)]
            d["M"] = [ar.bf16(4, 128), ar.bf16(4, 128)]
            d["Cp"] = [ar.bf16(4, 128), ar.bf16(4, 128)]
            d["xdt"] = [ar.bf16(256), ar.bf16(256)]
            d["xdtd"] = [ar.bf16(256), ar.bf16(256)]
            d["y1"] = [ar.f32(2, 128), ar.f32(2, 128)]
            d["y2"] = [ar.f32(2, 128), ar.f32(2, 128)]
            d["ysq"] = [ar.bf16(2, 128), ar.bf16(2, 128)]
            d["rg"] = [ar.f32(128), ar.f32(128)]
            d["sttmp"] = [ar.f32(256), ar.f32(256)]
            return d

        def blockbufs(ar, n):
            nt = n // 128
            d = {}
            d["xnb"] = ar.bf16(8, n)
            d["sz"] = [ar.bf16(2, n) for _ in range(8)]
            d["xsT"] = [ar.bf16(2, n) for _ in range(8)]
            d["BT"] = [ar.bf16(n) for _ in range(8)]
            d["CT"] = [ar.bf16(n) for _ in range(8)]
            d["tok"] = [ar.bf16(nt, 384) for _ in range(8)]
            d["raw"] = [ar.f32(n + 8), ar.f32(n + 8)] if n == 256 else [ar.f32(16, 11), ar.f32(16, 11)]
            d["acc"] = [ar.f32(n), ar.f32(n)]
            d["yn"] = ar.bf16(16, n)
            return d

        arp = Arena(un, UW)
        cm = common(arp)
        bp = blockbufs(arp, 256)
        carry = arp.f32(32, 3)
        hstate = arp.f32(2048)
        hbf = arp.bf16(2048)

        ars = Arena(un, UW)
        common(ars)
        bs = blockbufs(ars, 128)
        convh = ars.f32(32, 48)
        cd_all = ars.f32(16, 32)
        h0g = [ars.f32(8, 256), ars.f32(8, 256)]
        h0bf = [ars.bf16(8, 256), ars.bf16(8, 256)]
        Bm = ars.bf16(16, 128)

        S.add("pool", lambda e: e.dma_start(out=wdt, in_=ssd_w_in[:, 6144:6176].rearrange("(k p) n -> p k n", p=128)), writes=["wdt"], dsem="wdt")
        S.add("act", lambda e: e.activation(out=a_row, in_=rows[:, 32:64], func=AF.Exp), reads=["rows"], writes=["a_row"])
        S.add("dve", lambda e: e.tensor_scalar_mul(out=a_row, in0=a_row, scalar1=-1.0), reads=["a_row"], writes=["a_row"])

        win_slots = [wBf[:, s_ * 6144:(s_ + 1) * 6144].rearrange("p (k n) -> p k n", k=8) for s_ in range(2)]
        win_slots.append(wA.rearrange("p k n -> p (k n)")[:, 0:6144].rearrange("p (k n) -> p k n", k=8))
        WIN_SLOT_OF_G = [0, 1, 2, 0, 1, 2, 0, 1]
        wo_slots = [wA[:, s_ * 4:(s_ + 1) * 4, :] for s_ in range(2)]
        gctr = [0]
        woctr = [0]
        rawctr = [0]
        tctr = [0]
        cctr = [0]
        sctr = [0]
        hctr = [0]

        win_loaded = {}

        def get_win(sbi_, g):
            if sbi_ > 8:
                return None
            if (sbi_, g) not in win_loaded:
                win_loaded[(sbi_, g)] = load_win(g)
            return win_loaded[(sbi_, g)]

        wo_loaded = {}

        def get_wo(sbi_, q):
            if sbi_ > 8:
                return None
            if (sbi_, q) not in wo_loaded:
                ws_i = woctr[0] % 2
                woctr[0] += 1
                wq = wo_slots[ws_i]
                S.add("pool", lambda e, q=q, wq=wq: e.dma_start(out=wq, in_=ssd_w_out[q * 512:(q + 1) * 512, :].rearrange("(k p) n -> p k n", p=128)),
                      writes=[("wo", ws_i), ("win", 2)], dsem=("wo", ws_i))
                wo_loaded[(sbi_, q)] = ws_i
            return wo_loaded[(sbi_, q)]

        def load_win(g):
            s_ = WIN_SLOT_OF_G[g]
            ws = win_slots[s_]
            wkeys = [("win", s_)] + ([("wo", 0), ("wo", 1)] if s_ == 2 else [])
            segs = [(0, g * 256, 256), (256, 2048 + g * 256, 256), (512, 4096 + g * 128, 128), (640, 5120 + g * 128, 128)]
            for (o, c, wd_) in segs:
                S.add("pool", lambda e, ws=ws, o=o, c=c, wd_=wd_: e.dma_start(out=ws[:, :, o:o + wd_], in_=ssd_w_in[:, c:c + wd_].rearrange("(k p) n -> p k n", p=128)),
                      writes=wkeys, dsem=("win", s_))
            return s_

        def ssd_block(sbi):
            c0, n, kb = SBLK[sbi]
            nt = n // 128
            samp = (sbi == 8)
            first_blk = (sbi == 0)
            bb = bs if samp else bp
            xnb = bb["xnb"]
            yn = bb["yn"]
            UM = consts[:, K_US:K_US + 128] if samp else consts[:, K_UP:K_UP + 128]
            NEG = consts[:, K_NEGS:K_NEGS + 128] if samp else consts[:, K_NEGP:K_NEGP + 128]
            LAST = consts[:, K_BD:K_BD + 128] if samp else consts[:, K_ONES:K_ONES + 128]
            dtx, dt, a_, acum, dend, cd, f2 = (cm[k_] for k_ in ("dtx", "dt", "a", "acum", "dend", "cd", "f2"))
            rmsnorm(cm["sq"], cm["rstd"], C_MN, kb, lambda k: xnb[:, k, 0:n], lambda k: [("xnb", k)], SM, ps_keys=(pk(2),), span=(c0, n))
            for t in range(nt):
                for k in range(8):
                    S.add("pe", lambda e, t=t, k=k: e.matmul(SM[:, 128 + t * 32:128 + (t + 1) * 32], lhsT=xnb[:, k, t * 128:(t + 1) * 128], rhs=wdt[:, k, :],
                                                             start=(k == 0), stop=(k == 7)),
                          reads=[("xnb", k), "wdt"], writes=[pk(2)])
            smdt = SM[:, 128:128 + nt * 32].rearrange("p (t h) -> p t h", t=nt)
            S.add("dve", lambda e: e.tensor_tensor(out=dtx[:, 0:nt, :], in0=smdt, in1=bcast_mid(rows[:, 0:32], nt), op=ALU.add),
                  reads=[pk(2), "rows"], writes=["dt"])
            S.add("act", lambda e: e.activation(out=dtx[:, 0:nt, :], in_=dtx[:, 0:nt, :], func=AF.Exp), reads=["dt"], writes=["dt"])
            S.add("act", lambda e: e.activation(out=dt[:, 0:nt, :], in_=dtx[:, 0:nt, :], func=AF.Ln, bias=1.0), reads=["dt"], writes=["dt"])
            S.add("dve", lambda e: e.tensor_tensor(out=a_[:, 0:nt, :], in0=dt[:, 0:nt, :], in1=bcast_mid(a_row, nt), op=ALU.mult),
                  reads=["dt", "a_row"], writes=["a"])
            for t in range(nt):
                S.add("pe", lambda e, t=t: e.matmul(SM[:, 256:288], lhsT=UM, rhs=a_[:, t, :], start=True, stop=True),
                      reads=["a", "consts"], writes=[pk(2)])
                S.add("act", lambda e, t=t: e.activation(out=acum[:, t, :], in_=SM[:, 256:288], func=AF.Copy), reads=[pk(2)], writes=["acum"])
                S.add("pe", lambda e, t=t: e.matmul(SM[:, 288:320], lhsT=LAST, rhs=a_[:, t, :], start=True, stop=True),
                      reads=["a", "consts"], writes=[pk(2)])
                S.add("dve", lambda e, t=t: e.tensor_tensor(out=dend[:, t, :], in0=SM[:, 288:320], in1=acum[:, t, :], op=ALU.subtract),
                      reads=[pk(2), "acum"], writes=["dend"])
                S.add("act", lambda e, t=t: e.activation(out=cd[:, t, :], in_=SM[:, 288:320], func=AF.Exp), reads=[pk(2)], writes=["cd"])
            S.add("act", lambda e: e.activation(out=dend[:, 0:nt, :], in_=dend[:, 0:nt, :], func=AF.Exp), reads=["dend"], writes=["dend"])
            S.add("dve", lambda e: e.tensor_tensor(out=f2[:, 0:nt, :], in0=dt[:, 0:nt, :], in1=dend[:, 0:nt, :], op=ALU.mult),
                  reads=["dt", "dend"], writes=["f2"])
            if samp:
                for b in range(16):
                    S.add("pe", lambda e, b=b: e.matmul(SM[:, 0:512][:, b * 32:(b + 1) * 32], lhsT=bcast_col(consts[:, K_SEQM + b:K_SEQM + b + 1], 128), rhs=a_[:, 0, :],
                                                        start=True, stop=True),
                          reads=["a", "consts", "acum", "dend", "cd"], writes=[pk(2)])
                S.add("act", lambda e: e.activation(out=cd_all, in_=SM.rearrange("p (b h) -> p b h", b=16), func=AF.Exp), reads=[pk(2)], writes=["cd_all"])
                S.add("sp", lambda e: e.dma_start(out=convh.rearrange("p j c -> p (j c)"), in_=convT), writes=["convh"], dsem="convh")

            def inproj(g, ws_i):
                ws = win_slots[ws_i]
                sz = bb["sz"][g]; xsT = bb["xsT"][g]; BT = bb["BT"][g]; CT = bb["CT"][g]; tok = bb["tok"][g]
                pend = []
                for cc in range(6):
                    pai = cctr[0] % 2
                    cctr[0] += 1
                    pa = PS[pai]
                    for k in range(8):
                        S.add("pe", lambda e, k=k, pa=pa, cc=cc: e.matmul(pa[:, 0:n], lhsT=ws[:, k, cc * 128:(cc + 1) * 128], rhs=xnb[:, k, 0:n],
                                                                         start=(k == 0), stop=(k == 7)),
                              reads=[("win", ws_i), ("xnb", k)], writes=[pk(pai)])
                    if cc < 2:
                        S.add("act", lambda e, pa=pa, cc=cc: e.activation(out=sz[:, cc, 0:n], in_=pa[:, 0:n], func=AF.Silu),
                              reads=[pk(pai)], writes=[("sz", g)])
                        continue
                    j = (2 * g + cc - 2) if cc < 4 else ((16 + g) if cc == 4 else (24 + g))
                    ri = rawctr[0] % 2
                    rawctr[0] += 1
                    raw = bb["raw"][ri]
                    acc = bb["acc"][ri]
                    if samp:
                        rdat = raw[:, :, 3:11]
                        pav = pa[:, 0:128].rearrange("p (b t) -> p b t", b=16)
                        accv = acc[:, 0:128].rearrange("p (b t) -> p b t", b=16)
                        taps = [raw[:, :, kk:kk + 8] for kk in range(3)]
                        S.add("dve", lambda e, raw=raw, j=j: e.tensor_copy(out=raw[:, :, 0:3], in_=convh[:, j, :].rearrange("p (b k) -> p b k", b=16)),
                              reads=["convh"], writes=[("raw", ri)])
                    else:
                        rdat = raw[:, 3:3 + n]
                        pav = pa[:, 0:n]
                        accv = acc[:, 0:n]
                        taps = [raw[:, kk:kk + n] for kk in range(3)]
                        if first_blk:
                            S.add("dve", lambda e, raw=raw: e.memset(raw[:, 0:3], 0.0), writes=[("raw", ri)])
                        else:
                            S.add("dve", lambda e, raw=raw, j=j: e.tensor_copy(out=raw[:, 0:3], in_=carry[:, j, :]), reads=[("carry", j)], writes=[("raw", ri)])
                    S.add("act", lambda e, rdat=rdat, pav=pav: e.activation(out=rdat, in_=pav, func=AF.Copy), reads=[pk(pai)], writes=[("raw", ri)])
                    S.add("act", lambda e, accv=accv, pav=pav, j=j: e.activation(out=accv, in_=pav, func=AF.Identity, scale=col(C_CW, 3 * 32 + j), bias=col(C_CB, j)),
                          reads=[pk(pai), "cols"], writes=[("acc", ri)])
                    while pend:
                        pend.pop(0)()
                    if samp:
                        S.add("dve", lambda e, raw=raw, j=j: e.tensor_copy(out=convh[:, j, :].rearrange("p (b k) -> p b k", b=16), in_=raw[:, :, 8:11]),
                              reads=[("raw", ri)], writes=["convh"])
                    else:
                        S.add("dve", lambda e, raw=raw, j=j: e.tensor_copy(out=carry[:, j, :], in_=raw[:, n:n + 3]), reads=[("raw", ri)], writes=[("carry", j)])
                    for kk in range(3):
                        S.add("dve", lambda e, accv=accv, tp_=taps[kk], kk=kk, j=j: e.scalar_tensor_tensor(
                            out=accv, in0=tp_, scalar=col(C_CW, kk * 32 + j), in1=accv, op0=ALU.mult, op1=ALU.add),
                            reads=[("raw", ri), ("acc", ri), "cols"], writes=[("acc", ri)])
                    if cc < 4:
                        dst = xsT[:, cc - 2, 0:n]; dk = ("xsT", g)
                    elif cc == 4:
                        dst = BT[:, 0:n]; dk = ("BT", g)
                    else:
                        dst = CT[:, 0:n]; dk = ("CT", g)
                    pend.append(lambda dst=dst, acc=acc, ri=ri, dk=dk: S.add(
                        "act", lambda e: e.activation(out=dst, in_=acc[:, 0:n], func=AF.Silu), reads=[("acc", ri)], writes=[dk]))
                while pend:
                    pend.pop(0)()
            def inproj_tr(g):
                xsT = bb["xsT"][g]; BT = bb["BT"][g]; tok = bb["tok"][g]
                for t in range(nt):
                    th = tctr[0] % 2
                    tctr[0] += 1
                    tb_ = th * 512
                    for jj in range(2):
                        S.add("pe", lambda e, jj=jj, t=t, tb_=tb_: e.transpose(PT[:, tb_ + jj * 128:tb_ + (jj + 1) * 128], xsT[:, jj, t * 128:(t + 1) * 128], ident_bf),
                              reads=[("xsT", g), "ident"], writes=["pt"])
                    S.add("pe", lambda e, t=t, tb_=tb_: e.transpose(PT[:, tb_ + 256:tb_ + 384], BT[:, t * 128:(t + 1) * 128], ident_bf),
                          reads=[("BT", g), "ident"], writes=["pt"])
                    S.add("dve", lambda e, t=t, tb_=tb_: e.tensor_copy(out=tok[:, t, :], in_=PT[:, tb_:tb_ + 384]), reads=["pt"], writes=[("tok", g)])

            def _unpack(c):
                return (c[k_] for k_ in ("t", "g", "sz", "xsT", "BT", "CT", "tok", "si", "iAB", "iCB", "iYB", "iST", "AB", "CBG", "YB", "ST",
                                         "seg", "E", "M", "Cp", "xdt", "xdtd", "y1", "y2", "ysq", "rg", "sttmp", "first"))

            def scan_prep(t, g):
                sz = bb["sz"][g]; xsT = bb["xsT"][g]; BT = bb["BT"][g]; CT = bb["CT"][g]; tok = bb["tok"][g]
                si = sctr[0] % 2
                sctr[0] += 1
                iAB, iCB, iYB, iST = 0 + si, 2 + si, 4 + si, 6
                AB, CBG, YB, ST = PS[iAB], PS[iCB], PS[iYB], PS[iST]
                seg = cm["seg"][si]; E = cm["E"][si]; M = cm["M"][si]; Cp = cm["Cp"][si]
                xdt = cm["xdt"][si]; xdtd = cm["xdtd"][si]; y1 = cm["y1"][si]; y2 = cm["y2"][si]
                ysq = cm["ysq"][si]; rg = cm["rg"][si]; sttmp = cm["sttmp"][si]
                first = (first_blk and t == 0)
                S.add("pe", lambda e: e.matmul(CBG[:, 0:128], lhsT=BT[:, t * 128:(t + 1) * 128], rhs=CT[:, t * 128:(t + 1) * 128], start=True, stop=True),
                      reads=[("BT", g), ("CT", g)], writes=[pk(iCB)])
                for hh in range(4):
                    hd = 4 * g + hh
                    S.add("pe", lambda e, hh=hh, hd=hd: e.matmul(AB[:, hh * 128:(hh + 1) * 128], lhsT=bcast_col(a_[:, t, hd:hd + 1], 128), rhs=UM, start=True, stop=True),
                          reads=["a", "consts"], writes=[pk(iAB)])
                xs4 = tok[:, t, 0:256].rearrange("p (h q) -> p h q", h=4)
                S.add("pool", lambda e: e.tensor_tensor(out=xdt.rearrange("p (h q) -> p h q", h=4), in0=xs4, in1=bcast_last(dt[:, t, 4 * g:4 * g + 4], 64), op=ALU.mult),
                      reads=[("tok", g), "dt"], writes=[("xdt", si)])
                S.add("pool", lambda e: e.tensor_tensor(out=xdtd.rearrange("p (h q) -> p h q", h=4), in0=xs4, in1=bcast_last(f2[:, t, 4 * g:4 * g + 4], 64), op=ALU.mult),
                      reads=[("tok", g), "f2"], writes=[("xdtd", si)])
                if not samp and not first:
                    hsl0 = hstate[:, g * 256:(g + 1) * 256]
                    S.add("pool", lambda e: e.tensor_tensor(out=sttmp.rearrange("p (h q) -> p h q", h=4), in0=hsl0.rearrange("p (h q) -> p h q", h=4),
                                                           in1=bcast_last(cd[:, t, 4 * g:4 * g + 4], 64), op=ALU.mult),
                          reads=[("hst", g), "cd"], writes=[("sttmp", si)])
                for hh in range(4):
                    hd = 4 * g + hh
                    S.add("dve", lambda e, hh=hh, hd=hd: e.scalar_tensor_tensor(
                        out=seg[:, hh, :], in0=AB[:, hh * 128:(hh + 1) * 128], scalar=acum[:, t, hd:hd + 1], in1=NEG, op0=ALU.subtract, op1=ALU.add),
                        reads=[pk(iAB), "acum", "consts"], writes=[("seg", si)])
                S.add("act", lambda e: e.activation(out=seg, in_=seg, func=AF.Exp), reads=[("seg", si)], writes=[("seg", si)])
                S.add("act", lambda e: e.activation(out=E, in_=AB.rearrange("p (h l) -> p h l", h=4), func=AF.Exp), reads=[pk(iAB)], writes=[("E", si)])
                S.add("dve", lambda e: e.tensor_tensor(out=M, in0=seg, in1=bcast_mid(CBG[:, 0:128], 4), op=ALU.mult),
                      reads=[("seg", si), pk(iCB)], writes=[("M", si)])
                S.add("pool", lambda e: e.tensor_tensor(out=Cp, in0=E, in1=bcast_mid(CT[:, t * 128:(t + 1) * 128], 4), op=ALU.mult),
                      reads=[("E", si), ("CT", g)], writes=[("Cp", si)])
                return dict(locals())

            def scan_state(c):
                (t, g, sz, xsT, BT, CT, tok, si, iAB, iCB, iYB, iST, AB, CBG, YB, ST,
                 seg, E, M, Cp, xdt, xdtd, y1, y2, ysq, rg, sttmp, first) = _unpack(c)
                if samp:
                    for hh in range(4):
                        jj, half = hh // 2, hh % 2
                        yo = YB[half * 64:(half + 1) * 64, jj * 128:(jj + 1) * 128]
                        if hh < 2:
                            S.add("pe", lambda e, yo=yo, hh=hh: e.matmul(yo, lhsT=xdt[:, hh * 64:(hh + 1) * 64], rhs=M[:, hh, :], start=True, stop=True),
                                  reads=[("xdt", si), ("M", si)], writes=[pk(iYB)])
                        else:
                            S.add("pe", lambda e, yo=yo, hh=hh: e.matmul(yo, lhsT=xdt[:, hh * 64:(hh + 1) * 64], rhs=M[:, hh, :], start=False, stop=True, skip_group_check=True),
                                  reads=[("xdt", si), ("M", si)], writes=[pk(iYB)])
                    S.add("pool", lambda e: e.tensor_tensor(out=Bm, in0=bcast_mid(tok[:, 0, 256:384], 16), in1=bcast_last(consts[:, K_SEQM:K_SEQM + 16], 128), op=ALU.mult),
                          reads=[("tok", g), "consts"], writes=["Bm"])
                    for hf in range(2):
                        hs_i = hctr[0] % 2
                        hctr[0] += 1
                        hg = h0g[hs_i]
                        hb_ = h0bf[hs_i]
                        S.add("sp", lambda e, hg=hg, hf=hf: e.dma_start(out=hg.rearrange("p b c -> p (b c)"), in_=ssmT[g][:, hf * 2048:(hf + 1) * 2048]),
                              writes=[("h0g", hs_i)], dsem=("h0g", hs_i))
                        S.add("act", lambda e, hg=hg, hb_=hb_: e.activation(out=hb_, in_=hg, func=AF.Copy), reads=[("h0g", hs_i)], writes=[("h0bf", hs_i)])
                        for hh in range(4):
                            jj, half = hh // 2, hh % 2
                            for bl in range(8):
                                b = hf * 8 + bl
                                S.add("pe", lambda e, b=b, bl=bl, hh=hh, half=half, jj=jj, hb_=hb_: e.matmul(
                                    YB[half * 64:(half + 1) * 64, jj * 128 + b * 8:jj * 128 + (b + 1) * 8], lhsT=hb_[:, bl, hh * 64:(hh + 1) * 64],
                                    rhs=Cp[:, hh, b * 8:(b + 1) * 8], start=False, stop=True, skip_group_check=True),
                                    reads=[("h0bf", hs_i), ("Cp", si)], writes=[pk(iYB)])
                        for bl in range(8):
                            b = hf * 8 + bl
                            pq = PS[6]
                            S.add("pe", lambda e, b=b, pq=pq: e.matmul(pq[:, 0:256], lhsT=Bm[:, b, :], rhs=xdtd, start=True, stop=True),
                                  reads=["Bm", ("xdtd", si)], writes=[pk(6)])
                            S.add("pool", lambda e, b=b, bl=bl, hg=hg: e.tensor_tensor(out=hg[:, bl, :].rearrange("p (h q) -> p h q", h=4), in0=hg[:, bl, :].rearrange("p (h q) -> p h q", h=4),
                                                                                   in1=bcast_last(cd_all[:, b, 4 * g:4 * g + 4], 64), op=ALU.mult),
                                  reads=[("h0g", hs_i), ("h0bf", hs_i), "cd_all"], writes=[("h0g", hs_i)])
                            S.add("dve", lambda e, bl=bl, pq=pq, hg=hg: e.tensor_tensor(out=hg[:, bl, :], in0=hg[:, bl, :], in1=pq[:, 0:256], op=ALU.add),
                                  reads=[("h0g", hs_i), pk(6)], writes=[("h0g", hs_i)])
                        S.add("sp", lambda e, hg=hg, hf=hf: e.dma_start(out=ssm_s_out[g][:, hf * 2048:(hf + 1) * 2048], in_=hg.rearrange("p b c -> p (b c)")),
                              reads=[("h0g", hs_i)], dsem="ssm_s_out")
                else:
                    for hh in range(4):
                        hd = 4 * g + hh
                        jj, half = hh // 2, hh % 2
                        yo = YB[half * 64:(half + 1) * 64, jj * 128:(jj + 1) * 128]
                        S.add("pe", lambda e, yo=yo, hh=hh: e.matmul(yo, lhsT=xdt[:, hh * 64:(hh + 1) * 64], rhs=M[:, hh, :], start=True, stop=first),
                              reads=[("xdt", si), ("M", si)], writes=[pk(iYB)])
                        if not first:
                            S.add("pe", lambda e, yo=yo, hd=hd, hh=hh: e.matmul(yo, lhsT=hbf[:, hd * 64:(hd + 1) * 64], rhs=Cp[:, hh, :], start=False, stop=True),
                                  reads=[("hbf", g), ("Cp", si)], writes=[pk(iYB)])
                    S.add("pe", lambda e: e.matmul(ST[:, 0:256], lhsT=tok[:, t, 256:384], rhs=xdtd, start=True, stop=True),
                          reads=[("tok", g), ("xdtd", si)], writes=[pk(iST)])
                    hsl = hstate[:, g * 256:(g + 1) * 256]
                    hbl = hbf[:, g * 256:(g + 1) * 256]
                    if first:
                        S.add("dve", lambda e: e.tensor_copy(out=hbl, in_=ST[:, 0:256]), reads=[pk(iST)], writes=[("hbf", g)])
                        S.add("act", lambda e: e.activation(out=hsl, in_=ST[:, 0:256], func=AF.Copy), reads=[pk(iST)], writes=[("hst", g)])
                    else:
                        S.add("dve", lambda e: e.tensor_tensor(out=hbl, in0=sttmp, in1=ST[:, 0:256], op=ALU.add),
                              reads=[("sttmp", si), pk(iST)], writes=[("hbf", g)])
                        S.add("dve", lambda e: e.tensor_tensor(out=hsl, in0=sttmp, in1=ST[:, 0:256], op=ALU.add),
                              reads=[("sttmp", si), pk(iST)], writes=[("hst", g)])

            def scan_post(c):
                (t, g, sz, xsT, BT, CT, tok, si, iAB, iCB, iYB, iST, AB, CBG, YB, ST,
                 seg, E, M, Cp, xdt, xdtd, y1, y2, ysq, rg, sttmp, first) = _unpack(c)
                for jj in range(2):
                    j = 2 * g + jj
                    S.add("dve", lambda e, jj=jj, j=j: e.scalar_tensor_tensor(
                        out=y1[:, jj, :], in0=xsT[:, jj, t * 128:(t + 1) * 128], scalar=col(C_DD, j), in1=YB[:, jj * 128:(jj + 1) * 128], op0=ALU.mult, op1=ALU.add),
                        reads=[("xsT", g), pk(iYB), "cols"], writes=[("y1", si)])
                S.add("pool", lambda e: e.tensor_tensor(out=y2, in0=y1, in1=sz[:, :, t * 128:(t + 1) * 128], op=ALU.mult),
                      reads=[("y1", si), ("sz", g)], writes=[("y2", si)])
                S.add("act", lambda e: e.activation(out=ysq, in_=y2, func=AF.Square), reads=[("y2", si)], writes=[("ysq", si)])
                for jj in range(2):
                    S.add("pe", lambda e, jj=jj: e.matmul(YB[:, 256:384], lhsT=ones_bf, rhs=ysq[:, jj, :], start=(jj == 0), stop=(jj == 1)),
                          reads=[("ysq", si), "ones"], writes=[pk(iYB)])
                S.add("act", lambda e: e.activation(out=rg, in_=YB[:, 256:384], func=AF.Ln, bias=EPS, scale=1.0 / 256.0), reads=[pk(iYB)], writes=[("rg", si)])
                S.add("act", lambda e: e.activation(out=rg, in_=rg, func=AF.Exp, scale=-0.5), reads=[("rg", si)], writes=[("rg", si)])
                for jj in range(2):
                    j = 2 * g + jj
                    S.add("dve", lambda e, jj=jj, j=j: e.scalar_tensor_tensor(
                        out=yn[:, j, t * 128:(t + 1) * 128], in0=y2[:, jj, :], scalar=col(C_SN, j), in1=rg, op0=ALU.mult, op1=ALU.mult),
                        reads=[("y2", si), ("rg", si), "cols"], writes=[("yn", j)])

            PH = int(_os.environ.get("SSD_PH", "3"))
            for g in range(8):
                ws_cur = get_win(sbi, g)
                if g + 1 < 8:
                    get_win(sbi, g + 1)
                if g + 2 < 8:
                    get_win(sbi, g + 2)
                inproj(g, ws_cur)
                if g >= 1:
                    inproj_tr(g - 1)
            inproj_tr(7)

            def prefetch_next():
                get_win(sbi + 1, 0)
                get_win(sbi + 1, 1)
                get_wo(sbi, 0)
                get_wo(sbi, 1)
            if PH < 2:
                return
            its = [(t, g) for t in range(nt) for g in range(8)]
            ctxs = {}
            if samp:
                for n_, (t_, g_) in enumerate(its):
                    c_ = scan_prep(t_, g_)
                    scan_state(c_)
                    scan_post(c_)
                    if n_ == 0:
                        prefetch_next()
                its = []
            for i in range((len(its) + 2) if its else 0):
                if i < len(its):
                    ctxs[i] = scan_prep(*its[i])
                if 0 <= i - 1 < len(its):
                    scan_state(ctxs[i - 1])
                if 0 <= i - 2 < len(its):
                    scan_post(ctxs.pop(i - 2))
                if i == 1:
                    prefetch_next()
            if PH < 3:
                return
            for q in range(4):
                ws_i = get_wo(sbi, q)
                wq = wo_slots[ws_i]
                for m in range(8):
                    po = PS[m % 2]
                    for kk in range(4):
                        S.add("pe", lambda e, kk=kk, m=m, po=po, wq=wq, q=q: e.matmul(po[:, 0:n], lhsT=wq[:, kk, m * 128:(m + 1) * 128], rhs=yn[:, q * 4 + kk, 0:n],
                                                                                    start=(kk == 0), stop=(kk == 3)),
                              reads=[("wo", ws_i), ("yn", q * 4 + kk)], writes=[pk(m % 2)])
                    S.add("dve", lambda e, m=m, po=po: e.tensor_tensor(out=x[:, m, c0:c0 + n], in0=x[:, m, c0:c0 + n], in1=po[:, 0:n], op=ALU.add),
                          reads=[pk(m % 2), ("x", m, kb)], writes=[("x", m, kb)])

        for sbi in range(int(_os.environ.get("SSD_NB", "8"))):
            ssd_block(sbi)
        S.add("sp", lambda e: e.dma_start(out=ssm_p_out, in_=hstate), reads=[("hst", g) for g in range(8)], dsem="ssm_p_out")
        S.add("sp", lambda e: e.dma_start(out=conv_p_out, in_=carry.rearrange("p j k -> p (j k)")), reads=[("carry", j) for j in range(32)], dsem="conv_p_out")
        S.barrier()
        if _os.environ.get("SSD_NOSAMP"):
            return
        ssd_block(8)
        S.add("sp", lambda e: e.dma_start(out=conv_s_out, in_=convh.rearrange("p j c -> p (j c)")), reads=["convh"], dsem="conv_s_out")

    if do_ffn:
        ffn_phase(0, 0)
    S.barrier()
    if do_ssd:
        ssd_phase()
        out_dsems += ["ssm_p_out", "conv_p_out", "ssm_s_out", "conv_s_out"]
    S.barrier()
    if do_ffn:
        ffn_phase(0, 1)
        ffn_phase(1, 0)
    S.barrier()
    if do_pool:
        pool_phase()
        out_dsems += ["pool_p_out", "pool_s_out"]
    S.barrier()
    if do_ffn:
        ffn_phase(1, 1)
    S.barrier()
    ar = Arena(un, UW)
    sq = [ar.bf16(512), ar.bf16(512)]
    rstd = ar.f32(512)
    yo = [ar.f32(8, 512), ar.f32(8, 512)]
    yTv = yT.rearrange("(k p) t -> p k t", p=128)
    for bi in range(5):
        c0, n = BLOCKS[bi]
        yb = yo[bi % 2]
        rmsnorm(sq, rstd, C_FIN, bi, lambda k, yb=yb, n=n: yb[:, k, 0:n], lambda k, bi=bi: [("yo", bi % 2)], PS[6])
        S.add("sp", lambda e, yb=yb, c0=c0, n=n: e.dma_start(out=yTv[:, :, c0:c0 + n], in_=yb[:, :, 0:n]), reads=[("yo", bi % 2)], dsem="yT")
    out_dsems.append("yT")
    S.emit(out_dsems=out_dsems)
    return nc


def _consts():
    c = np.zeros((128, NCONST), np.float32)
    s = np.arange(128)[:, None]
    l = np.arange(128)[None, :]
    same = (s // 8) == (l // 8)
    c[:, K_UP:K_UP + 128] = (s <= l)
    c[:, K_US:K_US + 128] = (s <= l) & same
    c[:, K_NEGP:K_NEGP + 128] = np.where(l >= s, 0.0, -30000.0)
    c[:, K_NEGS:K_NEGS + 128] = np.where((l >= s) & same, 0.0, -30000.0)
    c[:, K_ONES:K_ONES + 128] = 1.0
    c[:, K_BD:K_BD + 128] = same
    c[:, K_SEQM:K_SEQM + 16] = (np.arange(128)[:, None] // 8) == np.arange(16)[None, :]
    for wi, w in enumerate(POOL_WINDOWS):
        c[:, K_INVC + wi * 16:K_INVC + (wi + 1) * 16] = 1.0 / np.minimum(float(w), np.arange(16) + 1.0)[None, :]
    c[:, K_ID:K_ID + 128] = np.eye(128)
    return c


def _colmajor(v):
    v = np.asarray(v, np.float32)
    return np.ascontiguousarray(v.reshape(-1, 128).T)


_PROG = {}


def kernel(x_prompt, x_sample, state_ssm, state_conv, state_pool,
           ffn_norm, ffn_w_gate, ffn_w_up, ffn_w_down, mix_norm,
           ssd_w_in, ssd_conv_w, ssd_conv_b, ssd_dt_bias, ssd_a_log, ssd_d, ssd_norm, ssd_w_out,
           pool_w_in, pool_w_group, pool_scale, pool_w_out, final_norm, _flags=(True, True, True)):
    f = np.float32
    cols = np.zeros((128, NCOL), f)
    for i in range(2):
        for j in range(2):
            o = C_FN + (i * 2 + j) * 8
            cols[:, o:o + 8] = _colmajor(ffn_norm[i, j])
    for i in range(2):
        cols[:, C_MN + i * 8:C_MN + (i + 1) * 8] = _colmajor(mix_norm[i])
    cols[:, C_FIN:C_FIN + 8] = _colmajor(final_norm)
    for k in range(4):
        cols[:, C_CW + k * 32:C_CW + (k + 1) * 32] = _colmajor(ssd_conv_w[0, k])
    cols[:, C_CB:C_CB + 32] = _colmajor(ssd_conv_b[0])
    cols[:, C_DD:C_DD + 16] = _colmajor(np.repeat(np.asarray(ssd_d[0], f), 64))
    cols[:, C_SN:C_SN + 16] = _colmajor(ssd_norm[0])
    cols[:, C_PS:C_PS + 8] = _colmajor(pool_scale[0])
    rows = np.zeros((128, 64), f)
    rows[:, 0:32] = np.asarray(ssd_dt_bias[0], f)[None, :]
    rows[:, 32:64] = np.asarray(ssd_a_log[0], f)[None, :]
    consts = _consts()

    shared = {
        "cols": cols, "rows": rows, "consts": consts,
        "ffn_w_gate": np.ascontiguousarray(ffn_w_gate, f), "ffn_w_up": np.ascontiguousarray(ffn_w_up, f),
        "ffn_w_down": np.ascontiguousarray(ffn_w_down, f),
        "ssd_w_in": np.ascontiguousarray(ssd_w_in[0], f), "ssd_w_out": np.ascontiguousarray(ssd_w_out[0], f),
        "pool_w_in": np.ascontiguousarray(pool_w_in[0], f), "pool_w_group": np.ascontiguousarray(pool_w_group[0], f),
        "pool_w_out": np.ascontiguousarray(pool_w_out[0], f),
    }
    in_maps = []
    for c in range(NCORES):
        sl = slice(c * 16, (c + 1) * 16)
        xs = np.asarray(x_sample[sl], f).reshape(128, 1024)
        xT = np.ascontiguousarray(np.concatenate([np.asarray(x_prompt[c], f), xs], axis=0).T)
        ssmT = np.ascontiguousarray(np.asarray(state_ssm[0, sl], f).reshape(16, 8, 256, 128).transpose(1, 3, 0, 2)).reshape(8, 128, 4096)
        convT = np.ascontiguousarray(np.asarray(state_conv[0, sl], f).reshape(16, 3, 32, 128).transpose(3, 2, 0, 1)).reshape(128, 32 * 48)
        poolT = np.ascontiguousarray(np.asarray(state_pool[0, sl], f).transpose(2, 0, 1))
        m = dict(shared)
        m.update({"xT": xT, "ssmT": ssmT, "convT": convT, "poolT": poolT})
        in_maps.append(m)

    key = tuple(_flags)
    if key not in _PROG:
        _PROG[key] = build_program(*_flags)
    nc = _PROG[key]
    res = run_bass_kernel_spmd(nc, in_maps, core_ids=list(range(NCORES)))
    R = res.results

    y_prompt = np.zeros((8, 2048, 1024), f)
    y_sample = np.zeros((128, 8, 1024), f)
    ssm_p = np.zeros((1, 8, 32, 64, 128), f)
    conv_p = np.zeros((1, 8, 3, 4096), f)
    pool_p = np.zeros((1, 8, 15, 1024), f)
    ssm_s = np.zeros((1, 128, 32, 64, 128), f)
    conv_s = np.zeros((1, 128, 3, 4096), f)
    pool_s = np.zeros((1, 128, 15, 1024), f)
    for c in range(NCORES):
        r = R[c]
        yT = np.asarray(r["yT"])
        y_prompt[c] = yT[:, :2048].T
        y_sample[c * 16:(c + 1) * 16] = yT[:, 2048:].T.reshape(16, 8, 1024)
        if "ssm_p_out" in r:
            ssm_p[0, c] = np.asarray(r["ssm_p_out"]).T.reshape(32, 64, 128)
            conv_p[0, c] = np.asarray(r["conv_p_out"]).reshape(128, 32, 3).transpose(2, 1, 0).reshape(3, 4096)
            ssm_s[0, c * 16:(c + 1) * 16] = np.asarray(r["ssm_s_out"]).reshape(8, 128, 16, 256).transpose(2, 0, 3, 1).reshape(16, 32, 64, 128)
            conv_s[0, c * 16:(c + 1) * 16] = np.asarray(r["conv_s_out"]).reshape(128, 32, 16, 3).transpose(2, 3, 1, 0).reshape(16, 3, 4096)
        if "pool_p_out" in r:
            pool_p[0, c] = np.asarray(r["pool_p_out"]).T
            pool_s[0, c * 16:(c + 1) * 16] = np.asarray(r["pool_s_out"]).transpose(1, 2, 0)
    return (y_prompt, y_sample, ssm_p, conv_p, pool_p, ssm_s, conv_s, pool_s)
```

```python
import numpy as np
import concourse.bass as bass
import concourse.mybir as mybir
from concourse.bass_utils import run_bass_kernel_spmd

F32 = mybir.dt.float32
BF16 = mybir.dt.bfloat16
AF = mybir.ActivationFunctionType
ALU = mybir.AluOpType

NCORES = 8
import os as _os
NTOK = 2176
BLOCKS = [(0, 512), (512, 512), (1024, 512), (1536, 512), (2048, 128)]
D_FF = 2816
EPS = 1e-6
POOL_WINDOWS = (2, 4, 8, 16)

C_FN, C_MN, C_FIN, C_CW, C_CB, C_DD, C_SN, C_PS, NCOL = 0, 32, 48, 56, 184, 216, 232, 248, 256
K_UP, K_US, K_NEGP, K_NEGS, K_ONES, K_BD, K_SEQM, K_INVC, K_ID, NCONST = 0, 128, 256, 384, 512, 640, 768, 784, 848, 976


class Op:
    __slots__ = ("eng", "fn", "waits", "signaled", "sigval", "pos", "clock", "is_dma", "dsem", "dval")


class Sched:
    ENGS = ("pe", "act", "dve", "pool", "sp")

    def __init__(self, nc):
        self.nc = nc
        self.ops = []
        self.streams = {e: [] for e in self.ENGS}
        self.clock = {e: {} for e in self.ENGS}
        self.dma_waited = {e: {} for e in self.ENGS}
        self.last_writer = {}
        self.readers = {}
        self.dsem_total = {}
        self.pending = {e: [] for e in self.ENGS}

    def add(self, eng, fn, reads=(), writes=(), dsem=None):
        op = Op()
        op.eng = eng; op.fn = fn; op.waits = []; op.signaled = False; op.sigval = None
        op.is_dma = dsem is not None; op.dsem = dsem; op.dval = None
        op.pos = len(self.streams[eng])
        deps = []
        for k in reads:
            lw = self.last_writer.get(k)
            if lw is not None:
                deps.append(lw)
            if k in ("pt", "psn") or (isinstance(k, tuple) and k[0] in ("ps", "pg", "pu", "pd")):
                deps.extend(r for r in self.readers.get(k, ()) if r.eng != eng)
        for k in writes:
            lw = self.last_writer.get(k)
            if lw is not None:
                deps.append(lw)
            deps.extend(self.readers.get(k, ()))
        forced = self.pending[eng]
        if forced:
            deps.extend(forced)
            self.pending[eng] = []
        clk = self.clock[eng]
        seen = set()
        deps.sort(key=lambda d_: -d_.pos)
        for d in deps:
            if id(d) in seen or d is op:
                continue
            seen.add(id(d))
            if d.is_dma:
                tot = self.dsem_total[d.dsem]
                if self.dma_waited[eng].get(d.dsem, 0) < tot:
                    op.waits.append(("dma", d.dsem, tot))
                    self.dma_waited[eng][d.dsem] = tot
            else:
                if d.eng == eng and (eng == "pe" or op.pos - d.pos > 6):
                    continue
                if clk.get(d.eng, -1) >= d.pos:
                    continue
                op.waits.append(("eng", d))
                d.signaled = True
                for e2, p2 in d.clock.items():
                    if clk.get(e2, -1) < p2:
                        clk[e2] = p2
                if clk.get(d.eng, -1) < d.pos:
                    clk[d.eng] = d.pos
        if op.is_dma:
            self.dsem_total[dsem] = self.dsem_total.get(dsem, 0) + 16
            op.dval = self.dsem_total[dsem]
            op.clock = None
        else:
            op.clock = dict(clk)
        for k in reads:
            self.readers.setdefault(k, []).append(op)
        for k in writes:
            self.last_writer[k] = op
            self.readers[k] = []
        self.ops.append(op)
        self.streams[eng].append(op)
        return op

    def barrier(self):
        lasts = []
        for e in self.ENGS:
            st = [o for o in self.streams[e][-1:] if not o.is_dma]
            for o in reversed(self.streams[e]):
                if not o.is_dma:
                    lasts.append(o)
                    break
        dmas = {}
        for o in self.ops:
            if o.is_dma:
                dmas[o.dsem] = o
        for e in self.ENGS:
            self.pending[e] = [o for o in lasts if o.eng != e] + list(dmas.values())

    def emit(self, out_dsems=()):
        nc = self.nc
        engobj = {"pe": nc.tensor, "act": nc.scalar, "dve": nc.vector, "pool": nc.gpsimd, "sp": nc.sync}
        esem = {e: nc.alloc_semaphore("es_" + e) for e in self.ENGS}
        dsems = {}
        for op in self.ops:
            if op.is_dma and op.dsem not in dsems:
                dsems[op.dsem] = nc.alloc_semaphore("ds_%d" % len(dsems))
        for e in self.ENGS:
            n = 0
            for op in self.streams[e]:
                if op.signaled:
                    n += 1
                    op.sigval = n
        for op in self.ops:
            eo = engobj[op.eng]
            for w in op.waits:
                if w[0] == "dma":
                    eo.wait_ge(dsems[w[1]], w[2])
                else:
                    eo.wait_ge(esem[w[1].eng], w[1].sigval)
            inst = op.fn(eo)
            if op.is_dma:
                inst.then_inc(dsems[op.dsem], 16)
            elif op.signaled:
                inst.then_inc(esem[op.eng], 1)
        for ds in out_dsems:
            if ds in dsems:
                nc.sync.wait_ge(dsems[ds], self.dsem_total[ds])


class Arena:
    def __init__(self, flat, nwords):
        self.flat = flat
        self.n = nwords
        self.off = 0

    def _take(self, words):
        words = (words + 7) // 8 * 8
        o = self.off
        self.off += words
        assert self.off <= self.n, ("arena overflow", self.off, self.n)
        return o

    @staticmethod
    def _shape(v, shape):
        if len(shape) == 1:
            return v
        if len(shape) == 2:
            return v.rearrange("p (a b) -> p a b", a=shape[0])
        if len(shape) == 3:
            return v.rearrange("p (a b c) -> p a b c", a=shape[0], b=shape[1])
        raise ValueError(shape)

    def f32(self, *shape):
        n = int(np.prod(shape))
        o = self._take(n)
        return self._shape(self.flat[:, o:o + n], shape)

    def bf16(self, *shape):
        n = int(np.prod(shape))
        assert n % 2 == 0
        o = self._take(n // 2)
        return self._shape(self.flat[:, o:o + n // 2].bitcast(BF16), shape)


def bcast_mid(ap2d, rep):
    a = ap2d.ap
    assert len(a) == 2
    return bass.AP(ap2d.tensor, ap2d.offset, [list(a[0]), [0, rep], list(a[1])])


def bcast_last(ap2d, rep):
    a = ap2d.ap
    assert len(a) == 2
    return bass.AP(ap2d.tensor, ap2d.offset, [list(a[0]), list(a[1]), [0, rep]])


def bcast_col(ap_col, rep):
    a = ap_col.ap
    return bass.AP(ap_col.tensor, ap_col.offset, [list(a[0]), [0, rep]])


def build_program(do_ffn=True, do_ssd=True, do_pool=True):
    nc = bass.Bass("TRN2", target_bir_lowering=False)

    def din(name, shape):
        return nc.dram_tensor(name, list(shape), F32, kind="ExternalInput").ap()

    def dout(name, shape):
        return nc.dram_tensor(name, list(shape), F32, kind="ExternalOutput").ap()

    xT = din("xT", [1024, NTOK])
    ssmT = din("ssmT", [8, 128, 16 * 256])
    convT = din("convT", [128, 32 * 48])
    poolT = din("poolT", [1024, 16, 15])
    cols_d = din("cols", [128, NCOL])
    rows_d = din("rows", [128, 64])
    const_d = din("consts", [128, NCONST])
    w_gate = din("ffn_w_gate", [2, 2, 1024, D_FF])
    w_up = din("ffn_w_up", [2, 2, 1024, D_FF])
    w_down = din("ffn_w_down", [2, 2, D_FF, 1024])
    ssd_w_in = din("ssd_w_in", [1024, 6176])
    ssd_w_out = din("ssd_w_out", [2048, 1024])
    pool_w_in = din("pool_w_in", [1024, 1024])
    pool_w_group = din("pool_w_group", [4, 256, 256])
    pool_w_out = din("pool_w_out", [1024, 1024])

    yT = dout("yT", [1024, NTOK])
    ssm_p_out = dout("ssm_p_out", [128, 2048])
    conv_p_out = dout("conv_p_out", [128, 32 * 3])
    pool_p_out = dout("pool_p_out", [1024, 15])
    ssm_s_out = dout("ssm_s_out", [8, 128, 16 * 256])
    conv_s_out = dout("conv_s_out", [128, 32 * 48])
    pool_s_out = dout("pool_s_out", [1024, 16, 15])

    def sb(name, shape, dt):
        return nc.alloc_sbuf_tensor(name, list(shape), dt).ap()

    x = sb("x", [128, 8, NTOK], F32)
    wA = sb("wA", [128, 8, 1024], BF16)
    wBf = sb("wB", [128, 12288], BF16)
    cols = sb("colsb", [128, NCOL], F32)
    rows = sb("rowsb", [128, 64], F32)
    consts = sb("constsb", [128, NCONST], F32)
    ident_bf = sb("ident_bf", [128, 128], BF16)
    ones_bf = sb("ones_bf", [128, 128], BF16)
    a_row = sb("a_row", [128, 32], F32)
    wdt = sb("wdt", [128, 8, 32], BF16)
    wgrp = sb("wgrp", [128, 8, 256], BF16)
    UW = 22944
    un = sb("union", [128, UW], F32)

    PS = [nc.alloc_psum_tensor("ps%d" % i, [128, 512], F32).ap() for i in range(7)]
    PT = nc.alloc_psum_tensor("pst", [128, 1024], BF16).ap()

    S = Sched(nc)
    out_dsems = []

    S.add("sp", lambda e: e.dma_start(out=cols, in_=cols_d), writes=["cols"], dsem="cols")
    S.add("sp", lambda e: e.dma_start(out=rows, in_=rows_d), writes=["rows"], dsem="rows")
    S.add("sp", lambda e: e.dma_start(out=consts, in_=const_d), writes=["consts"], dsem="consts")
    xTv = xT.rearrange("(k p) t -> p k t", p=128)
    for k in range(8):
        S.add("sp", lambda e, k=k: e.dma_start(out=x[:, k, :], in_=xTv[:, k, :]), writes=[("x", k, b) for b in range(5)], dsem="xin")
    S.add("dve", lambda e: e.tensor_copy(out=ident_bf, in_=consts[:, K_ID:K_ID + 128]), reads=["consts"], writes=["ident"])
    S.add("dve", lambda e: e.tensor_copy(out=ones_bf, in_=consts[:, K_ONES:K_ONES + 128]), reads=["consts"], writes=["ones"])

    def col(off, k):
        return cols[:, off + k:off + k + 1]

    nrm_ctr = [0]

    def rmsnorm(ar_sq, ar_rstd, goff, bi, out_fn, out_keys_fn, ps_norm, ps_keys=("psn",), span=None):
        c0, n = BLOCKS[bi] if span is None else span
        for k in range(8):
            i = 0 if ar_sq[0] is ar_sq[1] else nrm_ctr[0] % 2
            nrm_ctr[0] += 1
            sq = ar_sq[i]
            S.add("act", lambda e, k=k, sq=sq: e.activation(out=sq[:, 0:n], in_=x[:, k, c0:c0 + n], func=AF.Square),
                  reads=[("x", k, bi)], writes=[("sq", i)])
            S.add("pe", lambda e, k=k, sq=sq: e.matmul(ps_norm[:, 0:n], lhsT=ones_bf, rhs=sq[:, 0:n], start=(k == 0), stop=(k == 7)),
                  reads=[("sq", i), "ones"], writes=list(ps_keys))
        S.add("act", lambda e: e.activation(out=ar_rstd[:, 0:n], in_=ps_norm[:, 0:n], func=AF.Ln, bias=EPS, scale=1.0 / 1024.0),
              reads=list(ps_keys), writes=["rstd"])
        S.add("act", lambda e: e.activation(out=ar_rstd[:, 0:n], in_=ar_rstd[:, 0:n], func=AF.Exp, scale=-0.5), reads=["rstd"], writes=["rstd"])
        for k in range(8):
            S.add("dve", lambda e, k=k: e.scalar_tensor_tensor(out=out_fn(k), in0=x[:, k, c0:c0 + n], scalar=col(goff, k),
                                                               in1=ar_rstd[:, 0:n], op0=ALU.mult, op1=ALU.mult),
                  reads=[("x", k, bi), "rstd", "cols"], writes=out_keys_fn(k))

    SPLITS = [(0, 4), (4, 4), (8, 3)]

    def ffn_phase(i, j):
        ar = Arena(un, UW)
        xn = ar.bf16(8, NTOK)
        h = ar.bf16(8, NTOK)
        sq = [ar.bf16(512), ar.bf16(512)]
        rstd = ar.f32(512)
        sg = [ar.bf16(512), ar.bf16(512)]
        goff = C_FN + (i * 2 + j) * 8
        for bi in range(5):
            c0, n = BLOCKS[bi]
            rmsnorm(sq, rstd, goff, bi, lambda k, c0=c0, n=n: xn[:, k, c0:c0 + n], lambda k, bi=bi: [("xn", k, bi)], PS[6])
        wgd = w_gate[i, j]
        wud = w_up[i, j]
        wdd = w_down[i, j]
        wslots = [wBf[:, s * 2048:(s + 1) * 2048].rearrange("p (k n) -> p k n", k=8) for s in range(4)]
        ev = 0

        def load_gu(cc):
            sg_slot = (cc % 2) * 2
            su_slot = sg_slot + 1
            wg_s = wslots[sg_slot]
            wu_s = wslots[su_slot]
            S.add("pool", lambda e: e.dma_start(out=wg_s, in_=wgd[:, cc * 256:(cc + 1) * 256].rearrange("(k p) n -> p k n", p=128)),
                  writes=[("wB", sg_slot)], dsem=("wB", sg_slot))
            S.add("pool", lambda e: e.dma_start(out=wu_s, in_=wud[:, cc * 256:(cc + 1) * 256].rearrange("(k p) n -> p k n", p=128)),
                  writes=[("wB", su_slot)], dsem=("wB", su_slot))

        load_gu(0)
        for (cc0, ncc) in SPLITS:
            nh = ncc * 2
            r0 = cc0 * 256
            S.add("pool", lambda e, r0=r0, nh=nh: e.dma_start(out=wA[:, 0:nh, :], in_=wdd[r0:r0 + nh * 128, :].rearrange("(k p) n -> p k n", p=128)),
                  writes=["wA"], dsem="wA")
            for cc in range(cc0, cc0 + ncc):
                sg_slot = (cc % 2) * 2
                su_slot = sg_slot + 1
                wg_s = wslots[sg_slot]
                wu_s = wslots[su_slot]
                if cc + 1 < 11:
                    load_gu(cc + 1)
                for sub in range(2):
                    hc = (cc - cc0) * 2 + sub
                    for bi in range(5):
                        c0, n = BLOCKS[bi]
                        pg = PS[ev % 2]
                        pu = PS[2 + ev % 2]
                        sgt = sg[ev % 2]
                        evi = ev % 2
                        ev += 1
                        for k in range(8):
                            S.add("pe", lambda e, k=k, pg=pg, wg_s=wg_s, sub=sub, c0=c0, n=n: e.matmul(
                                pg[:, 0:n], lhsT=wg_s[:, k, sub * 128:(sub + 1) * 128], rhs=xn[:, k, c0:c0 + n], start=(k == 0), stop=(k == 7)),
                                reads=[("wB", sg_slot), ("xn", k, bi)], writes=[("pg", evi)])
                        for k in range(8):
                            S.add("pe", lambda e, k=k, pu=pu, wu_s=wu_s, sub=sub, c0=c0, n=n: e.matmul(
                                pu[:, 0:n], lhsT=wu_s[:, k, sub * 128:(sub + 1) * 128], rhs=xn[:, k, c0:c0 + n], start=(k == 0), stop=(k == 7)),
                                reads=[("wB", su_slot), ("xn", k, bi)], writes=[("pu", evi)])
                        S.add("act", lambda e, pg=pg, sgt=sgt, n=n: e.activation(out=sgt[:, 0:n], in_=pg[:, 0:n], func=AF.Silu),
                              reads=[("pg", evi)], writes=[("sg", evi)])
                        S.add("dve", lambda e, pu=pu, sgt=sgt, hc=hc, c0=c0, n=n: e.tensor_tensor(
                            out=h[:, hc, c0:c0 + n], in0=sgt[:, 0:n], in1=pu[:, 0:n], op=ALU.mult),
                            reads=[("sg", evi), ("pu", evi)], writes=[("h", hc, bi)])
            dctr = 0
            for bi in range(5):
                c0, n = BLOCKS[bi]
                for m in range(8):
                    pd = PS[4 + dctr % 2]
                    di = dctr % 2
                    dctr += 1
                    for kk in range(nh):
                        S.add("pe", lambda e, kk=kk, pd=pd, m=m, c0=c0, n=n, nh=nh: e.matmul(
                            pd[:, 0:n], lhsT=wA[:, kk, m * 128:(m + 1) * 128], rhs=h[:, kk, c0:c0 + n], start=(kk == 0), stop=(kk == nh - 1)),
                            reads=["wA", ("h", kk, bi)], writes=[("pd", di)])
                    S.add("dve", lambda e, pd=pd, m=m, c0=c0, n=n: e.scalar_tensor_tensor(
                        out=x[:, m, c0:c0 + n], in0=pd[:, 0:n], scalar=0.5, in1=x[:, m, c0:c0 + n], op0=ALU.mult, op1=ALU.add),
                        reads=[("pd", di), ("x", m, bi)], writes=[("x", m, bi)])

    def pool_phase():
        ar = Arena(un, UW)
        xn = ar.bf16(8, NTOK)
        mixed = ar.bf16(8, NTOK)
        sq = [ar.bf16(512), ar.bf16(512)]
        rstd = ar.f32(512)
        ext_p = [ar.f32(528) for _ in range(2)]
        ext_s = [ar.f32(16, 23) for _ in range(2)]
        tp = [ar.f32(528) for _ in range(2)]
        ts_ = [ar.f32(16, 23) for _ in range(2)]
        fix = ar.f32(16)
        goff = C_MN + 8
        S.add("pool", lambda e: e.dma_start(out=wA, in_=pool_w_in.rearrange("(k p) n -> p k n", p=128)), writes=["wA"], dsem="wA")
        wo = wBf[:, 0:8192].rearrange("p (k n) -> p k n", k=8)
        S.add("pool", lambda e: e.dma_start(out=wo, in_=pool_w_out.rearrange("(k p) n -> p k n", p=128)),
              writes=[("wB", s_) for s_ in range(4)], dsem=("wB", 0))
        S.add("pool", lambda e: e.dma_start(out=wgrp.rearrange("p (g kk) n -> p g kk n", g=4), in_=pool_w_group.rearrange("g (kk p) n -> p g kk n", p=128)),
              writes=["wgrp"], dsem="wgrp")
        for bi in range(5):
            c0, n = BLOCKS[bi]
            rmsnorm(sq, rstd, goff, bi, lambda k, c0=c0, n=n: xn[:, k, c0:c0 + n], lambda k, bi=bi: [("xn", k, bi)], PS[6])
        ev = 0
        ectr = 0
        for m in range(8):
            widx = m // 2
            w = POOL_WINDOWS[widx]
            prev = None
            for bi in range(5):
                c0, n = BLOCKS[bi]
                pp = PS[ev % 2]
                pi = ev % 2
                ev += 1
                for k in range(8):
                    S.add("pe", lambda e, k=k, pp=pp, m=m, c0=c0, n=n: e.matmul(
                        pp[:, 0:n], lhsT=wA[:, k, m * 128:(m + 1) * 128], rhs=xn[:, k, c0:c0 + n], start=(k == 0), stop=(k == 7)),
                        reads=["wA", ("xn", k, bi)], writes=[("pg", pi)])
                es = ectr % 2
                ectr += 1
                if bi < 4:
                    ep = ext_p[es]
                    ekey = ("extp", es)
                    if bi == 0:
                        S.add("pool", lambda e, ep=ep: e.memset(ep[:, 0:15], 0.0), writes=[ekey])
                    else:
                        S.add("pool", lambda e, ep=ep, prev=prev: e.tensor_copy(out=ep[:, 0:15], in_=prev[:, 512:527]), reads=[("extp", 1 - es)], writes=[ekey])
                    S.add("act", lambda e, pp=pp, ep=ep: e.activation(out=ep[:, 15:527], in_=pp[:, 0:512], func=AF.Copy),
                          reads=[("pg", pi)], writes=[ekey])
                    prev = ep
                    if bi == 3:
                        S.add("sp", lambda e, m=m, ep=ep: e.dma_start(out=pool_p_out[m * 128:(m + 1) * 128, :], in_=ep[:, 512:527]),
                              reads=[ekey], dsem="pool_p_out")
                    src = ep
                    skey = ekey
                    hi = 527
                    sl = lambda a, lo_, hi_: a[:, lo_:hi_]
                    tmps = tp
                    tkey = "tp"
                else:
                    esm = ext_s[es]
                    ekey = ("exts", es)
                    for bq in range(2):
                        S.add("sp", lambda e, m=m, esm=esm, bq=bq: e.dma_start(out=esm[:, bq * 8:(bq + 1) * 8, 0:15], in_=poolT[m * 128:(m + 1) * 128, bq * 8:(bq + 1) * 8, :]),
                              writes=[ekey], dsem=("exts", es))
                    S.add("act", lambda e, pp=pp, esm=esm: e.activation(out=esm[:, :, 15:23], in_=pp[:, 0:128].rearrange("p (b t) -> p b t", b=16), func=AF.Copy),
                          reads=[("pg", pi)], writes=[ekey])
                    for bq in range(2):
                        S.add("sp", lambda e, m=m, esm=esm, bq=bq: e.dma_start(out=pool_s_out[m * 128:(m + 1) * 128, bq * 8:(bq + 1) * 8, :], in_=esm[:, bq * 8:(bq + 1) * 8, 8:23]),
                              reads=[ekey], dsem="pool_s_out")
                    src = esm
                    skey = ekey
                    hi = 23
                    sl = lambda a, lo_, hi_: a[:, :, lo_:hi_]
                    tmps = ts_
                    tkey = "ts"
                base = src
                lo = 0
                step = 1
                lvl = 0
                while step < w:
                    dst = tmps[lvl % 2]
                    nlo = lo + step
                    S.add("pool", lambda e, dst=dst, src=src, nlo=nlo, step=step, hi=hi, sl=sl: e.tensor_tensor(
                        out=sl(dst, nlo, hi), in0=sl(src, nlo, hi), in1=sl(src, nlo - step, hi - step), op=ALU.add),
                        reads=[skey], writes=[(tkey, lvl % 2)])
                    src = dst
                    skey = (tkey, lvl % 2)
                    lo = nlo
                    step *= 2
                    lvl += 1
                if bi < 4:
                    S.add("dve", lambda e, src=src, base=base, m=m, w=w, c0=c0: e.scalar_tensor_tensor(
                        out=mixed[:, m, c0:c0 + 512], in0=src[:, 15:527], scalar=1.0 / w, in1=base[:, 15:527], op0=ALU.mult, op1=ALU.subtract),
                        reads=[skey, ekey], writes=[("mixed", m, bi)])
                    if bi == 0:
                        S.add("dve", lambda e, src=src, widx=widx: e.tensor_tensor(
                            out=fix, in0=src[:, 15:31], in1=consts[:, K_INVC + widx * 16:K_INVC + widx * 16 + 16], op=ALU.mult),
                            reads=[skey, "consts"], writes=["fix"])
                        S.add("dve", lambda e, base=base, m=m: e.tensor_tensor(out=mixed[:, m, 0:16], in0=fix, in1=base[:, 15:31], op=ALU.subtract),
                              reads=["fix", ekey], writes=[("mixed", m, 0)])
                else:
                    S.add("dve", lambda e, src=src, base=base, m=m, w=w: e.scalar_tensor_tensor(
                        out=mixed[:, m, 2048:2176].rearrange("p (b t) -> p b t", b=16), in0=src[:, :, 15:23], scalar=1.0 / w, in1=base[:, :, 15:23],
                        op0=ALU.mult, op1=ALU.subtract),
                        reads=[skey, ekey], writes=[("mixed", m, 4)])
        wg4 = wgrp.rearrange("p (g kk) n -> p g kk n", g=4)
        mg = xn
        ev = 0
        for m in range(8):
            g, mm = m // 2, m % 2
            for bi in range(5):
                c0, n = BLOCKS[bi]
                pp = PS[ev % 2]
                pi = ev % 2
                ev += 1
                for kk in range(2):
                    S.add("pe", lambda e, kk=kk, pp=pp, g=g, mm=mm, c0=c0, n=n: e.matmul(
                        pp[:, 0:n], lhsT=wg4[:, g, kk, mm * 128:(mm + 1) * 128], rhs=mixed[:, 2 * g + kk, c0:c0 + n], start=(kk == 0), stop=(kk == 1)),
                        reads=["wgrp", ("mixed", 2 * g + kk, bi)], writes=[("pg", pi)])
                S.add("act", lambda e, pp=pp, m=m, c0=c0, n=n: e.activation(out=mg[:, m, c0:c0 + n], in_=pp[:, 0:n], func=AF.Identity, scale=col(C_PS, m)),
                      reads=[("pg", pi), "cols"] + [("xn", k, bi) for k in range(8)], writes=[("mg", m, bi), ("xn", m, bi)])
        ev = 0
        for bi in range(5):
            c0, n = BLOCKS[bi]
            for m in range(8):
                pd = PS[4 + ev % 2]
                di = ev % 2
                ev += 1
                for k in range(8):
                    S.add("pe", lambda e, k=k, pd=pd, m=m, c0=c0, n=n: e.matmul(
                        pd[:, 0:n], lhsT=wo[:, k, m * 128:(m + 1) * 128], rhs=mg[:, k, c0:c0 + n], start=(k == 0), stop=(k == 7)),
                        reads=[("wB", 0), ("mg", k, bi)], writes=[("pd", di)])
                S.add("dve", lambda e, pd=pd, m=m, c0=c0, n=n: e.tensor_tensor(out=x[:, m, c0:c0 + n], in0=x[:, m, c0:c0 + n], in1=pd[:, 0:n], op=ALU.add),
                      reads=[("pd", di), ("x", m, bi)], writes=[("x", m, bi)])

    def ssd_phase():
        SBLK = [(i * 256, 256, i // 2) for i in range(8)] + [(2048, 128, 4)]
        SM = PS[2]

        def pk(i):
            return ("ps", i)

        def common(ar):
            d = {}
            _sq = ar.bf16(256)
            d["sq"] = [_sq, _sq]
            d["rstd"] = ar.f32(256)
            for nm in ("dt", "a", "acum", "dend", "cd", "f2"):
                d[nm] = ar.f32(2, 32)
            d["dtx"] = d["dt"]
            d["seg"] = [ar.f32(4, 128), ar.f32(4, 128)]
            d["E"] = [ar.f32(4, 128), ar.f32(4, 128)]
            d["M"] = [ar.bf16(4, 128), ar.bf16(4, 128)]
            d["Cp"] = [ar.bf16(4, 128), ar.bf16(4, 128)]
            d["xdt"] = [ar.bf16(256), ar.bf16(256)]
            d["xdtd"] = [ar.bf16(256), ar.bf16(256)]
            d["y1"] = [ar.f32(2, 128), ar.f32(2, 128)]
            d["y2"] = [ar.f32(2, 128), ar.f32(2, 128)]
            d["ysq"] = [ar.bf16(2, 128), ar.bf16(2, 128)]
            d["rg"] = [ar.f32(128), ar.f32(128)]
            d["sttmp"] = [ar.f32(256), ar.f32(256)]
            return d

        def blockbufs(ar, n):
            nt = n // 128
            d = {}
            d["xnb"] = ar.bf16(8, n)
            d["sz"] = [ar.bf16(2, n) for _ in range(8)]
            d["xsT"] = [ar.bf16(2, n) for _ in range(8)]
            d["BT"] = [ar.bf16(n) for _ in range(8)]
            d["CT"] = [ar.bf16(n) for _ in range(8)]
            d["tok"] = [ar.bf16(nt, 384) for _ in range(8)]
            d["raw"] = [ar.f32(n + 8), ar.f32(n + 8)] if n == 256 else [ar.f32(16, 11), ar.f32(16, 11)]
            d["acc"] = [ar.f32(n), ar.f32(n)]
            d["yn"] = ar.bf16(16, n)
            return d

        arp = Arena(un, UW)
        cm = common(arp)
        bp = blockbufs(arp, 256)
        carry = arp.f32(32, 3)
        hstate = arp.f32(2048)
        hbf = arp.bf16(2048)

        ars = Arena(un, UW)
        common(ars)
        bs = blockbufs(ars, 128)
        convh = ars.f32(32, 48)
        cd_all = ars.f32(16, 32)
        h0g = [ars.f32(8, 256), ars.f32(8, 256)]
        h0bf = [ars.bf16(8, 256), ars.bf16(8, 256)]
        Bm = ars.bf16(16, 128)

        S.add("pool", lambda e: e.dma_start(out=wdt, in_=ssd_w_in[:, 6144:6176].rearrange("(k p) n -> p k n", p=128)), writes=["wdt"], dsem="wdt")
        S.add("act", lambda e: e.activation(out=a_row, in_=rows[:, 32:64], func=AF.Exp), reads=["rows"], writes=["a_row"])
        S.add("dve", lambda e: e.tensor_scalar_mul(out=a_row, in0=a_row, scalar1=-1.0), reads=["a_row"], writes=["a_row"])

        win_slots = [wBf[:, s_ * 6144:(s_ + 1) * 6144].rearrange("p (k n) -> p k n", k=8) for s_ in range(2)]
        win_slots.append(wA.rearrange("p k n -> p (k n)")[:, 0:6144].rearrange("p (k n) -> p k n", k=8))
        WIN_SLOT_OF_G = [0, 1, 2, 0, 1, 2, 0, 1]
        wo_slots = [wA[:, s_ * 4:(s_ + 1) * 4, :] for s_ in range(2)]
        gctr = [0]
        woctr = [0]
        rawctr = [0]
        tctr = [0]
        cctr = [0]
        sctr = [0]
        hctr = [0]

        win_loaded = {}

        def get_win(sbi_, g):
            if sbi_ > 8:
                return None
            if (sbi_, g) not in win_loaded:
                win_loaded[(sbi_, g)] = load_win(g)
            return win_loaded[(sbi_, g)]

        wo_loaded = {}

        def get_wo(sbi_, q):
            if sbi_ > 8:
                return None
            if (sbi_, q) not in wo_loaded:
                ws_i = woctr[0] % 2
                woctr[0] += 1
                wq = wo_slots[ws_i]
                S.add("pool", lambda e, q=q, wq=wq: e.dma_start(out=wq, in_=ssd_w_out[q * 512:(q + 1) * 512, :].rearrange("(k p) n -> p k n", p=128)),
                      writes=[("wo", ws_i), ("win", 2)], dsem=("wo", ws_i))
                wo_loaded[(sbi_, q)] = ws_i
            return wo_loaded[(sbi_, q)]

        def load_win(g):
            s_ = WIN_SLOT_OF_G[g]
            ws = win_slots[s_]
            wkeys = [("win", s_)] + ([("wo", 0), ("wo", 1)] if s_ == 2 else [])
            S.add("pool", lambda e, ws=ws: e.dma_start(out=ws, in_=ssd_w_in[:, g * 768:(g + 1) * 768].rearrange("(k p) n -> p k n", p=128)),
                  writes=wkeys, dsem=("win", s_))
            return s_

        def ssd_block(sbi):
            c0, n, kb = SBLK[sbi]
            nt = n // 128
            samp = (sbi == 8)
            first_blk = (sbi == 0)
            bb = bs if samp else bp
            xnb = bb["xnb"]
            yn = bb["yn"]
            UM = consts[:, K_US:K_US + 128] if samp else consts[:, K_UP:K_UP + 128]
            NEG = consts[:, K_NEGS:K_NEGS + 128] if samp else consts[:, K_NEGP:K_NEGP + 128]
            LAST = consts[:, K_BD:K_BD + 128] if samp else consts[:, K_ONES:K_ONES + 128]
            dtx, dt, a_, acum, dend, cd, f2 = (cm[k_] for k_ in ("dtx", "dt", "a", "acum", "dend", "cd", "f2"))
            rmsnorm(cm["sq"], cm["rstd"], C_MN, kb, lambda k: xnb[:, k, 0:n], lambda k: [("xnb", k)], SM, ps_keys=(pk(2),), span=(c0, n))
            for t in range(nt):
                for k in range(8):
                    S.add("pe", lambda e, t=t, k=k: e.matmul(SM[:, 128 + t * 32:128 + (t + 1) * 32], lhsT=xnb[:, k, t * 128:(t + 1) * 128], rhs=wdt[:, k, :],
                                                             start=(k == 0), stop=(k == 7)),
                          reads=[("xnb", k), "wdt"], writes=[pk(2)])
            smdt = SM[:, 128:128 + nt * 32].rearrange("p (t h) -> p t h", t=nt)
            S.add("dve", lambda e: e.tensor_tensor(out=dtx[:, 0:nt, :], in0=smdt, in1=bcast_mid(rows[:, 0:32], nt), op=ALU.add),
                  reads=[pk(2), "rows"], writes=["dt"])
            S.add("act", lambda e: e.activation(out=dtx[:, 0:nt, :], in_=dtx[:, 0:nt, :], func=AF.Exp), reads=["dt"], writes=["dt"])
            S.add("act", lambda e: e.activation(out=dt[:, 0:nt, :], in_=dtx[:, 0:nt, :], func=AF.Ln, bias=1.0), reads=["dt"], writes=["dt"])
            S.add("dve", lambda e: e.tensor_tensor(out=a_[:, 0:nt, :], in0=dt[:, 0:nt, :], in1=bcast_mid(a_row, nt), op=ALU.mult),
                  reads=["dt", "a_row"], writes=["a"])
            for t in range(nt):
                S.add("pe", lambda e, t=t: e.matmul(SM[:, 256:288], lhsT=UM, rhs=a_[:, t, :], start=True, stop=True),
                      reads=["a", "consts"], writes=[pk(2)])
                S.add("act", lambda e, t=t: e.activation(out=acum[:, t, :], in_=SM[:, 256:288], func=AF.Copy), reads=[pk(2)], writes=["acum"])
                S.add("pe", lambda e, t=t: e.matmul(SM[:, 288:320], lhsT=LAST, rhs=a_[:, t, :], start=True, stop=True),
                      reads=["a", "consts"], writes=[pk(2)])
                S.add("dve", lambda e, t=t: e.tensor_tensor(out=dend[:, t, :], in0=SM[:, 288:320], in1=acum[:, t, :], op=ALU.subtract),
                      reads=[pk(2), "acum"], writes=["dend"])
                S.add("act", lambda e, t=t: e.activation(out=cd[:, t, :], in_=SM[:, 288:320], func=AF.Exp), reads=[pk(2)], writes=["cd"])
            S.add("act", lambda e: e.activation(out=dend[:, 0:nt, :], in_=dend[:, 0:nt, :], func=AF.Exp), reads=["dend"], writes=["dend"])
            S.add("dve", lambda e: e.tensor_tensor(out=f2[:, 0:nt, :], in0=dt[:, 0:nt, :], in1=dend[:, 0:nt, :], op=ALU.mult),
                  reads=["dt", "dend"], writes=["f2"])
            if samp:
                for b in range(16):
                    S.add("pe", lambda e, b=b: e.matmul(SM[:, 0:512][:, b * 32:(b + 1) * 32], lhsT=bcast_col(consts[:, K_SEQM + b:K_SEQM + b + 1], 128), rhs=a_[:, 0, :],
                                                        start=True, stop=True),
                          reads=["a", "consts", "acum", "dend", "cd"], writes=[pk(2)])
                S.add("act", lambda e: e.activation(out=cd_all, in_=SM.rearrange("p (b h) -> p b h", b=16), func=AF.Exp), reads=[pk(2)], writes=["cd_all"])
                S.add("sp", lambda e: e.dma_start(out=convh.rearrange("p j c -> p (j c)"), in_=convT), writes=["convh"], dsem="convh")

            def inproj(g, ws_i):
                ws = win_slots[ws_i]
                sz = bb["sz"][g]; xsT = bb["xsT"][g]; BT = bb["BT"][g]; CT = bb["CT"][g]; tok = bb["tok"][g]
                pend = []
                for cc in range(6):
                    pai = cctr[0] % 2
                    cctr[0] += 1
                    pa = PS[pai]
                    for k in range(8):
                        S.add("pe", lambda e, k=k, pa=pa, cc=cc: e.matmul(pa[:, 0:n], lhsT=ws[:, k, cc * 128:(cc + 1) * 128], rhs=xnb[:, k, 0:n],
                                                                         start=(k == 0), stop=(k == 7)),
                              reads=[("win", ws_i), ("xnb", k)], writes=[pk(pai)])
                    if cc < 2:
                        S.add("act", lambda e, pa=pa, cc=cc: e.activation(out=sz[:, cc, 0:n], in_=pa[:, 0:n], func=AF.Silu),
                              reads=[pk(pai)], writes=[("sz", g)])
                        continue
                    j = (2 * g + cc - 2) if cc < 4 else ((16 + g) if cc == 4 else (24 + g))
                    ri = rawctr[0] % 2
                    rawctr[0] += 1
                    raw = bb["raw"][ri]
                    acc = bb["acc"][ri]
                    if samp:
                        rdat = raw[:, :, 3:11]
                        pav = pa[:, 0:128].rearrange("p (b t) -> p b t", b=16)
                        accv = acc[:, 0:128].rearrange("p (b t) -> p b t", b=16)
                        taps = [raw[:, :, kk:kk + 8] for kk in range(3)]
                        S.add("dve", lambda e, raw=raw, j=j: e.tensor_copy(out=raw[:, :, 0:3], in_=convh[:, j, :].rearrange("p (b k) -> p b k", b=16)),
                              reads=["convh"], writes=[("raw", ri)])
                    else:
                        rdat = raw[:, 3:3 + n]
                        pav = pa[:, 0:n]
                        accv = acc[:, 0:n]
                        taps = [raw[:, kk:kk + n] for kk in range(3)]
                        if first_blk:
                            S.add("dve", lambda e, raw=raw: e.memset(raw[:, 0:3], 0.0), writes=[("raw", ri)])
                        else:
                            S.add("dve", lambda e, raw=raw, j=j: e.tensor_copy(out=raw[:, 0:3], in_=carry[:, j, :]), reads=[("carry", j)], writes=[("raw", ri)])
                    S.add("act", lambda e, rdat=rdat, pav=pav: e.activation(out=rdat, in_=pav, func=AF.Copy), reads=[pk(pai)], writes=[("raw", ri)])
                    S.add("act", lambda e, accv=accv, pav=pav, j=j: e.activation(out=accv, in_=pav, func=AF.Identity, scale=col(C_CW, 3 * 32 + j), bias=col(C_CB, j)),
                          reads=[pk(pai), "cols"], writes=[("acc", ri)])
                    while pend:
                        pend.pop(0)()
                    if samp:
                        S.add("dve", lambda e, raw=raw, j=j: e.tensor_copy(out=convh[:, j, :].rearrange("p (b k) -> p b k", b=16), in_=raw[:, :, 8:11]),
                              reads=[("raw", ri)], writes=["convh"])
                    else:
                        S.add("dve", lambda e, raw=raw, j=j: e.tensor_copy(out=carry[:, j, :], in_=raw[:, n:n + 3]), reads=[("raw", ri)], writes=[("carry", j)])
                    for kk in range(3):
                        S.add("dve", lambda e, accv=accv, tp_=taps[kk], kk=kk, j=j: e.scalar_tensor_tensor(
                            out=accv, in0=tp_, scalar=col(C_CW, kk * 32 + j), in1=accv, op0=ALU.mult, op1=ALU.add),
                            reads=[("raw", ri), ("acc", ri), "cols"], writes=[("acc", ri)])
                    if cc < 4:
                        dst = xsT[:, cc - 2, 0:n]; dk = ("xsT", g)
                    elif cc == 4:
                        dst = BT[:, 0:n]; dk = ("BT", g)
                    else:
                        dst = CT[:, 0:n]; dk = ("CT", g)
                    pend.append(lambda dst=dst, acc=acc, ri=ri, dk=dk: S.add(
                        "act", lambda e: e.activation(out=dst, in_=acc[:, 0:n], func=AF.Silu), reads=[("acc", ri)], writes=[dk]))
                while pend:
                    pend.pop(0)()
            def inproj_tr(g):
                xsT = bb["xsT"][g]; BT = bb["BT"][g]; tok = bb["tok"][g]
                for t in range(nt):
                    th = tctr[0] % 2
                    tctr[0] += 1
                    tb_ = th * 512
                    for jj in range(2):
                        S.add("pe", lambda e, jj=jj, t=t, tb_=tb_: e.transpose(PT[:, tb_ + jj * 128:tb_ + (jj + 1) * 128], xsT[:, jj, t * 128:(t + 1) * 128], ident_bf),
                              reads=[("xsT", g), "ident"], writes=["pt"])
                    S.add("pe", lambda e, t=t, tb_=tb_: e.transpose(PT[:, tb_ + 256:tb_ + 384], BT[:, t * 128:(t + 1) * 128], ident_bf),
                          reads=[("BT", g), "ident"], writes=["pt"])
                    S.add("dve", lambda e, t=t, tb_=tb_: e.tensor_copy(out=tok[:, t, :], in_=PT[:, tb_:tb_ + 384]), reads=["pt"], writes=[("tok", g)])

            def _unpack(c):
                return (c[k_] for k_ in ("t", "g", "sz", "xsT", "BT", "CT", "tok", "si", "iAB", "iCB", "iYB", "iST", "AB", "CBG", "YB", "ST",
                                         "seg", "E", "M", "Cp", "xdt", "xdtd", "y1", "y2", "ysq", "rg", "sttmp", "first"))

            def scan_prep(t, g):
                sz = bb["sz"][g]; xsT = bb["xsT"][g]; BT = bb["BT"][g]; CT = bb["CT"][g]; tok = bb["tok"][g]
                si = sctr[0] % 2
                sctr[0] += 1
                iAB, iCB, iYB, iST = 0 + si, 2 + si, 4 + si, 6
                AB, CBG, YB, ST = PS[iAB], PS[iCB], PS[iYB], PS[iST]
                seg = cm["seg"][si]; E = cm["E"][si]; M = cm["M"][si]; Cp = cm["Cp"][si]
                xdt = cm["xdt"][si]; xdtd = cm["xdtd"][si]; y1 = cm["y1"][si]; y2 = cm["y2"][si]
                ysq = cm["ysq"][si]; rg = cm["rg"][si]; sttmp = cm["sttmp"][si]
                first = (first_blk and t == 0)
                S.add("pe", lambda e: e.matmul(CBG[:, 0:128], lhsT=BT[:, t * 128:(t + 1) * 128], rhs=CT[:, t * 128:(t + 1) * 128], start=True, stop=True),
                      reads=[("BT", g), ("CT", g)], writes=[pk(iCB)])
                for hh in range(4):
                    hd = 4 * g + hh
                    S.add("pe", lambda e, hh=hh, hd=hd: e.matmul(AB[:, hh * 128:(hh + 1) * 128], lhsT=bcast_col(a_[:, t, hd:hd + 1], 128), rhs=UM, start=True, stop=True),
                          reads=["a", "consts"], writes=[pk(iAB)])
                xs4 = tok[:, t, 0:256].rearrange("p (h q) -> p h q", h=4)
                S.add("pool", lambda e: e.tensor_tensor(out=xdt.rearrange("p (h q) -> p h q", h=4), in0=xs4, in1=bcast_last(dt[:, t, 4 * g:4 * g + 4], 64), op=ALU.mult),
                      reads=[("tok", g), "dt"], writes=[("xdt", si)])
                S.add("pool", lambda e: e.tensor_tensor(out=xdtd.rearrange("p (h q) -> p h q", h=4), in0=xs4, in1=bcast_last(f2[:, t, 4 * g:4 * g + 4], 64), op=ALU.mult),
                      reads=[("tok", g), "f2"], writes=[("xdtd", si)])
                if not samp and not first:
                    hsl0 = hstate[:, g * 256:(g + 1) * 256]
                    S.add("pool", lambda e: e.tensor_tensor(out=sttmp.rearrange("p (h q) -> p h q", h=4), in0=hsl0.rearrange("p (h q) -> p h q", h=4),
                                                           in1=bcast_last(cd[:, t, 4 * g:4 * g + 4], 64), op=ALU.mult),
                          reads=[("hst", g), "cd"], writes=[("sttmp", si)])
                for hh in range(4):
                    hd = 4 * g + hh
                    S.add("dve", lambda e, hh=hh, hd=hd: e.scalar_tensor_tensor(
                        out=seg[:, hh, :], in0=AB[:, hh * 128:(hh + 1) * 128], scalar=acum[:, t, hd:hd + 1], in1=NEG, op0=ALU.subtract, op1=ALU.add),
                        reads=[pk(iAB), "acum", "consts"], writes=[("seg", si)])
                S.add("act", lambda e: e.activation(out=seg, in_=seg, func=AF.Exp), reads=[("seg", si)], writes=[("seg", si)])
                S.add("act", lambda e: e.activation(out=E, in_=AB.rearrange("p (h l) -> p h l", h=4), func=AF.Exp), reads=[pk(iAB)], writes=[("E", si)])
                S.add("dve", lambda e: e.tensor_tensor(out=M, in0=seg, in1=bcast_mid(CBG[:, 0:128], 4), op=ALU.mult),
                      reads=[("seg", si), pk(iCB)], writes=[("M", si)])
                S.add("pool", lambda e: e.tensor_tensor(out=Cp, in0=E, in1=bcast_mid(CT[:, t * 128:(t + 1) * 128], 4), op=ALU.mult),
                      reads=[("E", si), ("CT", g)], writes=[("Cp", si)])
                return dict(locals())

            def scan_state(c):
                (t, g, sz, xsT, BT, CT, tok, si, iAB, iCB, iYB, iST, AB, CBG, YB, ST,
                 seg, E, M, Cp, xdt, xdtd, y1, y2, ysq, rg, sttmp, first) = _unpack(c)
                if samp:
                    for hh in range(4):
                        jj, half = hh // 2, hh % 2
                        yo = YB[half * 64:(half + 1) * 64, jj * 128:(jj + 1) * 128]
                        if hh < 2:
                            S.add("pe", lambda e, yo=yo, hh=hh: e.matmul(yo, lhsT=xdt[:, hh * 64:(hh + 1) * 64], rhs=M[:, hh, :], start=True, stop=True),
                                  reads=[("xdt", si), ("M", si)], writes=[pk(iYB)])
                        else:
                            S.add("pe", lambda e, yo=yo, hh=hh: e.matmul(yo, lhsT=xdt[:, hh * 64:(hh + 1) * 64], rhs=M[:, hh, :], start=False, stop=True, skip_group_check=True),
                                  reads=[("xdt", si), ("M", si)], writes=[pk(iYB)])
                    S.add("pool", lambda e: e.tensor_tensor(out=Bm, in0=bcast_mid(tok[:, 0, 256:384], 16), in1=bcast_last(consts[:, K_SEQM:K_SEQM + 16], 128), op=ALU.mult),
                          reads=[("tok", g), "consts"], writes=["Bm"])
                    for hf in range(2):
                        hs_i = hctr[0] % 2
                        hctr[0] += 1
                        hg = h0g[hs_i]
                        hb_ = h0bf[hs_i]
                        S.add("sp", lambda e, hg=hg, hf=hf: e.dma_start(out=hg.rearrange("p b c -> p (b c)"), in_=ssmT[g][:, hf * 2048:(hf + 1) * 2048]),
                              writes=[("h0g", hs_i)], dsem=("h0g", hs_i))
                        S.add("act", lambda e, hg=hg, hb_=hb_: e.activation(out=hb_, in_=hg, func=AF.Copy), reads=[("h0g", hs_i)], writes=[("h0bf", hs_i)])
                        for hh in range(4):
                            jj, half = hh // 2, hh % 2
                            for bl in range(8):
                                b = hf * 8 + bl
                                S.add("pe", lambda e, b=b, bl=bl, hh=hh, half=half, jj=jj, hb_=hb_: e.matmul(
                                    YB[half * 64:(half + 1) * 64, jj * 128 + b * 8:jj * 128 + (b + 1) * 8], lhsT=hb_[:, bl, hh * 64:(hh + 1) * 64],
                                    rhs=Cp[:, hh, b * 8:(b + 1) * 8], start=False, stop=True, skip_group_check=True),
                                    reads=[("h0bf", hs_i), ("Cp", si)], writes=[pk(iYB)])
                        for bl in range(8):
                            b = hf * 8 + bl
                            pq = PS[6]
                            S.add("pe", lambda e, b=b, pq=pq: e.matmul(pq[:, 0:256], lhsT=Bm[:, b, :], rhs=xdtd, start=True, stop=True),
                                  reads=["Bm", ("xdtd", si)], writes=[pk(6)])
                            S.add("pool", lambda e, b=b, bl=bl, hg=hg: e.tensor_tensor(out=hg[:, bl, :].rearrange("p (h q) -> p h q", h=4), in0=hg[:, bl, :].rearrange("p (h q) -> p h q", h=4),
                                                                                   in1=bcast_last(cd_all[:, b, 4 * g:4 * g + 4], 64), op=ALU.mult),
                                  reads=[("h0g", hs_i), ("h0bf", hs_i), "cd_all"], writes=[("h0g", hs_i)])
                            S.add("dve", lambda e, bl=bl, pq=pq, hg=hg: e.tensor_tensor(out=hg[:, bl, :], in0=hg[:, bl, :], in1=pq[:, 0:256], op=ALU.add),
                                  reads=[("h0g", hs_i), pk(6)], writes=[("h0g", hs_i)])
                        S.add("sp", lambda e, hg=hg, hf=hf: e.dma_start(out=ssm_s_out[g][:, hf * 2048:(hf + 1) * 2048], in_=hg.rearrange("p b c -> p (b c)")),
                              reads=[("h0g", hs_i)], dsem="ssm_s_out")
                else:
                    for hh in range(4):
                        hd = 4 * g + hh
                        jj, half = hh // 2, hh % 2
                        yo = YB[half * 64:(half + 1) * 64, jj * 128:(jj + 1) * 128]
                        S.add("pe", lambda e, yo=yo, hh=hh: e.matmul(yo, lhsT=xdt[:, hh * 64:(hh + 1) * 64], rhs=M[:, hh, :], start=True, stop=first),
                              reads=[("xdt", si), ("M", si)], writes=[pk(iYB)])
                        if not first:
                            S.add("pe", lambda e, yo=yo, hd=hd, hh=hh: e.matmul(yo, lhsT=hbf[:, hd * 64:(hd + 1) * 64], rhs=Cp[:, hh, :], start=False, stop=True),
                                  reads=[("hbf", g), ("Cp", si)], writes=[pk(iYB)])
                    S.add("pe", lambda e: e.matmul(ST[:, 0:256], lhsT=tok[:, t, 256:384], rhs=xdtd, start=True, stop=True),
                          reads=[("tok", g), ("xdtd", si)], writes=[pk(iST)])
                    hsl = hstate[:, g * 256:(g + 1) * 256]
                    hbl = hbf[:, g * 256:(g + 1) * 256]
                    if first:
                        S.add("dve", lambda e: e.tensor_copy(out=hbl, in_=ST[:, 0:256]), reads=[pk(iST)], writes=[("hbf", g)])
                        S.add("act", lambda e: e.activation(out=hsl, in_=ST[:, 0:256], func=AF.Copy), reads=[pk(iST)], writes=[("hst", g)])
                    else:
                        S.add("dve", lambda e: e.tensor_tensor(out=hbl, in0=sttmp, in1=ST[:, 0:256], op=ALU.add),
                              reads=[("sttmp", si), pk(iST)], writes=[("hbf", g)])
                        S.add("dve", lambda e: e.tensor_tensor(out=hsl, in0=sttmp, in1=ST[:, 0:256], op=ALU.add),
                              reads=[("sttmp", si), pk(iST)], writes=[("hst", g)])

            def scan_post(c):
                (t, g, sz, xsT, BT, CT, tok, si, iAB, iCB, iYB, iST, AB, CBG, YB, ST,
                 seg, E, M, Cp, xdt, xdtd, y1, y2, ysq, rg, sttmp, first) = _unpack(c)
                for jj in range(2):
                    j = 2 * g + jj
                    S.add("dve", lambda e, jj=jj, j=j: e.scalar_tensor_tensor(
                        out=y1[:, jj, :], in0=xsT[:, jj, t * 128:(t + 1) * 128], scalar=col(C_DD, j), in1=YB[:, jj * 128:(jj + 1) * 128], op0=ALU.mult, op1=ALU.add),
                        reads=[("xsT", g), pk(iYB), "cols"], writes=[("y1", si)])
                S.add("pool", lambda e: e.tensor_tensor(out=y2, in0=y1, in1=sz[:, :, t * 128:(t + 1) * 128], op=ALU.mult),
                      reads=[("y1", si), ("sz", g)], writes=[("y2", si)])
                S.add("act", lambda e: e.activation(out=ysq, in_=y2, func=AF.Square), reads=[("y2", si)], writes=[("ysq", si)])
                for jj in range(2):
                    S.add("pe", lambda e, jj=jj: e.matmul(YB[:, 256:384], lhsT=ones_bf, rhs=ysq[:, jj, :], start=(jj == 0), stop=(jj == 1)),
                          reads=[("ysq", si), "ones"], writes=[pk(iYB)])
                S.add("act", lambda e: e.activation(out=rg, in_=YB[:, 256:384], func=AF.Ln, bias=EPS, scale=1.0 / 256.0), reads=[pk(iYB)], writes=[("rg", si)])
                S.add("act", lambda e: e.activation(out=rg, in_=rg, func=AF.Exp, scale=-0.5), reads=[("rg", si)], writes=[("rg", si)])
                for jj in range(2):
                    j = 2 * g + jj
                    S.add("dve", lambda e, jj=jj, j=j: e.scalar_tensor_tensor(
                        out=yn[:, j, t * 128:(t + 1) * 128], in0=y2[:, jj, :], scalar=col(C_SN, j), in1=rg, op0=ALU.mult, op1=ALU.mult),
                        reads=[("y2", si), ("rg", si), "cols"], writes=[("yn", j)])

            PH = int(_os.environ.get("SSD_PH", "3"))
            for g in range(8):
                ws_cur = get_win(sbi, g)
                if g + 1 < 8:
                    get_win(sbi, g + 1)
                if g + 2 < 8:
                    get_win(sbi, g + 2)
                inproj(g, ws_cur)
                if g >= 1:
                    inproj_tr(g - 1)
            inproj_tr(7)
            get_win(sbi + 1, 0)
            get_win(sbi + 1, 1)
            get_wo(sbi, 0)
            get_wo(sbi, 1)
            if PH < 2:
                return
            its = [(t, g) for t in range(nt) for g in range(8)]
            ctxs = {}
            if samp:
                for (t_, g_) in its:
                    c_ = scan_prep(t_, g_)
                    scan_state(c_)
                    scan_post(c_)
                its = []
            for i in range((len(its) + 2) if its else 0):
                if i < len(its):
                    ctxs[i] = scan_prep(*its[i])
                if 0 <= i - 1 < len(its):
                    scan_state(ctxs[i - 1])
                if 0 <= i - 2 < len(its):
                    scan_post(ctxs.pop(i - 2))
            if PH < 3:
                return
            for q in range(4):
                ws_i = get_wo(sbi, q)
                wq = wo_slots[ws_i]
                for m in range(8):
                    po = PS[m % 2]
                    for kk in range(4):
                        S.add("pe", lambda e, kk=kk, m=m, po=po, wq=wq, q=q: e.matmul(po[:, 0:n], lhsT=wq[:, kk, m * 128:(m + 1) * 128], rhs=yn[:, q * 4 + kk, 0:n],
                                                                                    start=(kk == 0), stop=(kk == 3)),
                              reads=[("wo", ws_i), ("yn", q * 4 + kk)], writes=[pk(m % 2)])
                    S.add("dve", lambda e, m=m, po=po: e.tensor_tensor(out=x[:, m, c0:c0 + n], in0=x[:, m, c0:c0 + n], in1=po[:, 0:n], op=ALU.add),
                          reads=[pk(m % 2), ("x", m, kb)], writes=[("x", m, kb)])

        for sbi in range(int(_os.environ.get("SSD_NB", "8"))):
            ssd_block(sbi)
        S.add("sp", lambda e: e.dma_start(out=ssm_p_out, in_=hstate), reads=[("hst", g) for g in range(8)], dsem="ssm_p_out")
        S.add("sp", lambda e: e.dma_start(out=conv_p_out, in_=carry.rearrange("p j k -> p (j k)")), reads=[("carry", j) for j in range(32)], dsem="conv_p_out")
        S.barrier()
        if _os.environ.get("SSD_NOSAMP"):
            return
        ssd_block(8)
        S.add("sp", lambda e: e.dma_start(out=conv_s_out, in_=convh.rearrange("p j c -> p (j c)")), reads=["convh"], dsem="conv_s_out")

    if do_ffn:
        ffn_phase(0, 0)
    S.barrier()
    if do_ssd:
        ssd_phase()
        out_dsems += ["ssm_p_out", "conv_p_out", "ssm_s_out", "conv_s_out"]
    S.barrier()
    if do_ffn:
        ffn_phase(0, 1)
        ffn_phase(1, 0)
    S.barrier()
    if do_pool:
        pool_phase()
        out_dsems += ["pool_p_out", "pool_s_out"]
    S.barrier()
    if do_ffn:
        ffn_phase(1, 1)
    S.barrier()
    ar = Arena(un, UW)
    sq = [ar.bf16(512), ar.bf16(512)]
    rstd = ar.f32(512)
    yo = [ar.f32(8, 512), ar.f32(8, 512)]
    yTv = yT.rearrange("(k p) t -> p k t", p=128)
    for bi in range(5):
        c0, n = BLOCKS[bi]
        yb = yo[bi % 2]
        rmsnorm(sq, rstd, C_FIN, bi, lambda k, yb=yb, n=n: yb[:, k, 0:n], lambda k, bi=bi: [("yo", bi % 2)], PS[6])
        S.add("sp", lambda e, yb=yb, c0=c0, n=n: e.dma_start(out=yTv[:, :, c0:c0 + n], in_=yb[:, :, 0:n]), reads=[("yo", bi % 2)], dsem="yT")
    out_dsems.append("yT")
    S.emit(out_dsems=out_dsems)
    return nc


def _consts():
    c = np.zeros((128, NCONST), np.float32)
    s = np.arange(128)[:, None]
    l = np.arange(128)[None, :]
    same = (s // 8) == (l // 8)
    c[:, K_UP:K_UP + 128] = (s <= l)
    c[:, K_US:K_US + 128] = (s <= l) & same
    c[:, K_NEGP:K_NEGP + 128] = np.where(l >= s, 0.0, -30000.0)
    c[:, K_NEGS:K_NEGS + 128] = np.where((l >= s) & same, 0.0, -30000.0)
    c[:, K_ONES:K_ONES + 128] = 1.0
    c[:, K_BD:K_BD + 128] = same
    c[:, K_SEQM:K_SEQM + 16] = (np.arange(128)[:, None] // 8) == np.arange(16)[None, :]
    for wi, w in enumerate(POOL_WINDOWS):
        c[:, K_INVC + wi * 16:K_INVC + (wi + 1) * 16] = 1.0 / np.minimum(float(w), np.arange(16) + 1.0)[None, :]
    c[:, K_ID:K_ID + 128] = np.eye(128)
    return c


def _colmajor(v):
    v = np.asarray(v, np.float32)
    return np.ascontiguousarray(v.reshape(-1, 128).T)


_PROG = {}
_WIN_PERM = np.concatenate([np.r_[g * 256:(g + 1) * 256, 2048 + g * 256:2048 + (g + 1) * 256,
                                  4096 + g * 128:4096 + (g + 1) * 128, 5120 + g * 128:5120 + (g + 1) * 128] for g in range(8)]
                           + [np.arange(6144, 6176)])


def kernel(x_prompt, x_sample, state_ssm, state_conv, state_pool,
           ffn_norm, ffn_w_gate, ffn_w_up, ffn_w_down, mix_norm,
           ssd_w_in, ssd_conv_w, ssd_conv_b, ssd_dt_bias, ssd_a_log, ssd_d, ssd_norm, ssd_w_out,
           pool_w_in, pool_w_group, pool_scale, pool_w_out, final_norm, _flags=(True, True, True)):
    f = np.float32
    cols = np.zeros((128, NCOL), f)
    for i in range(2):
        for j in range(2):
            o = C_FN + (i * 2 + j) * 8
            cols[:, o:o + 8] = _colmajor(ffn_norm[i, j])
    for i in range(2):
        cols[:, C_MN + i * 8:C_MN + (i + 1) * 8] = _colmajor(mix_norm[i])
    cols[:, C_FIN:C_FIN + 8] = _colmajor(final_norm)
    for k in range(4):
        cols[:, C_CW + k * 32:C_CW + (k + 1) * 32] = _colmajor(ssd_conv_w[0, k])
    cols[:, C_CB:C_CB + 32] = _colmajor(ssd_conv_b[0])
    cols[:, C_DD:C_DD + 16] = _colmajor(np.repeat(np.asarray(ssd_d[0], f), 64))
    cols[:, C_SN:C_SN + 16] = _colmajor(ssd_norm[0])
    cols[:, C_PS:C_PS + 8] = _colmajor(pool_scale[0])
    rows = np.zeros((128, 64), f)
    rows[:, 0:32] = np.asarray(ssd_dt_bias[0], f)[None, :]
    rows[:, 32:64] = np.asarray(ssd_a_log[0], f)[None, :]
    consts = _consts()

    shared = {
        "cols": cols, "rows": rows, "consts": consts,
        "ffn_w_gate": np.ascontiguousarray(ffn_w_gate, f), "ffn_w_up": np.ascontiguousarray(ffn_w_up, f),
        "ffn_w_down": np.ascontiguousarray(ffn_w_down, f),
        "ssd_w_in": np.ascontiguousarray(np.asarray(ssd_w_in[0], f)[:, _WIN_PERM]), "ssd_w_out": np.ascontiguousarray(ssd_w_out[0], f),
        "pool_w_in": np.ascontiguousarray(pool_w_in[0], f), "pool_w_group": np.ascontiguousarray(pool_w_group[0], f),
        "pool_w_out": np.ascontiguousarray(pool_w_out[0], f),
    }
    in_maps = []
    for c in range(NCORES):
        sl = slice(c * 16, (c + 1) * 16)
        xs = np.asarray(x_sample[sl], f).reshape(128, 1024)
        xT = np.ascontiguousarray(np.concatenate([np.asarray(x_prompt[c], f), xs], axis=0).T)
        ssmT = np.ascontiguousarray(np.asarray(state_ssm[0, sl], f).reshape(16, 8, 256, 128).transpose(1, 3, 0, 2)).reshape(8, 128, 4096)
        convT = np.ascontiguousarray(np.asarray(state_conv[0, sl], f).reshape(16, 3, 32, 128).transpose(3, 2, 0, 1)).reshape(128, 32 * 48)
        poolT = np.ascontiguousarray(np.asarray(state_pool[0, sl], f).transpose(2, 0, 1))
        m = dict(shared)
        m.update({"xT": xT, "ssmT": ssmT, "convT": convT, "poolT": poolT})
        in_maps.append(m)

    key = tuple(_flags)
    if key not in _PROG:
        _PROG[key] = build_program(*_flags)
    nc = _PROG[key]
    res = run_bass_kernel_spmd(nc, in_maps, core_ids=list(range(NCORES)))
    R = res.results

    y_prompt = np.zeros((8, 2048, 1024), f)
    y_sample = np.zeros((128, 8, 1024), f)
    ssm_p = np.zeros((1, 8, 32, 64, 128), f)
    conv_p = np.zeros((1, 8, 3, 4096), f)
    pool_p = np.zeros((1, 8, 15, 1024), f)
    ssm_s = np.zeros((1, 128, 32, 64, 128), f)
    conv_s = np.zeros((1, 128, 3, 4096), f)
    pool_s = np.zeros((1, 128, 15, 1024), f)
    for c in range(NCORES):
        r = R[c]
        yT = np.asarray(r["yT"])
        y_prompt[c] = yT[:, :2048].T
        y_sample[c * 16:(c + 1) * 16] = yT[:, 2048:].T.reshape(16, 8, 1024)
        if "ssm_p_out" in r:
            ssm_p[0, c] = np.asarray(r["ssm_p_out"]).T.reshape(32, 64, 128)
            conv_p[0, c] = np.asarray(r["conv_p_out"]).reshape(128, 32, 3).transpose(2, 1, 0).reshape(3, 4096)
            ssm_s[0, c * 16:(c + 1) * 16] = np.asarray(r["ssm_s_out"]).reshape(8, 128, 16, 256).transpose(2, 0, 3, 1).reshape(16, 32, 64, 128)
            conv_s[0, c * 16:(c + 1) * 16] = np.asarray(r["conv_s_out"]).reshape(128, 32, 16, 3).transpose(2, 3, 1, 0).reshape(16, 3, 4096)
        if "pool_p_out" in r:
            pool_p[0, c] = np.asarray(r["pool_p_out"]).T
            pool_s[0, c * 16:(c + 1) * 16] = np.asarray(r["pool_s_out"]).transpose(1, 2, 0)
    return (y_prompt, y_sample, ssm_p, conv_p, pool_p, ssm_s, conv_s, pool_s)
```

```python
import numpy as np
import concourse.bass as bass
import concourse.mybir as mybir
from concourse.bass_utils import run_bass_kernel_spmd

F32 = mybir.dt.float32
BF16 = mybir.dt.bfloat16
AF = mybir.ActivationFunctionType
ALU = mybir.AluOpType

NCORES = 8
import os as _os
NTOK = 2176
BLOCKS = [(0, 512), (512, 512), (1024, 512), (1536, 512), (2048, 128)]
D_FF = 2816
EPS = 1e-6
POOL_WINDOWS = (2, 4, 8, 16)

C_FN, C_MN, C_FIN, C_CW, C_CB, C_DD, C_SN, C_PS, NCOL = 0, 32, 48, 56, 184, 216, 232, 248, 256
K_UP, K_US, K_NEGP, K_NEGS, K_ONES, K_BD, K_SEQM, K_INVC, K_ID, NCONST = 0, 128, 256, 384, 512, 640, 768, 784, 848, 976


class Op:
    __slots__ = ("eng", "fn", "waits", "signaled", "sigval", "pos", "clock", "is_dma", "dsem", "dval")


class Sched:
    ENGS = ("pe", "act", "dve", "pool", "sp")

    def __init__(self, nc):
        self.nc = nc
        self.ops = []
        self.streams = {e: [] for e in self.ENGS}
        self.clock = {e: {} for e in self.ENGS}
        self.dma_waited = {e: {} for e in self.ENGS}
        self.last_writer = {}
        self.readers = {}
        self.dsem_total = {}
        self.pending = {e: [] for e in self.ENGS}

    def add(self, eng, fn, reads=(), writes=(), dsem=None):
        op = Op()
        op.eng = eng; op.fn = fn; op.waits = []; op.signaled = False; op.sigval = None
        op.is_dma = dsem is not None; op.dsem = dsem; op.dval = None
        op.pos = len(self.streams[eng])
        deps = []
        for k in reads:
            lw = self.last_writer.get(k)
            if lw is not None:
                deps.append(lw)
            if k in ("pt", "psn") or (isinstance(k, tuple) and k[0] in ("ps", "pg", "pu", "pd")):
                deps.extend(r for r in self.readers.get(k, ()) if r.eng != eng)
        for k in writes:
            lw = self.last_writer.get(k)
            if lw is not None:
                deps.append(lw)
            deps.extend(self.readers.get(k, ()))
        forced = self.pending[eng]
        if forced:
            deps.extend(forced)
            self.pending[eng] = []
        clk = self.clock[eng]
        seen = set()
        deps.sort(key=lambda d_: -d_.pos)
        for d in deps:
            if id(d) in seen or d is op:
                continue
            seen.add(id(d))
            if d.is_dma:
                tot = self.dsem_total[d.dsem]
                if self.dma_waited[eng].get(d.dsem, 0) < tot:
                    op.waits.append(("dma", d.dsem, tot))
                    self.dma_waited[eng][d.dsem] = tot
            else:
                if d.eng == eng and (eng == "pe" or op.pos - d.pos > 6):
                    continue
                if clk.get(d.eng, -1) >= d.pos:
                    continue
                op.waits.append(("eng", d))
                d.signaled = True
                for e2, p2 in d.clock.items():
                    if clk.get(e2, -1) < p2:
                        clk[e2] = p2
                if clk.get(d.eng, -1) < d.pos:
                    clk[d.eng] = d.pos
        if op.is_dma:
            self.dsem_total[dsem] = self.dsem_total.get(dsem, 0) + 16
            op.dval = self.dsem_total[dsem]
            op.clock = None
        else:
            op.clock = dict(clk)
        for k in reads:
            self.readers.setdefault(k, []).append(op)
        for k in writes:
            self.last_writer[k] = op
            self.readers[k] = []
        self.ops.append(op)
        self.streams[eng].append(op)
        return op

    def barrier(self):
        lasts = []
        for e in self.ENGS:
            st = [o for o in self.streams[e][-1:] if not o.is_dma]
            for o in reversed(self.streams[e]):
                if not o.is_dma:
                    lasts.append(o)
                    break
        dmas = {}
        for o in self.ops:
            if o.is_dma:
                dmas[o.dsem] = o
        for e in self.ENGS:
            self.pending[e] = [o for o in lasts if o.eng != e] + list(dmas.values())

    def emit(self, out_dsems=()):
        nc = self.nc
        engobj = {"pe": nc.tensor, "act": nc.scalar, "dve": nc.vector, "pool": nc.gpsimd, "sp": nc.sync}
        esem = {e: nc.alloc_semaphore("es_" + e) for e in self.ENGS}
        dsems = {}
        for op in self.ops:
            if op.is_dma and op.dsem not in dsems:
                dsems[op.dsem] = nc.alloc_semaphore("ds_%d" % len(dsems))
        for e in self.ENGS:
            n = 0
            for op in self.streams[e]:
                if op.signaled:
                    n += 1
                    op.sigval = n
        for op in self.ops:
            eo = engobj[op.eng]
            for w in op.waits:
                if w[0] == "dma":
                    eo.wait_ge(dsems[w[1]], w[2])
                else:
                    eo.wait_ge(esem[w[1].eng], w[1].sigval)
            inst = op.fn(eo)
            if op.is_dma:
                inst.then_inc(dsems[op.dsem], 16)
            elif op.signaled:
                inst.then_inc(esem[op.eng], 1)
        for ds in out_dsems:
            if ds in dsems:
                nc.sync.wait_ge(dsems[ds], self.dsem_total[ds])


class Arena:
    def __init__(self, flat, nwords):
        self.flat = flat
        self.n = nwords
        self.off = 0

    def _take(self, words):
        words = (words + 7) // 8 * 8
        o = self.off
        self.off += words
        assert self.off <= self.n, ("arena overflow", self.off, self.n)
        return o

    @staticmethod
    def _shape(v, shape):
        if len(shape) == 1:
            return v
        if len(shape) == 2:
            return v.rearrange("p (a b) -> p a b", a=shape[0])
        if len(shape) == 3:
            return v.rearrange("p (a b c) -> p a b c", a=shape[0], b=shape[1])
        raise ValueError(shape)

    def f32(self, *shape):
        n = int(np.prod(shape))
        o = self._take(n)
        return self._shape(self.flat[:, o:o + n], shape)

    def bf16(self, *shape):
        n = int(np.prod(shape))
        assert n % 2 == 0
        o = self._take(n // 2)
        return self._shape(self.flat[:, o:o + n // 2].bitcast(BF16), shape)


def bcast_mid(ap2d, rep):
    a = ap2d.ap
    assert len(a) == 2
    return bass.AP(ap2d.tensor, ap2d.offset, [list(a[0]), [0, rep], list(a[1])])


def bcast_last(ap2d, rep):
    a = ap2d.ap
    assert len(a) == 2
    return bass.AP(ap2d.tensor, ap2d.offset, [list(a[0]), list(a[1]), [0, rep]])


def bcast_col(ap_col, rep):
    a = ap_col.ap
    return bass.AP(ap_col.tensor, ap_col.offset, [list(a[0]), [0, rep]])


def build_program(do_ffn=True, do_ssd=True, do_pool=True):
    nc = bass.Bass("TRN2", target_bir_lowering=False)

    def din(name, shape):
        return nc.dram_tensor(name, list(shape), F32, kind="ExternalInput").ap()

    def dout(name, shape):
        return nc.dram_tensor(name, list(shape), F32, kind="ExternalOutput").ap()

    xT = din("xT", [1024, NTOK])
    ssmT = din("ssmT", [8, 128, 16 * 256])
    convT = din("convT", [128, 32 * 48])
    poolT = din("poolT", [1024, 16, 15])
    cols_d = din("cols", [128, NCOL])
    rows_d = din("rows", [128, 64])
    const_d = din("consts", [128, NCONST])
    w_gate = din("ffn_w_gate", [2, 2, 1024, D_FF])
    w_up = din("ffn_w_up", [2, 2, 1024, D_FF])
    w_down = din("ffn_w_down", [2, 2, D_FF, 1024])
    ssd_w_in = din("ssd_w_in", [1024, 6176])
    ssd_w_out = din("ssd_w_out", [2048, 1024])
    pool_w_in = din("pool_w_in", [1024, 1024])
    pool_w_group = din("pool_w_group", [4, 256, 256])
    pool_w_out = din("pool_w_out", [1024, 1024])

    yT = dout("yT", [1024, NTOK])
    ssm_p_out = dout("ssm_p_out", [128, 2048])
    conv_p_out = dout("conv_p_out", [128, 32 * 3])
    pool_p_out = dout("pool_p_out", [1024, 15])
    ssm_s_out = dout("ssm_s_out", [8, 128, 16 * 256])
    conv_s_out = dout("conv_s_out", [128, 32 * 48])
    pool_s_out = dout("pool_s_out", [1024, 16, 15])

    def sb(name, shape, dt):
        return nc.alloc_sbuf_tensor(name, list(shape), dt).ap()

    x = sb("x", [128, 8, NTOK], F32)
    wA = sb("wA", [128, 8, 1024], BF16)
    wBf = sb("wB", [128, 12288], BF16)
    cols = sb("colsb", [128, NCOL], F32)
    rows = sb("rowsb", [128, 64], F32)
    consts = sb("constsb", [128, NCONST], F32)
    ident_bf = sb("ident_bf", [128, 128], BF16)
    ones_bf = sb("ones_bf", [128, 128], BF16)
    a_row = sb("a_row", [128, 32], F32)
    wdt = sb("wdt", [128, 8, 32], BF16)
    wgrp = sb("wgrp", [128, 8, 256], BF16)
    UW = 22944
    un = sb("union", [128, UW], F32)

    PS = [nc.alloc_psum_tensor("ps%d" % i, [128, 512], F32).ap() for i in range(7)]
    PT = nc.alloc_psum_tensor("pst", [128, 1024], BF16).ap()

    S = Sched(nc)
    out_dsems = []

    S.add("sp", lambda e: e.dma_start(out=cols, in_=cols_d), writes=["cols"], dsem="cols")
    S.add("sp", lambda e: e.dma_start(out=rows, in_=rows_d), writes=["rows"], dsem="rows")
    S.add("sp", lambda e: e.dma_start(out=consts, in_=const_d), writes=["consts"], dsem="consts")
    xTv = xT.rearrange("(k p) t -> p k t", p=128)
    for k in range(8):
        S.add("sp", lambda e, k=k: e.dma_start(out=x[:, k, :], in_=xTv[:, k, :]), writes=[("x", k, b) for b in range(5)], dsem="xin")
    S.add("dve", lambda e: e.tensor_copy(out=ident_bf, in_=consts[:, K_ID:K_ID + 128]), reads=["consts"], writes=["ident"])
    S.add("dve", lambda e: e.tensor_copy(out=ones_bf, in_=consts[:, K_ONES:K_ONES + 128]), reads=["consts"], writes=["ones"])

    def col(off, k):
        return cols[:, off + k:off + k + 1]

    nrm_ctr = [0]

    def rmsnorm(ar_sq, ar_rstd, goff, bi, out_fn, out_keys_fn, ps_norm, ps_keys=("psn",), span=None):
        c0, n = BLOCKS[bi] if span is None else span
        for k in range(8):
            i = 0 if ar_sq[0] is ar_sq[1] else nrm_ctr[0] % 2
            nrm_ctr[0] += 1
            sq = ar_sq[i]
            S.add("act", lambda e, k=k, sq=sq: e.activation(out=sq[:, 0:n], in_=x[:, k, c0:c0 + n], func=AF.Square),
                  reads=[("x", k, bi)], writes=[("sq", i)])
            S.add("pe", lambda e, k=k, sq=sq: e.matmul(ps_norm[:, 0:n], lhsT=ones_bf, rhs=sq[:, 0:n], start=(k == 0), stop=(k == 7)),
                  reads=[("sq", i), "ones"], writes=list(ps_keys))
        S.add("act", lambda e: e.activation(out=ar_rstd[:, 0:n], in_=ps_norm[:, 0:n], func=AF.Ln, bias=EPS, scale=1.0 / 1024.0),
              reads=list(ps_keys), writes=["rstd"])
        S.add("act", lambda e: e.activation(out=ar_rstd[:, 0:n], in_=ar_rstd[:, 0:n], func=AF.Exp, scale=-0.5), reads=["rstd"], writes=["rstd"])
        for k in range(8):
            S.add("dve", lambda e, k=k: e.scalar_tensor_tensor(out=out_fn(k), in0=x[:, k, c0:c0 + n], scalar=col(goff, k),
                                                               in1=ar_rstd[:, 0:n], op0=ALU.mult, op1=ALU.mult),
                  reads=[("x", k, bi), "rstd", "cols"], writes=out_keys_fn(k))

    SPLITS = [(0, 4), (4, 4), (8, 3)]

    def ffn_phase(i, j):
        ar = Arena(un, UW)
        xn = ar.bf16(8, NTOK)
        h = ar.bf16(8, NTOK)
        sq = [ar.bf16(512), ar.bf16(512)]
        rstd = ar.f32(512)
        sg = [ar.bf16(512), ar.bf16(512)]
        goff = C_FN + (i * 2 + j) * 8
        for bi in range(5):
            c0, n = BLOCKS[bi]
            rmsnorm(sq, rstd, goff, bi, lambda k, c0=c0, n=n: xn[:, k, c0:c0 + n], lambda k, bi=bi: [("xn", k, bi)], PS[6])
        wgd = w_gate[i, j]
        wud = w_up[i, j]
        wdd = w_down[i, j]
        wslots = [wBf[:, s * 2048:(s + 1) * 2048].rearrange("p (k n) -> p k n", k=8) for s in range(4)]
        ev = 0

        def load_gu(cc):
            sg_slot = (cc % 2) * 2
            su_slot = sg_slot + 1
            wg_s = wslots[sg_slot]
            wu_s = wslots[su_slot]
            S.add("pool", lambda e: e.dma_start(out=wg_s, in_=wgd[:, cc * 256:(cc + 1) * 256].rearrange("(k p) n -> p k n", p=128)),
                  writes=[("wB", sg_slot)], dsem=("wB", sg_slot))
            S.add("pool", lambda e: e.dma_start(out=wu_s, in_=wud[:, cc * 256:(cc + 1) * 256].rearrange("(k p) n -> p k n", p=128)),
                  writes=[("wB", su_slot)], dsem=("wB", su_slot))

        load_gu(0)
        for (cc0, ncc) in SPLITS:
            nh = ncc * 2
            r0 = cc0 * 256
            S.add("pool", lambda e, r0=r0, nh=nh: e.dma_start(out=wA[:, 0:nh, :], in_=wdd[r0:r0 + nh * 128, :].rearrange("(k p) n -> p k n", p=128)),
                  writes=["wA"], dsem="wA")
            for cc in range(cc0, cc0 + ncc):
                sg_slot = (cc % 2) * 2
                su_slot = sg_slot + 1
                wg_s = wslots[sg_slot]
                wu_s = wslots[su_slot]
                if cc + 1 < 11:
                    load_gu(cc + 1)
                for sub in range(2):
                    hc = (cc - cc0) * 2 + sub
                    for bi in range(5):
                        c0, n = BLOCKS[bi]
                        pg = PS[ev % 2]
                        pu = PS[2 + ev % 2]
                        sgt = sg[ev % 2]
                        evi = ev % 2
                        ev += 1
                        for k in range(8):
                            S.add("pe", lambda e, k=k, pg=pg, wg_s=wg_s, sub=sub, c0=c0, n=n: e.matmul(
                                pg[:, 0:n], lhsT=wg_s[:, k, sub * 128:(sub + 1) * 128], rhs=xn[:, k, c0:c0 + n], start=(k == 0), stop=(k == 7)),
                                reads=[("wB", sg_slot), ("xn", k, bi)], writes=[("pg", evi)])
                        for k in range(8):
                            S.add("pe", lambda e, k=k, pu=pu, wu_s=wu_s, sub=sub, c0=c0, n=n: e.matmul(
                                pu[:, 0:n], lhsT=wu_s[:, k, sub * 128:(sub + 1) * 128], rhs=xn[:, k, c0:c0 + n], start=(k == 0), stop=(k == 7)),
                                reads=[("wB", su_slot), ("xn", k, bi)], writes=[("pu", evi)])
                        S.add("act", lambda e, pg=pg, sgt=sgt, n=n: e.activation(out=sgt[:, 0:n], in_=pg[:, 0:n], func=AF.Silu),
                              reads=[("pg", evi)], writes=[("sg", evi)])
                        S.add("dve", lambda e, pu=pu, sgt=sgt, hc=hc, c0=c0, n=n: e.tensor_tensor(
                            out=h[:, hc, c0:c0 + n], in0=sgt[:, 0:n], in1=pu[:, 0:n], op=ALU.mult),
                            reads=[("sg", evi), ("pu", evi)], writes=[("h", hc, bi)])
            dctr = 0
            for bi in range(5):
                c0, n = BLOCKS[bi]
                for m in range(8):
                    pd = PS[4 + dctr % 2]
                    di = dctr % 2
                    dctr += 1
                    for kk in range(nh):
                        S.add("pe", lambda e, kk=kk, pd=pd, m=m, c0=c0, n=n, nh=nh: e.matmul(
                            pd[:, 0:n], lhsT=wA[:, kk, m * 128:(m + 1) * 128], rhs=h[:, kk, c0:c0 + n], start=(kk == 0), stop=(kk == nh - 1)),
                            reads=["wA", ("h", kk, bi)], writes=[("pd", di)])
                    S.add("dve", lambda e, pd=pd, m=m, c0=c0, n=n: e.scalar_tensor_tensor(
                        out=x[:, m, c0:c0 + n], in0=pd[:, 0:n], scalar=0.5, in1=x[:, m, c0:c0 + n], op0=ALU.mult, op1=ALU.add),
                        reads=[("pd", di), ("x", m, bi)], writes=[("x", m, bi)])

    def pool_phase():
        ar = Arena(un, UW)
        xn = ar.bf16(8, NTOK)
        mixed = ar.bf16(8, NTOK)
        sq = [ar.bf16(512), ar.bf16(512)]
        rstd = ar.f32(512)
        ext_p = [ar.f32(528) for _ in range(2)]
        ext_s = [ar.f32(16, 23) for _ in range(2)]
        tp = [ar.f32(528) for _ in range(2)]
        ts_ = [ar.f32(16, 23) for _ in range(2)]
        fix = ar.f32(16)
        goff = C_MN + 8
        S.add("pool", lambda e: e.dma_start(out=wA, in_=pool_w_in.rearrange("(k p) n -> p k n", p=128)), writes=["wA"], dsem="wA")
        wo = wBf[:, 0:8192].rearrange("p (k n) -> p k n", k=8)
        S.add("pool", lambda e: e.dma_start(out=wo, in_=pool_w_out.rearrange("(k p) n -> p k n", p=128)),
              writes=[("wB", s_) for s_ in range(4)], dsem=("wB", 0))
        S.add("pool", lambda e: e.dma_start(out=wgrp.rearrange("p (g kk) n -> p g kk n", g=4), in_=pool_w_group.rearrange("g (kk p) n -> p g kk n", p=128)),
              writes=["wgrp"], dsem="wgrp")
        for bi in range(5):
            c0, n = BLOCKS[bi]
            rmsnorm(sq, rstd, goff, bi, lambda k, c0=c0, n=n: xn[:, k, c0:c0 + n], lambda k, bi=bi: [("xn", k, bi)], PS[6])
        ev = 0
        ectr = 0
        for m in range(8):
            widx = m // 2
            w = POOL_WINDOWS[widx]
            prev = None
            for bi in range(5):
                c0, n = BLOCKS[bi]
                pp = PS[ev % 2]
                pi = ev % 2
                ev += 1
                for k in range(8):
                    S.add("pe", lambda e, k=k, pp=pp, m=m, c0=c0, n=n: e.matmul(
                        pp[:, 0:n], lhsT=wA[:, k, m * 128:(m + 1) * 128], rhs=xn[:, k, c0:c0 + n], start=(k == 0), stop=(k == 7)),
                        reads=["wA", ("xn", k, bi)], writes=[("pg", pi)])
                es = ectr % 2
                ectr += 1
                if bi < 4:
                    ep = ext_p[es]
                    ekey = ("extp", es)
                    if bi == 0:
                        S.add("pool", lambda e, ep=ep: e.memset(ep[:, 0:15], 0.0), writes=[ekey])
                    else:
                        S.add("pool", lambda e, ep=ep, prev=prev: e.tensor_copy(out=ep[:, 0:15], in_=prev[:, 512:527]), reads=[("extp", 1 - es)], writes=[ekey])
                    S.add("act", lambda e, pp=pp, ep=ep: e.activation(out=ep[:, 15:527], in_=pp[:, 0:512], func=AF.Copy),
                          reads=[("pg", pi)], writes=[ekey])
                    prev = ep
                    if bi == 3:
                        S.add("sp", lambda e, m=m, ep=ep: e.dma_start(out=pool_p_out[m * 128:(m + 1) * 128, :], in_=ep[:, 512:527]),
                              reads=[ekey], dsem="pool_p_out")
                    src = ep
                    skey = ekey
                    hi = 527
                    sl = lambda a, lo_, hi_: a[:, lo_:hi_]
                    tmps = tp
                    tkey = "tp"
                else:
                    esm = ext_s[es]
                    ekey = ("exts", es)
                    for bq in range(2):
                        S.add("sp", lambda e, m=m, esm=esm, bq=bq: e.dma_start(out=esm[:, bq * 8:(bq + 1) * 8, 0:15], in_=poolT[m * 128:(m + 1) * 128, bq * 8:(bq + 1) * 8, :]),
                              writes=[ekey], dsem=("exts", es))
                    S.add("act", lambda e, pp=pp, esm=esm: e.activation(out=esm[:, :, 15:23], in_=pp[:, 0:128].rearrange("p (b t) -> p b t", b=16), func=AF.Copy),
                          reads=[("pg", pi)], writes=[ekey])
                    for bq in range(2):
                        S.add("sp", lambda e, m=m, esm=esm, bq=bq: e.dma_start(out=pool_s_out[m * 128:(m + 1) * 128, bq * 8:(bq + 1) * 8, :], in_=esm[:, bq * 8:(bq + 1) * 8, 8:23]),
                              reads=[ekey], dsem="pool_s_out")
                    src = esm
                    skey = ekey
                    hi = 23
                    sl = lambda a, lo_, hi_: a[:, :, lo_:hi_]
                    tmps = ts_
                    tkey = "ts"
                base = src
                lo = 0
                step = 1
                lvl = 0
                while step < w:
                    dst = tmps[lvl % 2]
                    nlo = lo + step
                    S.add("pool", lambda e, dst=dst, src=src, nlo=nlo, step=step, hi=hi, sl=sl: e.tensor_tensor(
                        out=sl(dst, nlo, hi), in0=sl(src, nlo, hi), in1=sl(src, nlo - step, hi - step), op=ALU.add),
                        reads=[skey], writes=[(tkey, lvl % 2)])
                    src = dst
                    skey = (tkey, lvl % 2)
                    lo = nlo
                    step *= 2
                    lvl += 1
                if bi < 4:
                    S.add("dve", lambda e, src=src, base=base, m=m, w=w, c0=c0: e.scalar_tensor_tensor(
                        out=mixed[:, m, c0:c0 + 512], in0=src[:, 15:527], scalar=1.0 / w, in1=base[:, 15:527], op0=ALU.mult, op1=ALU.subtract),
                        reads=[skey, ekey], writes=[("mixed", m, bi)])
                    if bi == 0:
                        S.add("dve", lambda e, src=src, widx=widx: e.tensor_tensor(
                            out=fix, in0=src[:, 15:31], in1=consts[:, K_INVC + widx * 16:K_INVC + widx * 16 + 16], op=ALU.mult),
                            reads=[skey, "consts"], writes=["fix"])
                        S.add("dve", lambda e, base=base, m=m: e.tensor_tensor(out=mixed[:, m, 0:16], in0=fix, in1=base[:, 15:31], op=ALU.subtract),
                              reads=["fix", ekey], writes=[("mixed", m, 0)])
                else:
                    S.add("dve", lambda e, src=src, base=base, m=m, w=w: e.scalar_tensor_tensor(
                        out=mixed[:, m, 2048:2176].rearrange("p (b t) -> p b t", b=16), in0=src[:, :, 15:23], scalar=1.0 / w, in1=base[:, :, 15:23],
                        op0=ALU.mult, op1=ALU.subtract),
                        reads=[skey, ekey], writes=[("mixed", m, 4)])
        wg4 = wgrp.rearrange("p (g kk) n -> p g kk n", g=4)
        mg = xn
        ev = 0
        for m in range(8):
            g, mm = m // 2, m % 2
            for bi in range(5):
                c0, n = BLOCKS[bi]
                pp = PS[ev % 2]
                pi = ev % 2
                ev += 1
                for kk in range(2):
                    S.add("pe", lambda e, kk=kk, pp=pp, g=g, mm=mm, c0=c0, n=n: e.matmul(
                        pp[:, 0:n], lhsT=wg4[:, g, kk, mm * 128:(mm + 1) * 128], rhs=mixed[:, 2 * g + kk, c0:c0 + n], start=(kk == 0), stop=(kk == 1)),
                        reads=["wgrp", ("mixed", 2 * g + kk, bi)], writes=[("pg", pi)])
                S.add("act", lambda e, pp=pp, m=m, c0=c0, n=n: e.activation(out=mg[:, m, c0:c0 + n], in_=pp[:, 0:n], func=AF.Identity, scale=col(C_PS, m)),
                      reads=[("pg", pi), "cols"] + [("xn", k, bi) for k in range(8)], writes=[("mg", m, bi), ("xn", m, bi)])
        ev = 0
        for bi in range(5):
            c0, n = BLOCKS[bi]
            for m in range(8):
                pd = PS[4 + ev % 2]
                di = ev % 2
                ev += 1
                for k in range(8):
                    S.add("pe", lambda e, k=k, pd=pd, m=m, c0=c0, n=n: e.matmul(
                        pd[:, 0:n], lhsT=wo[:, k, m * 128:(m + 1) * 128], rhs=mg[:, k, c0:c0 + n], start=(k == 0), stop=(k == 7)),
                        reads=[("wB", 0), ("mg", k, bi)], writes=[("pd", di)])
                S.add("dve", lambda e, pd=pd, m=m, c0=c0, n=n: e.tensor_tensor(out=x[:, m, c0:c0 + n], in0=x[:, m, c0:c0 + n], in1=pd[:, 0:n], op=ALU.add),
                      reads=[("pd", di), ("x", m, bi)], writes=[("x", m, bi)])

    def ssd_phase():
        SBLK = [(i * 256, 256, i // 2) for i in range(8)] + [(2048, 128, 4)]
        SM = PS[2]

        def pk(i):
            return ("ps", i)

        def common(ar):
            d = {}
            _sq = ar.bf16(256)
            d["sq"] = [_sq, _sq]
            d["rstd"] = ar.f32(256)
            for nm in ("dt", "a", "acum", "dend", "cd", "f2"):
                d[nm] = ar.f32(2, 32)
            d["dtx"] = d["dt"]
            d["seg"] = [ar.f32(4, 128), ar.f32(4, 128)]
            d["E"] = [ar.f32(4, 128), ar.f32(4, 128)]
            d["M"] = [ar.bf16(4, 128), ar.bf16(4, 128)]
            d["Cp"] = [ar.bf16(4, 128), ar.bf16(4, 128)]
            d["xdt"] = [ar.bf16(256), ar.bf16(256)]
            d["xdtd"] = [ar.bf16(256), ar.bf16(256)]
            d["y1"] = [ar.f32(2, 128), ar.f32(2, 128)]
            d["y2"] = [ar.f32(2, 128), ar.f32(2, 128)]
            d["ysq"] = [ar.bf16(2, 128), ar.bf16(2, 128)]
            d["rg"] = [ar.f32(128), ar.f32(128)]
            d["sttmp"] = [ar.f32(256), ar.f32(256)]
            return d

        def blockbufs(ar, n):
            nt = n // 128
            d = {}
            d["xnb"] = ar.bf16(8, n)
            d["sz"] = [ar.bf16(2, n) for _ in range(8)]
            d["xsT"] = [ar.bf16(2, n) for _ in range(8)]
            d["BT"] = [ar.bf16(n) for _ in range(8)]
            d["CT"] = [ar.bf16(n) for _ in range(8)]
            d["tok"] = [ar.bf16(nt, 384) for _ in range(8)]
            d["raw"] = [ar.f32(n + 8), ar.f32(n + 8)] if n == 256 else [ar.f32(16, 11), ar.f32(16, 11)]
            d["acc"] = [ar.f32(n), ar.f32(n)]
            d["yn"] = ar.bf16(16, n)
            return d

        arp = Arena(un, UW)
        cm = common(arp)
        bp = blockbufs(arp, 256)
        carry = arp.f32(32, 3)
        hstate = arp.f32(2048)
        hbf = arp.bf16(2048)

        ars = Arena(un, UW)
        common(ars)
        bs = blockbufs(ars, 128)
        convh = ars.f32(32, 48)
        cd_all = ars.f32(16, 32)
        h0g = [ars.f32(8, 256), ars.f32(8, 256)]
        h0bf = [ars.bf16(8, 256), ars.bf16(8, 256)]
        Bm = ars.bf16(16, 128)

        S.add("pool", lambda e: e.dma_start(out=wdt, in_=ssd_w_in[:, 6144:6176].rearrange("(k p) n -> p k n", p=128)), writes=["wdt"], dsem="wdt")
        S.add("act", lambda e: e.activation(out=a_row, in_=rows[:, 32:64], func=AF.Exp), reads=["rows"], writes=["a_row"])
        S.add("dve", lambda e: e.tensor_scalar_mul(out=a_row, in0=a_row, scalar1=-1.0), reads=["a_row"], writes=["a_row"])

        win_slots = [wBf[:, s_ * 6144:(s_ + 1) * 6144].rearrange("p (k n) -> p k n", k=8) for s_ in range(2)]
        win_slots.append(wA.rearrange("p k n -> p (k n)")[:, 0:6144].rearrange("p (k n) -> p k n", k=8))
        WIN_SLOT_OF_G = [0, 1, 2, 0, 1, 2, 0, 1]
        wo_slots = [wA[:, s_ * 4:(s_ + 1) * 4, :] for s_ in range(2)]
        gctr = [0]
        woctr = [0]
        rawctr = [0]
        tctr = [0]
        cctr = [0]
        sctr = [0]
        hctr = [0]

        win_loaded = {}

        def get_win(sbi_, g):
            if sbi_ > 8:
                return None
            if (sbi_, g) not in win_loaded:
                win_loaded[(sbi_, g)] = load_win(g)
            return win_loaded[(sbi_, g)]

        wo_loaded = {}

        def get_wo(sbi_, q):
            if sbi_ > 8:
                return None
            if (sbi_, q) not in wo_loaded:
                ws_i = woctr[0] % 2
                woctr[0] += 1
                wq = wo_slots[ws_i]
                S.add("pool", lambda e, q=q, wq=wq: e.dma_start(out=wq, in_=ssd_w_out[q * 512:(q + 1) * 512, :].rearrange("(k p) n -> p k n", p=128)),
                      writes=[("wo", ws_i), ("win", 2)], dsem=("wo", ws_i))
                wo_loaded[(sbi_, q)] = ws_i
            return wo_loaded[(sbi_, q)]

        def load_win(g):
            s_ = WIN_SLOT_OF_G[g]
            ws = win_slots[s_]
            wkeys = [("win", s_)] + ([("wo", 0), ("wo", 1)] if s_ == 2 else [])
            S.add("pool", lambda e, ws=ws: e.dma_start(out=ws, in_=ssd_w_in[:, g * 768:(g + 1) * 768].rearrange("(k p) n -> p k n", p=128)),
                  writes=wkeys, dsem=("win", s_))
            return s_

        def ssd_block(sbi):
            c0, n, kb = SBLK[sbi]
            nt = n // 128
            samp = (sbi == 8)
            first_blk = (sbi == 0)
            bb = bs if samp else bp
            xnb = bb["xnb"]
            yn = bb["yn"]
            UM = consts[:, K_US:K_US + 128] if samp else consts[:, K_UP:K_UP + 128]
            NEG = consts[:, K_NEGS:K_NEGS + 128] if samp else consts[:, K_NEGP:K_NEGP + 128]
            LAST = consts[:, K_BD:K_BD + 128] if samp else consts[:, K_ONES:K_ONES + 128]
            dtx, dt, a_, acum, dend, cd, f2 = (cm[k_] for k_ in ("dtx", "dt", "a", "acum", "dend", "cd", "f2"))
            rmsnorm(cm["sq"], cm["rstd"], C_MN, kb, lambda k: xnb[:, k, 0:n], lambda k: [("xnb", k)], SM, ps_keys=(pk(2),), span=(c0, n))
            for t in range(nt):
                for k in range(8):
                    S.add("pe", lambda e, t=t, k=k: e.matmul(SM[:, 128 + t * 32:128 + (t + 1) * 32], lhsT=xnb[:, k, t * 128:(t + 1) * 128], rhs=wdt[:, k, :],
                                                             start=(k == 0), stop=(k == 7)),
                          reads=[("xnb", k), "wdt"], writes=[pk(2)])
            smdt = SM[:, 128:128 + nt * 32].rearrange("p (t h) -> p t h", t=nt)
            S.add("dve", lambda e: e.tensor_tensor(out=dtx[:, 0:nt, :], in0=smdt, in1=bcast_mid(rows[:, 0:32], nt), op=ALU.add),
                  reads=[pk(2), "rows"], writes=["dt"])
            S.add("act", lambda e: e.activation(out=dtx[:, 0:nt, :], in_=dtx[:, 0:nt, :], func=AF.Exp), reads=["dt"], writes=["dt"])
            S.add("act", lambda e: e.activation(out=dt[:, 0:nt, :], in_=dtx[:, 0:nt, :], func=AF.Ln, bias=1.0), reads=["dt"], writes=["dt"])
            S.add("dve", lambda e: e.tensor_tensor(out=a_[:, 0:nt, :], in0=dt[:, 0:nt, :], in1=bcast_mid(a_row, nt), op=ALU.mult),
                  reads=["dt", "a_row"], writes=["a"])
            for t in range(nt):
                S.add("pe", lambda e, t=t: e.matmul(SM[:, 256:288], lhsT=UM, rhs=a_[:, t, :], start=True, stop=True),
                      reads=["a", "consts"], writes=[pk(2)])
                S.add("act", lambda e, t=t: e.activation(out=acum[:, t, :], in_=SM[:, 256:288], func=AF.Copy), reads=[pk(2)], writes=["acum"])
                S.add("pe", lambda e, t=t: e.matmul(SM[:, 288:320], lhsT=LAST, rhs=a_[:, t, :], start=True, stop=True),
                      reads=["a", "consts"], writes=[pk(2)])
                S.add("dve", lambda e, t=t: e.tensor_tensor(out=dend[:, t, :], in0=SM[:, 288:320], in1=acum[:, t, :], op=ALU.subtract),
                      reads=[pk(2), "acum"], writes=["dend"])
                S.add("act", lambda e, t=t: e.activation(out=cd[:, t, :], in_=SM[:, 288:320], func=AF.Exp), reads=[pk(2)], writes=["cd"])
            S.add("act", lambda e: e.activation(out=dend[:, 0:nt, :], in_=dend[:, 0:nt, :], func=AF.Exp), reads=["dend"], writes=["dend"])
            S.add("dve", lambda e: e.tensor_tensor(out=f2[:, 0:nt, :], in0=dt[:, 0:nt, :], in1=dend[:, 0:nt, :], op=ALU.mult),
                  reads=["dt", "dend"], writes=["f2"])
            if samp:
                for b in range(16):
                    S.add("pe", lambda e, b=b: e.matmul(SM[:, 0:512][:, b * 32:(b + 1) * 32], lhsT=bcast_col(consts[:, K_SEQM + b:K_SEQM + b + 1], 128), rhs=a_[:, 0, :],
                                                        start=True, stop=True),
                          reads=["a", "consts", "acum", "dend", "cd"], writes=[pk(2)])
                S.add("act", lambda e: e.activation(out=cd_all, in_=SM.rearrange("p (b h) -> p b h", b=16), func=AF.Exp), reads=[pk(2)], writes=["cd_all"])
                S.add("sp", lambda e: e.dma_start(out=convh.rearrange("p j c -> p (j c)"), in_=convT), writes=["convh"], dsem="convh")

            def inproj(g, ws_i):
                ws = win_slots[ws_i]
                sz = bb["sz"][g]; xsT = bb["xsT"][g]; BT = bb["BT"][g]; CT = bb["CT"][g]; tok = bb["tok"][g]
                pend = []
                for cc in range(6):
                    pai = cctr[0] % 2
                    cctr[0] += 1
                    pa = PS[pai]
                    for k in range(8):
                        S.add("pe", lambda e, k=k, pa=pa, cc=cc: e.matmul(pa[:, 0:n], lhsT=ws[:, k, cc * 128:(cc + 1) * 128], rhs=xnb[:, k, 0:n],
                                                                         start=(k == 0), stop=(k == 7)),
                              reads=[("win", ws_i), ("xnb", k)], writes=[pk(pai)])
                    if cc < 2:
                        S.add("act", lambda e, pa=pa, cc=cc: e.activation(out=sz[:, cc, 0:n], in_=pa[:, 0:n], func=AF.Silu),
                              reads=[pk(pai)], writes=[("sz", g)])
                        continue
                    j = (2 * g + cc - 2) if cc < 4 else ((16 + g) if cc == 4 else (24 + g))
                    ri = rawctr[0] % 2
                    rawctr[0] += 1
                    raw = bb["raw"][ri]
                    acc = bb["acc"][ri]
                    if samp:
                        rdat = raw[:, :, 3:11]
                        pav = pa[:, 0:128].rearrange("p (b t) -> p b t", b=16)
                        accv = acc[:, 0:128].rearrange("p (b t) -> p b t", b=16)
                        taps = [raw[:, :, kk:kk + 8] for kk in range(3)]
                        S.add("dve", lambda e, raw=raw, j=j: e.tensor_copy(out=raw[:, :, 0:3], in_=convh[:, j, :].rearrange("p (b k) -> p b k", b=16)),
                              reads=["convh"], writes=[("rawh", ri)])
                    else:
                        rdat = raw[:, 3:3 + n]
                        pav = pa[:, 0:n]
                        accv = acc[:, 0:n]
                        taps = [raw[:, kk:kk + n] for kk in range(3)]
                        if first_blk:
                            S.add("dve", lambda e, raw=raw: e.memset(raw[:, 0:3], 0.0), writes=[("rawh", ri)])
                        else:
                            S.add("dve", lambda e, raw=raw, j=j: e.tensor_copy(out=raw[:, 0:3], in_=carry[:, j, :]), reads=[("carry", j)], writes=[("rawh", ri)])
                    S.add("act", lambda e, rdat=rdat, pav=pav: e.activation(out=rdat, in_=pav, func=AF.Copy), reads=[pk(pai)], writes=[("raw", ri)])
                    S.add("act", lambda e, accv=accv, pav=pav, j=j: e.activation(out=accv, in_=pav, func=AF.Identity, scale=col(C_CW, 3 * 32 + j), bias=col(C_CB, j)),
                          reads=[pk(pai), "cols"], writes=[("acc", ri)])
                    while pend:
                        pend.pop(0)()
                    if samp:
                        S.add("dve", lambda e, raw=raw, j=j: e.tensor_copy(out=convh[:, j, :].rearrange("p (b k) -> p b k", b=16), in_=raw[:, :, 8:11]),
                              reads=[("raw", ri)], writes=["convh"])
                    else:
                        S.add("dve", lambda e, raw=raw, j=j: e.tensor_copy(out=carry[:, j, :], in_=raw[:, n:n + 3]), reads=[("raw", ri)], writes=[("carry", j)])
                    for kk in range(3):
                        S.add("dve", lambda e, accv=accv, tp_=taps[kk], kk=kk, j=j: e.scalar_tensor_tensor(
                            out=accv, in0=tp_, scalar=col(C_CW, kk * 32 + j), in1=accv, op0=ALU.mult, op1=ALU.add),
                            reads=[("raw", ri), ("rawh", ri), ("acc", ri), "cols"], writes=[("acc", ri)])
                    if cc < 4:
                        dst = xsT[:, cc - 2, 0:n]; dk = ("xsT", g)
                    elif cc == 4:
                        dst = BT[:, 0:n]; dk = ("BT", g)
                    else:
                        dst = CT[:, 0:n]; dk = ("CT", g)
                    pend.append(lambda dst=dst, acc=acc, ri=ri, dk=dk: S.add(
                        "act", lambda e: e.activation(out=dst, in_=acc[:, 0:n], func=AF.Silu), reads=[("acc", ri)], writes=[dk]))
                while pend:
                    pend.pop(0)()
            def inproj_tr(g):
                xsT = bb["xsT"][g]; BT = bb["BT"][g]; tok = bb["tok"][g]
                for t in range(nt):
                    th = tctr[0] % 2
                    tctr[0] += 1
                    tb_ = th * 512
                    for jj in range(2):
                        S.add("pe", lambda e, jj=jj, t=t, tb_=tb_: e.transpose(PT[:, tb_ + jj * 128:tb_ + (jj + 1) * 128], xsT[:, jj, t * 128:(t + 1) * 128], ident_bf),
                              reads=[("xsT", g), "ident"], writes=["pt"])
                    S.add("pe", lambda e, t=t, tb_=tb_: e.transpose(PT[:, tb_ + 256:tb_ + 384], BT[:, t * 128:(t + 1) * 128], ident_bf),
                          reads=[("BT", g), "ident"], writes=["pt"])
                    S.add("dve", lambda e, t=t, tb_=tb_: e.tensor_copy(out=tok[:, t, :], in_=PT[:, tb_:tb_ + 384]), reads=["pt"], writes=[("tok", g)])

            def _unpack(c):
                return (c[k_] for k_ in ("t", "g", "sz", "xsT", "BT", "CT", "tok", "si", "iAB", "iCB", "iYB", "iST", "AB", "CBG", "YB", "ST",
                                         "seg", "E", "M", "Cp", "xdt", "xdtd", "y1", "y2", "ysq", "rg", "sttmp", "first"))

            def scan_prep(t, g):
                sz = bb["sz"][g]; xsT = bb["xsT"][g]; BT = bb["BT"][g]; CT = bb["CT"][g]; tok = bb["tok"][g]
                si = sctr[0] % 2
                sctr[0] += 1
                iAB, iCB, iYB, iST = 0 + si, 2 + si, 4 + si, 6
                AB, CBG, YB, ST = PS[iAB], PS[iCB], PS[iYB], PS[iST]
                seg = cm["seg"][si]; E = cm["E"][si]; M = cm["M"][si]; Cp = cm["Cp"][si]
                xdt = cm["xdt"][si]; xdtd = cm["xdtd"][si]; y1 = cm["y1"][si]; y2 = cm["y2"][si]
                ysq = cm["ysq"][si]; rg = cm["rg"][si]; sttmp = cm["sttmp"][si]
                first = (first_blk and t == 0)
                S.add("pe", lambda e: e.matmul(CBG[:, 0:128], lhsT=BT[:, t * 128:(t + 1) * 128], rhs=CT[:, t * 128:(t + 1) * 128], start=True, stop=True),
                      reads=[("BT", g), ("CT", g)], writes=[pk(iCB)])
                for hh in range(4):
                    hd = 4 * g + hh
                    S.add("pe", lambda e, hh=hh, hd=hd: e.matmul(AB[:, hh * 128:(hh + 1) * 128], lhsT=bcast_col(a_[:, t, hd:hd + 1], 128), rhs=UM, start=True, stop=True),
                          reads=["a", "consts"], writes=[pk(iAB)])
                xs4 = tok[:, t, 0:256].rearrange("p (h q) -> p h q", h=4)
                S.add("pool", lambda e: e.tensor_tensor(out=xdt.rearrange("p (h q) -> p h q", h=4), in0=xs4, in1=bcast_last(dt[:, t, 4 * g:4 * g + 4], 64), op=ALU.mult),
                      reads=[("tok", g), "dt"], writes=[("xdt", si)])
                S.add("pool", lambda e: e.tensor_tensor(out=xdtd.rearrange("p (h q) -> p h q", h=4), in0=xs4, in1=bcast_last(f2[:, t, 4 * g:4 * g + 4], 64), op=ALU.mult),
                      reads=[("tok", g), "f2"], writes=[("xdtd", si)])
                if not samp and not first:
                    hsl0 = hstate[:, g * 256:(g + 1) * 256]
                    S.add("pool", lambda e: e.tensor_tensor(out=sttmp.rearrange("p (h q) -> p h q", h=4), in0=hsl0.rearrange("p (h q) -> p h q", h=4),
                                                           in1=bcast_last(cd[:, t, 4 * g:4 * g + 4], 64), op=ALU.mult),
                          reads=[("hst", g), "cd"], writes=[("sttmp", si)])
                for hh in range(4):
                    hd = 4 * g + hh
                    S.add("dve", lambda e, hh=hh, hd=hd: e.scalar_tensor_tensor(
                        out=seg[:, hh, :], in0=AB[:, hh * 128:(hh + 1) * 128], scalar=acum[:, t, hd:hd + 1], in1=NEG, op0=ALU.subtract, op1=ALU.add),
                        reads=[pk(iAB), "acum", "consts"], writes=[("seg", si)])
                S.add("act", lambda e: e.activation(out=seg, in_=seg, func=AF.Exp), reads=[("seg", si)], writes=[("seg", si)])
                S.add("act", lambda e: e.activation(out=E, in_=AB.rearrange("p (h l) -> p h l", h=4), func=AF.Exp), reads=[pk(iAB)], writes=[("E", si)])
                S.add("dve", lambda e: e.tensor_tensor(out=M, in0=seg, in1=bcast_mid(CBG[:, 0:128], 4), op=ALU.mult),
                      reads=[("seg", si), pk(iCB)], writes=[("M", si)])
                S.add("pool", lambda e: e.tensor_tensor(out=Cp, in0=E, in1=bcast_mid(CT[:, t * 128:(t + 1) * 128], 4), op=ALU.mult),
                      reads=[("E", si), ("CT", g)], writes=[("Cp", si)])
                return dict(locals())

            def scan_state(c):
                (t, g, sz, xsT, BT, CT, tok, si, iAB, iCB, iYB, iST, AB, CBG, YB, ST,
                 seg, E, M, Cp, xdt, xdtd, y1, y2, ysq, rg, sttmp, first) = _unpack(c)
                if samp:
                    for hh in range(4):
                        jj, half = hh // 2, hh % 2
                        yo = YB[half * 64:(half + 1) * 64, jj * 128:(jj + 1) * 128]
                        if hh < 2:
                            S.add("pe", lambda e, yo=yo, hh=hh: e.matmul(yo, lhsT=xdt[:, hh * 64:(hh + 1) * 64], rhs=M[:, hh, :], start=True, stop=True),
                                  reads=[("xdt", si), ("M", si)], writes=[pk(iYB)])
                        else:
                            S.add("pe", lambda e, yo=yo, hh=hh: e.matmul(yo, lhsT=xdt[:, hh * 64:(hh + 1) * 64], rhs=M[:, hh, :], start=False, stop=True, skip_group_check=True),
                                  reads=[("xdt", si), ("M", si)], writes=[pk(iYB)])
                    S.add("pool", lambda e: e.tensor_tensor(out=Bm, in0=bcast_mid(tok[:, 0, 256:384], 16), in1=bcast_last(consts[:, K_SEQM:K_SEQM + 16], 128), op=ALU.mult),
                          reads=[("tok", g), "consts"], writes=["Bm"])
                    for hf in range(2):
                        hs_i = hctr[0] % 2
                        hctr[0] += 1
                        hg = h0g[hs_i]
                        hb_ = h0bf[hs_i]
                        S.add("sp", lambda e, hg=hg, hf=hf: e.dma_start(out=hg.rearrange("p b c -> p (b c)"), in_=ssmT[g][:, hf * 2048:(hf + 1) * 2048]),
                              writes=[("h0g", hs_i)], dsem=("h0g", hs_i))
                        S.add("act", lambda e, hg=hg, hb_=hb_: e.activation(out=hb_, in_=hg, func=AF.Copy), reads=[("h0g", hs_i)], writes=[("h0bf", hs_i)])
                        for hh in range(4):
                            jj, half = hh // 2, hh % 2
                            for bl in range(8):
                                b = hf * 8 + bl
                                S.add("pe", lambda e, b=b, bl=bl, hh=hh, half=half, jj=jj, hb_=hb_: e.matmul(
                                    YB[half * 64:(half + 1) * 64, jj * 128 + b * 8:jj * 128 + (b + 1) * 8], lhsT=hb_[:, bl, hh * 64:(hh + 1) * 64],
                                    rhs=Cp[:, hh, b * 8:(b + 1) * 8], start=False, stop=True, skip_group_check=True),
                                    reads=[("h0bf", hs_i), ("Cp", si)], writes=[pk(iYB)])
                        for bl in range(8):
                            b = hf * 8 + bl
                            pq = PS[6]
                            S.add("pe", lambda e, b=b, pq=pq: e.matmul(pq[:, 0:256], lhsT=Bm[:, b, :], rhs=xdtd, start=True, stop=True),
                                  reads=["Bm", ("xdtd", si)], writes=[pk(6)])
                            S.add("pool", lambda e, b=b, bl=bl, hg=hg: e.tensor_tensor(out=hg[:, bl, :].rearrange("p (h q) -> p h q", h=4), in0=hg[:, bl, :].rearrange("p (h q) -> p h q", h=4),
                                                                                   in1=bcast_last(cd_all[:, b, 4 * g:4 * g + 4], 64), op=ALU.mult),
                                  reads=[("h0g", hs_i), ("h0bf", hs_i), "cd_all"], writes=[("h0g", hs_i)])
                            S.add("dve", lambda e, bl=bl, pq=pq, hg=hg: e.tensor_tensor(out=hg[:, bl, :], in0=hg[:, bl, :], in1=pq[:, 0:256], op=ALU.add),
                                  reads=[("h0g", hs_i), pk(6)], writes=[("h0g", hs_i)])
                        S.add("sp", lambda e, hg=hg, hf=hf: e.dma_start(out=ssm_s_out[g][:, hf * 2048:(hf + 1) * 2048], in_=hg.rearrange("p b c -> p (b c)")),
                              reads=[("h0g", hs_i)], dsem="ssm_s_out")
                else:
                    for hh in range(4):
                        hd = 4 * g + hh
                        jj, half = hh // 2, hh % 2
                        yo = YB[half * 64:(half + 1) * 64, jj * 128:(jj + 1) * 128]
                        S.add("pe", lambda e, yo=yo, hh=hh: e.matmul(yo, lhsT=xdt[:, hh * 64:(hh + 1) * 64], rhs=M[:, hh, :], start=True, stop=first),
                              reads=[("xdt", si), ("M", si)], writes=[pk(iYB)])
                        if not first:
                            S.add("pe", lambda e, yo=yo, hd=hd, hh=hh: e.matmul(yo, lhsT=hbf[:, hd * 64:(hd + 1) * 64], rhs=Cp[:, hh, :], start=False, stop=True),
                                  reads=[("hbf", g), ("Cp", si)], writes=[pk(iYB)])
                    S.add("pe", lambda e: e.matmul(ST[:, 0:256], lhsT=tok[:, t, 256:384], rhs=xdtd, start=True, stop=True),
                          reads=[("tok", g), ("xdtd", si)], writes=[pk(iST)])
                    hsl = hstate[:, g * 256:(g + 1) * 256]
                    hbl = hbf[:, g * 256:(g + 1) * 256]
                    if first:
                        S.add("dve", lambda e: e.tensor_copy(out=hbl, in_=ST[:, 0:256]), reads=[pk(iST)], writes=[("hbf", g)])
                        S.add("act", lambda e: e.activation(out=hsl, in_=ST[:, 0:256], func=AF.Copy), reads=[pk(iST)], writes=[("hst", g)])
                    else:
                        S.add("dve", lambda e: e.tensor_tensor(out=hbl, in0=sttmp, in1=ST[:, 0:256], op=ALU.add),
                              reads=[("sttmp", si), pk(iST)], writes=[("hbf", g)])
                        S.add("dve", lambda e: e.tensor_tensor(out=hsl, in0=sttmp, in1=ST[:, 0:256], op=ALU.add),
                              reads=[("sttmp", si), pk(iST)], writes=[("hst", g)])

            def scan_post(c):
                (t, g, sz, xsT, BT, CT, tok, si, iAB, iCB, iYB, iST, AB, CBG, YB, ST,
                 seg, E, M, Cp, xdt, xdtd, y1, y2, ysq, rg, sttmp, first) = _unpack(c)
                for jj in range(2):
                    j = 2 * g + jj
                    S.add("dve", lambda e, jj=jj, j=j: e.scalar_tensor_tensor(
                        out=y1[:, jj, :], in0=xsT[:, jj, t * 128:(t + 1) * 128], scalar=col(C_DD, j), in1=YB[:, jj * 128:(jj + 1) * 128], op0=ALU.mult, op1=ALU.add),
                        reads=[("xsT", g), pk(iYB), "cols"], writes=[("y1", si)])
                S.add("pool", lambda e: e.tensor_tensor(out=y2, in0=y1, in1=sz[:, :, t * 128:(t + 1) * 128], op=ALU.mult),
                      reads=[("y1", si), ("sz", g)], writes=[("y2", si)])
                S.add("act", lambda e: e.activation(out=ysq, in_=y2, func=AF.Square), reads=[("y2", si)], writes=[("ysq", si)])
                for jj in range(2):
                    S.add("pe", lambda e, jj=jj: e.matmul(YB[:, 256:384], lhsT=ones_bf, rhs=ysq[:, jj, :], start=(jj == 0), stop=(jj == 1)),
                          reads=[("ysq", si), "ones"], writes=[pk(iYB)])
                S.add("act", lambda e: e.activation(out=rg, in_=YB[:, 256:384], func=AF.Ln, bias=EPS, scale=1.0 / 256.0), reads=[pk(iYB)], writes=[("rg", si)])
                S.add("act", lambda e: e.activation(out=rg, in_=rg, func=AF.Exp, scale=-0.5), reads=[("rg", si)], writes=[("rg", si)])
                for jj in range(2):
                    j = 2 * g + jj
                    S.add("dve", lambda e, jj=jj, j=j: e.scalar_tensor_tensor(
                        out=yn[:, j, t * 128:(t + 1) * 128], in0=y2[:, jj, :], scalar=col(C_SN, j), in1=rg, op0=ALU.mult, op1=ALU.mult),
                        reads=[("y2", si), ("rg", si), "cols"], writes=[("yn", j)])

            PH = int(_os.environ.get("SSD_PH", "3"))
            for g in range(8):
                ws_cur = get_win(sbi, g)
                if g + 1 < 8:
                    get_win(sbi, g + 1)
                if g + 2 < 8:
                    get_win(sbi, g + 2)
                inproj(g, ws_cur)
                if g >= 1:
                    inproj_tr(g - 1)
            inproj_tr(7)
            get_win(sbi + 1, 0)
            get_win(sbi + 1, 1)
            get_wo(sbi, 0)
            get_wo(sbi, 1)
            if PH < 2:
                return
            its = [(t, g) for t in range(nt) for g in range(8)]
            ctxs = {}
            if samp:
                for (t_, g_) in its:
                    c_ = scan_prep(t_, g_)
                    scan_state(c_)
                    scan_post(c_)
                its = []
            for i in range((len(its) + 2) if its else 0):
                if i < len(its):
                    ctxs[i] = scan_prep(*its[i])
                if 0 <= i - 1 < len(its):
                    scan_state(ctxs[i - 1])
                if 0 <= i - 2 < len(its):
                    scan_post(ctxs.pop(i - 2))
            if PH < 3:
                return
            for q in range(4):
                ws_i = get_wo(sbi, q)
                wq = wo_slots[ws_i]
                for m in range(8):
                    po = PS[m % 2]
                    for kk in range(4):
                        S.add("pe", lambda e, kk=kk, m=m, po=po, wq=wq, q=q: e.matmul(po[:, 0:n], lhsT=wq[:, kk, m * 128:(m + 1) * 128], rhs=yn[:, q * 4 + kk, 0:n],
                                                                                    start=(kk == 0), stop=(kk == 3)),
                              reads=[("wo", ws_i), ("yn", q * 4 + kk)], writes=[pk(m % 2)])
                    S.add("dve", lambda e, m=m, po=po: e.tensor_tensor(out=x[:, m, c0:c0 + n], in0=x[:, m, c0:c0 + n], in1=po[:, 0:n], op=ALU.add),
                          reads=[pk(m % 2), ("x", m, kb)], writes=[("x", m, kb)])

        for sbi in range(int(_os.environ.get("SSD_NB", "8"))):
            ssd_block(sbi)
        S.add("sp", lambda e: e.dma_start(out=ssm_p_out, in_=hstate), reads=[("hst", g) for g in range(8)], dsem="ssm_p_out")
        S.add("sp", lambda e: e.dma_start(out=conv_p_out, in_=carry.rearrange("p j k -> p (j k)")), reads=[("carry", j) for j in range(32)], dsem="conv_p_out")
        S.barrier()
        if _os.environ.get("SSD_NOSAMP"):
            return
        ssd_block(8)
        S.add("sp", lambda e: e.dma_start(out=conv_s_out, in_=convh.rearrange("p j c -> p (j c)")), reads=["convh"], dsem="conv_s_out")

    if do_ffn:
        ffn_phase(0, 0)
    S.barrier()
    if do_ssd:
        ssd_phase()
        out_dsems += ["ssm_p_out", "conv_p_out", "ssm_s_out", "conv_s_out"]
    S.barrier()
    if do_ffn:
        ffn_phase(0, 1)
        ffn_phase(1, 0)
    S.barrier()
    if do_pool:
        pool_phase()
        out_dsems += ["pool_p_out", "pool_s_out"]
    S.barrier()
    if do_ffn:
        ffn_phase(1, 1)
    S.barrier()
    ar = Arena(un, UW)
    sq = [ar.bf16(512), ar.bf16(512)]
    rstd = ar.f32(512)
    yo = [ar.f32(8, 512), ar.f32(8, 512)]
    yTv = yT.rearrange("(k p) t -> p k t", p=128)
    for bi in range(5):
        c0, n = BLOCKS[bi]
        yb = yo[bi % 2]
        rmsnorm(sq, rstd, C_FIN, bi, lambda k, yb=yb, n=n: yb[:, k, 0:n], lambda k, bi=bi: [("yo", bi % 2)], PS[6])
        S.add("sp", lambda e, yb=yb, c0=c0, n=n: e.dma_start(out=yTv[:, :, c0:c0 + n], in_=yb[:, :, 0:n]), reads=[("yo", bi % 2)], dsem="yT")
    out_dsems.append("yT")
    S.emit(out_dsems=out_dsems)
    return nc


def _consts():
    c = np.zeros((128, NCONST), np.float32)
    s = np.arange(128)[:, None]
    l = np.arange(128)[None, :]
    same = (s // 8) == (l // 8)
    c[:, K_UP:K_UP + 128] = (s <= l)
    c[:, K_US:K_US + 128] = (s <= l) & same
    c[:, K_NEGP:K_NEGP + 128] = np.where(l >= s, 0.0, -30000.0)
    c[:, K_NEGS:K_NEGS + 128] = np.where((l >= s) & same, 0.0, -30000.0)
    c[:, K_ONES:K_ONES + 128] = 1.0
    c[:, K_BD:K_BD + 128] = same
    c[:, K_SEQM:K_SEQM + 16] = (np.arange(128)[:, None] // 8) == np.arange(16)[None, :]
    for wi, w in enumerate(POOL_WINDOWS):
        c[:, K_INVC + wi * 16:K_INVC + (wi + 1) * 16] = 1.0 / np.minimum(float(w), np.arange(16) + 1.0)[None, :]
    c[:, K_ID:K_ID + 128] = np.eye(128)
    return c


def _colmajor(v):
    v = np.asarray(v, np.float32)
    return np.ascontiguousarray(v.reshape(-1, 128).T)


_PROG = {}
_WIN_PERM = np.concatenate([np.r_[g * 256:(g + 1) * 256, 2048 + g * 256:2048 + (g + 1) * 256,
                                  4096 + g * 128:4096 + (g + 1) * 128, 5120 + g * 128:5120 + (g + 1) * 128] for g in range(8)]
                           + [np.arange(6144, 6176)])


def kernel(x_prompt, x_sample, state_ssm, state_conv, state_pool,
           ffn_norm, ffn_w_gate, ffn_w_up, ffn_w_down, mix_norm,
           ssd_w_in, ssd_conv_w, ssd_conv_b, ssd_dt_bias, ssd_a_log, ssd_d, ssd_norm, ssd_w_out,
           pool_w_in, pool_w_group, pool_scale, pool_w_out, final_norm, _flags=(True, True, True)):
    f = np.float32
    cols = np.zeros((128, NCOL), f)
    for i in range(2):
        for j in range(2):
            o = C_FN + (i * 2 + j) * 8
            cols[:, o:o + 8] = _colmajor(ffn_norm[i, j])
    for i in range(2):
        cols[:, C_MN + i * 8:C_MN + (i + 1) * 8] = _colmajor(mix_norm[i])
    cols[:, C_FIN:C_FIN + 8] = _colmajor(final_norm)
    for k in range(4):
        cols[:, C_CW + k * 32:C_CW + (k + 1) * 32] = _colmajor(ssd_conv_w[0, k])
    cols[:, C_CB:C_CB + 32] = _colmajor(ssd_conv_b[0])
    cols[:, C_DD:C_DD + 16] = _colmajor(np.repeat(np.asarray(ssd_d[0], f), 64))
    cols[:, C_SN:C_SN + 16] = _colmajor(ssd_norm[0])
    cols[:, C_PS:C_PS + 8] = _colmajor(pool_scale[0])
    rows = np.zeros((128, 64), f)
    rows[:, 0:32] = np.asarray(ssd_dt_bias[0], f)[None, :]
    rows[:, 32:64] = np.asarray(ssd_a_log[0], f)[None, :]
    consts = _consts()

    shared = {
        "cols": cols, "rows": rows, "consts": consts,
        "ffn_w_gate": np.ascontiguousarray(ffn_w_gate, f), "ffn_w_up": np.ascontiguousarray(ffn_w_up, f),
        "ffn_w_down": np.ascontiguousarray(ffn_w_down, f),
        "ssd_w_in": np.ascontiguousarray(np.asarray(ssd_w_in[0], f)[:, _WIN_PERM]), "ssd_w_out": np.ascontiguousarray(ssd_w_out[0], f),
        "pool_w_in": np.ascontiguousarray(pool_w_in[0], f), "pool_w_group": np.ascontiguousarray(pool_w_group[0], f),
        "pool_w_out": np.ascontiguousarray(pool_w_out[0], f),
    }
    in_maps = []
    for c in range(NCORES):
        sl = slice(c * 16, (c + 1) * 16)
        xs = np.asarray(x_sample[sl], f).reshape(128, 1024)
        xT = np.ascontiguousarray(np.concatenate([np.asarray(x_prompt[c], f), xs], axis=0).T)
        ssmT = np.ascontiguousarray(np.asarray(state_ssm[0, sl], f).reshape(16, 8, 256, 128).transpose(1, 3, 0, 2)).reshape(8, 128, 4096)
        convT = np.ascontiguousarray(np.asarray(state_conv[0, sl], f).reshape(16, 3, 32, 128).transpose(3, 2, 0, 1)).reshape(128, 32 * 48)
        poolT = np.ascontiguousarray(np.asarray(state_pool[0, sl], f).transpose(2, 0, 1))
        m = dict(shared)
        m.update({"xT": xT, "ssmT": ssmT, "convT": convT, "poolT": poolT})
        in_maps.append(m)

    key = tuple(_flags)
    if key not in _PROG:
        _PROG[key] = build_program(*_flags)
    nc = _PROG[key]
    res = run_bass_kernel_spmd(nc, in_maps, core_ids=list(range(NCORES)))
    R = res.results

    y_prompt = np.zeros((8, 2048, 1024), f)
    y_sample = np.zeros((128, 8, 1024), f)
    ssm_p = np.zeros((1, 8, 32, 64, 128), f)
    conv_p = np.zeros((1, 8, 3, 4096), f)
    pool_p = np.zeros((1, 8, 15, 1024), f)
    ssm_s = np.zeros((1, 128, 32, 64, 128), f)
    conv_s = np.zeros((1, 128, 3, 4096), f)
    pool_s = np.zeros((1, 128, 15, 1024), f)
    for c in range(NCORES):
        r = R[c]
        yT = np.asarray(r["yT"])
        y_prompt[c] = yT[:, :2048].T
        y_sample[c * 16:(c + 1) * 16] = yT[:, 2048:].T.reshape(16, 8, 1024)
        if "ssm_p_out" in r:
            ssm_p[0, c] = np.asarray(r["ssm_p_out"]).T.reshape(32, 64, 128)
            conv_p[0, c] = np.asarray(r["conv_p_out"]).reshape(128, 32, 3).transpose(2, 1, 0).reshape(3, 4096)
            ssm_s[0, c * 16:(c + 1) * 16] = np.asarray(r["ssm_s_out"]).reshape(8, 128, 16, 256).transpose(2, 0, 3, 1).reshape(16, 32, 64, 128)
            conv_s[0, c * 16:(c + 1) * 16] = np.asarray(r["conv_s_out"]).reshape(128, 32, 16, 3).transpose(2, 3, 1, 0).reshape(16, 3, 4096)
        if "pool_p_out" in r:
            pool_p[0, c] = np.asarray(r["pool_p_out"]).T
            pool_s[0, c * 16:(c + 1) * 16] = np.asarray(r["pool_s_out"]).transpose(1, 2, 0)
    return (y_prompt, y_sample, ssm_p, conv_p, pool_p, ssm_s, conv_s, pool_s)
```

```python
import numpy as np
import concourse.bass as bass
import concourse.mybir as mybir
from concourse.bass_utils import run_bass_kernel_spmd

F32 = mybir.dt.float32
BF16 = mybir.dt.bfloat16
AF = mybir.ActivationFunctionType
ALU = mybir.AluOpType

NCORES = 8
import os as _os
NTOK = 2176
BLOCKS = [(0, 512), (512, 512), (1024, 512), (1536, 512), (2048, 128)]
D_FF = 2816
EPS = 1e-6
POOL_WINDOWS = (2, 4, 8, 16)

C_FN, C_MN, C_FIN, C_CW, C_CB, C_DD, C_SN, C_PS, NCOL = 0, 32, 48, 56, 184, 216, 232, 248, 256
K_UP, K_US, K_NEGP, K_NEGS, K_ONES, K_BD, K_SEQM, K_INVC, K_ID, NCONST = 0, 128, 256, 384, 512, 640, 768, 784, 848, 976


class Op:
    __slots__ = ("eng", "fn", "waits", "signaled", "sigval", "pos", "clock", "is_dma", "dsem", "dval")


class Sched:
    ENGS = ("pe", "act", "dve", "pool", "sp")

    def __init__(self, nc):
        self.nc = nc
        self.ops = []
        self.streams = {e: [] for e in self.ENGS}
        self.clock = {e: {} for e in self.ENGS}
        self.dma_waited = {e: {} for e in self.ENGS}
        self.last_writer = {}
        self.readers = {}
        self.dsem_total = {}
        self.pending = {e: [] for e in self.ENGS}

    def add(self, eng, fn, reads=(), writes=(), dsem=None):
        op = Op()
        op.eng = eng; op.fn = fn; op.waits = []; op.signaled = False; op.sigval = None
        op.is_dma = dsem is not None; op.dsem = dsem; op.dval = None
        op.pos = len(self.streams[eng])
        deps = []
        for k in reads:
            lw = self.last_writer.get(k)
            if lw is not None:
                deps.append(lw)
            if k in ("pt", "psn") or (isinstance(k, tuple) and k[0] in ("ps", "pg", "pu", "pd")):
                deps.extend(r for r in self.readers.get(k, ()) if r.eng != eng)
        for k in writes:
            lw = self.last_writer.get(k)
            if lw is not None:
                deps.append(lw)
            deps.extend(self.readers.get(k, ()))
        forced = self.pending[eng]
        if forced:
            deps.extend(forced)
            self.pending[eng] = []
        clk = self.clock[eng]
        seen = set()
        deps.sort(key=lambda d_: -d_.pos)
        for d in deps:
            if id(d) in seen or d is op:
                continue
            seen.add(id(d))
            if d.is_dma:
                tot = self.dsem_total[d.dsem]
                if self.dma_waited[eng].get(d.dsem, 0) < tot:
                    op.waits.append(("dma", d.dsem, tot))
                    self.dma_waited[eng][d.dsem] = tot
            else:
                if d.eng == eng and (eng == "pe" or op.pos - d.pos > 6):
                    continue
                if clk.get(d.eng, -1) >= d.pos:
                    continue
                op.waits.append(("eng", d))
                d.signaled = True
                for e2, p2 in d.clock.items():
                    if clk.get(e2, -1) < p2:
                        clk[e2] = p2
                if clk.get(d.eng, -1) < d.pos:
                    clk[d.eng] = d.pos
        if op.is_dma:
            self.dsem_total[dsem] = self.dsem_total.get(dsem, 0) + 16
            op.dval = self.dsem_total[dsem]
            op.clock = None
        else:
            op.clock = dict(clk)
        for k in reads:
            self.readers.setdefault(k, []).append(op)
        for k in writes:
            self.last_writer[k] = op
            self.readers[k] = []
        self.ops.append(op)
        self.streams[eng].append(op)
        return op

    def barrier(self):
        lasts = []
        for e in self.ENGS:
            st = [o for o in self.streams[e][-1:] if not o.is_dma]
            for o in reversed(self.streams[e]):
                if not o.is_dma:
                    lasts.append(o)
                    break
        dmas = {}
        for o in self.ops:
            if o.is_dma:
                dmas[o.dsem] = o
        for e in self.ENGS:
            self.pending[e] = [o for o in lasts if o.eng != e] + list(dmas.values())

    def emit(self, out_dsems=()):
        nc = self.nc
        engobj = {"pe": nc.tensor, "act": nc.scalar, "dve": nc.vector, "pool": nc.gpsimd, "sp": nc.sync}
        esem = {e: nc.alloc_semaphore("es_" + e) for e in self.ENGS}
        dsems = {}
        for op in self.ops:
            if op.is_dma and op.dsem not in dsems:
                dsems[op.dsem] = nc.alloc_semaphore("ds_%d" % len(dsems))
        for e in self.ENGS:
            n = 0
            for op in self.streams[e]:
                if op.signaled:
                    n += 1
                    op.sigval = n
        for op in self.ops:
            eo = engobj[op.eng]
            for w in op.waits:
                if w[0] == "dma":
                    eo.wait_ge(dsems[w[1]], w[2])
                else:
                    eo.wait_ge(esem[w[1].eng], w[1].sigval)
            inst = op.fn(eo)
            if op.is_dma:
                inst.then_inc(dsems[op.dsem], 16)
            elif op.signaled:
                inst.then_inc(esem[op.eng], 1)
        for ds in out_dsems:
            if ds in dsems:
                nc.sync.wait_ge(dsems[ds], self.dsem_total[ds])


class Arena:
    def __init__(self, flat, nwords):
        self.flat = flat
        self.n = nwords
        self.off = 0

    def _take(self, words):
        words = (words + 7) // 8 * 8
        o = self.off
        self.off += words
        assert self.off <= self.n, ("arena overflow", self.off, self.n)
        return o

    @staticmethod
    def _shape(v, shape):
        if len(shape) == 1:
            return v
        if len(shape) == 2:
            return v.rearrange("p (a b) -> p a b", a=shape[0])
        if len(shape) == 3:
            return v.rearrange("p (a b c) -> p a b c", a=shape[0], b=shape[1])
        raise ValueError(shape)

    def f32(self, *shape):
        n = int(np.prod(shape))
        o = self._take(n)
        return self._shape(self.flat[:, o:o + n], shape)

    def bf16(self, *shape):
        n = int(np.prod(shape))
        assert n % 2 == 0
        o = self._take(n // 2)
        return self._shape(self.flat[:, o:o + n // 2].bitcast(BF16), shape)


def bcast_mid(ap2d, rep):
    a = ap2d.ap
    assert len(a) == 2
    return bass.AP(ap2d.tensor, ap2d.offset, [list(a[0]), [0, rep], list(a[1])])


def bcast_last(ap2d, rep):
    a = ap2d.ap
    assert len(a) == 2
    return bass.AP(ap2d.tensor, ap2d.offset, [list(a[0]), list(a[1]), [0, rep]])


def bcast_col(ap_col, rep):
    a = ap_col.ap
    return bass.AP(ap_col.tensor, ap_col.offset, [list(a[0]), [0, rep]])


def build_program(do_ffn=True, do_ssd=True, do_pool=True):
    nc = bass.Bass("TRN2", target_bir_lowering=False)

    def din(name, shape):
        return nc.dram_tensor(name, list(shape), F32, kind="ExternalInput").ap()

    def dout(name, shape):
        return nc.dram_tensor(name, list(shape), F32, kind="ExternalOutput").ap()

    xT = din("xT", [1024, NTOK])
    ssmT = din("ssmT", [8, 128, 16 * 256])
    convT = din("convT", [128, 32 * 48])
    poolT = din("poolT", [1024, 16, 15])
    cols_d = din("cols", [128, NCOL])
    rows_d = din("rows", [128, 64])
    const_d = din("consts", [128, NCONST])
    w_gate = din("ffn_w_gate", [2, 2, 1024, D_FF])
    w_up = din("ffn_w_up", [2, 2, 1024, D_FF])
    w_down = din("ffn_w_down", [2, 2, D_FF, 1024])
    ssd_w_in = din("ssd_w_in", [1024, 6176])
    ssd_w_out = din("ssd_w_out", [2048, 1024])
    pool_w_in = din("pool_w_in", [1024, 1024])
    pool_w_group = din("pool_w_group", [4, 256, 256])
    pool_w_out = din("pool_w_out", [1024, 1024])

    yT = dout("yT", [1024, NTOK])
    ssm_p_out = dout("ssm_p_out", [128, 2048])
    conv_p_out = dout("conv_p_out", [128, 32 * 3])
    pool_p_out = dout("pool_p_out", [1024, 15])
    ssm_s_out = dout("ssm_s_out", [8, 128, 16 * 256])
    conv_s_out = dout("conv_s_out", [128, 32 * 48])
    pool_s_out = dout("pool_s_out", [1024, 16, 15])

    def sb(name, shape, dt):
        return nc.alloc_sbuf_tensor(name, list(shape), dt).ap()

    x = sb("x", [128, 8, NTOK], F32)
    wA = sb("wA", [128, 8, 1024], BF16)
    wBf = sb("wB", [128, 12288], BF16)
    cols = sb("colsb", [128, NCOL], F32)
    rows = sb("rowsb", [128, 64], F32)
    consts = sb("constsb", [128, NCONST], F32)
    ident_bf = sb("ident_bf", [128, 128], BF16)
    ones_bf = sb("ones_bf", [128, 128], BF16)
    a_row = sb("a_row", [128, 32], F32)
    wdt = sb("wdt", [128, 8, 32], BF16)
    wgrp = sb("wgrp", [128, 8, 256], BF16)
    UW = 22944
    un = sb("union", [128, UW], F32)

    PS = [nc.alloc_psum_tensor("ps%d" % i, [128, 512], F32).ap() for i in range(7)]
    PT = nc.alloc_psum_tensor("pst", [128, 1024], BF16).ap()

    S = Sched(nc)
    out_dsems = []

    S.add("sp", lambda e: e.dma_start(out=cols, in_=cols_d), writes=["cols"], dsem="cols")
    S.add("sp", lambda e: e.dma_start(out=rows, in_=rows_d), writes=["rows"], dsem="rows")
    S.add("sp", lambda e: e.dma_start(out=consts, in_=const_d), writes=["consts"], dsem="consts")
    xTv = xT.rearrange("(k p) t -> p k t", p=128)
    for k in range(8):
        S.add("sp", lambda e, k=k: e.dma_start(out=x[:, k, :], in_=xTv[:, k, :]), writes=[("x", k, b) for b in range(5)], dsem="xin")
    S.add("dve", lambda e: e.tensor_copy(out=ident_bf, in_=consts[:, K_ID:K_ID + 128]), reads=["consts"], writes=["ident"])
    S.add("dve", lambda e: e.tensor_copy(out=ones_bf, in_=consts[:, K_ONES:K_ONES + 128]), reads=["consts"], writes=["ones"])

    def col(off, k):
        return cols[:, off + k:off + k + 1]

    nrm_ctr = [0]

    def rmsnorm(ar_sq, ar_rstd, goff, bi, out_fn, out_keys_fn, ps_norm, ps_keys=("psn",), span=None):
        c0, n = BLOCKS[bi] if span is None else span
        for k in range(8):
            i = 0 if ar_sq[0] is ar_sq[1] else nrm_ctr[0] % 2
            nrm_ctr[0] += 1
            sq = ar_sq[i]
            S.add("act", lambda e, k=k, sq=sq: e.activation(out=sq[:, 0:n], in_=x[:, k, c0:c0 + n], func=AF.Square),
                  reads=[("x", k, bi)], writes=[("sq", i)])
            S.add("pe", lambda e, k=k, sq=sq: e.matmul(ps_norm[:, 0:n], lhsT=ones_bf, rhs=sq[:, 0:n], start=(k == 0), stop=(k == 7)),
                  reads=[("sq", i), "ones"], writes=list(ps_keys))
        S.add("act", lambda e: e.activation(out=ar_rstd[:, 0:n], in_=ps_norm[:, 0:n], func=AF.Ln, bias=EPS, scale=1.0 / 1024.0),
              reads=list(ps_keys), writes=["rstd"])
        S.add("act", lambda e: e.activation(out=ar_rstd[:, 0:n], in_=ar_rstd[:, 0:n], func=AF.Exp, scale=-0.5), reads=["rstd"], writes=["rstd"])
        for k in range(8):
            S.add("dve", lambda e, k=k: e.scalar_tensor_tensor(out=out_fn(k), in0=x[:, k, c0:c0 + n], scalar=col(goff, k),
                                                               in1=ar_rstd[:, 0:n], op0=ALU.mult, op1=ALU.mult),
                  reads=[("x", k, bi), "rstd", "cols"], writes=out_keys_fn(k))

    SPLITS = [(0, 4), (4, 4), (8, 3)]

    def ffn_phase(i, j):
        ar = Arena(un, UW)
        xn = ar.bf16(8, NTOK)
        h = ar.bf16(8, NTOK)
        sq = [ar.bf16(512), ar.bf16(512)]
        rstd = ar.f32(512)
        sg = [ar.bf16(512), ar.bf16(512)]
        goff = C_FN + (i * 2 + j) * 8
        for bi in range(5):
            c0, n = BLOCKS[bi]
            rmsnorm(sq, rstd, goff, bi, lambda k, c0=c0, n=n: xn[:, k, c0:c0 + n], lambda k, bi=bi: [("xn", k, bi)], PS[6])
        wgd = w_gate[i, j]
        wud = w_up[i, j]
        wdd = w_down[i, j]
        wslots = [wBf[:, s * 2048:(s + 1) * 2048].rearrange("p (k n) -> p k n", k=8) for s in range(4)]
        ev = 0

        def load_gu(cc):
            sg_slot = (cc % 2) * 2
            su_slot = sg_slot + 1
            wg_s = wslots[sg_slot]
            wu_s = wslots[su_slot]
            S.add("pool", lambda e: e.dma_start(out=wg_s, in_=wgd[:, cc * 256:(cc + 1) * 256].rearrange("(k p) n -> p k n", p=128)),
                  writes=[("wB", sg_slot)], dsem=("wB", sg_slot))
            S.add("pool", lambda e: e.dma_start(out=wu_s, in_=wud[:, cc * 256:(cc + 1) * 256].rearrange("(k p) n -> p k n", p=128)),
                  writes=[("wB", su_slot)], dsem=("wB", su_slot))

        load_gu(0)
        for (cc0, ncc) in SPLITS:
            nh = ncc * 2
            r0 = cc0 * 256
            S.add("pool", lambda e, r0=r0, nh=nh: e.dma_start(out=wA[:, 0:nh, :], in_=wdd[r0:r0 + nh * 128, :].rearrange("(k p) n -> p k n", p=128)),
                  writes=["wA"], dsem="wA")
            for cc in range(cc0, cc0 + ncc):
                sg_slot = (cc % 2) * 2
                su_slot = sg_slot + 1
                wg_s = wslots[sg_slot]
                wu_s = wslots[su_slot]
                if cc + 1 < 11:
                    load_gu(cc + 1)
                for sub in range(2):
                    hc = (cc - cc0) * 2 + sub
                    for bi in range(5):
                        c0, n = BLOCKS[bi]
                        pg = PS[ev % 2]
                        pu = PS[2 + ev % 2]
                        sgt = sg[ev % 2]
                        evi = ev % 2
                        ev += 1
                        for k in range(8):
                            S.add("pe", lambda e, k=k, pg=pg, wg_s=wg_s, sub=sub, c0=c0, n=n: e.matmul(
                                pg[:, 0:n], lhsT=wg_s[:, k, sub * 128:(sub + 1) * 128], rhs=xn[:, k, c0:c0 + n], start=(k == 0), stop=(k == 7)),
                                reads=[("wB", sg_slot), ("xn", k, bi)], writes=[("pg", evi)])
                        for k in range(8):
                            S.add("pe", lambda e, k=k, pu=pu, wu_s=wu_s, sub=sub, c0=c0, n=n: e.matmul(
                                pu[:, 0:n], lhsT=wu_s[:, k, sub * 128:(sub + 1) * 128], rhs=xn[:, k, c0:c0 + n], start=(k == 0), stop=(k == 7)),
                                reads=[("wB", su_slot), ("xn", k, bi)], writes=[("pu", evi)])
                        S.add("act", lambda e, pg=pg, sgt=sgt, n=n: e.activation(out=sgt[:, 0:n], in_=pg[:, 0:n], func=AF.Silu),
                              reads=[("pg", evi)], writes=[("sg", evi)])
                        S.add("dve", lambda e, pu=pu, sgt=sgt, hc=hc, c0=c0, n=n: e.tensor_tensor(
                            out=h[:, hc, c0:c0 + n], in0=sgt[:, 0:n], in1=pu[:, 0:n], op=ALU.mult),
                            reads=[("sg", evi), ("pu", evi)], writes=[("h", hc, bi)])
            dctr = 0
            for bi in range(5):
                c0, n = BLOCKS[bi]
                for m in range(8):
                    pd = PS[4 + dctr % 2]
                    di = dctr % 2
                    dctr += 1
                    for kk in range(nh):
                        S.add("pe", lambda e, kk=kk, pd=pd, m=m, c0=c0, n=n, nh=nh: e.matmul(
                            pd[:, 0:n], lhsT=wA[:, kk, m * 128:(m + 1) * 128], rhs=h[:, kk, c0:c0 + n], start=(kk == 0), stop=(kk == nh - 1)),
                            reads=["wA", ("h", kk, bi)], writes=[("pd", di)])
                    S.add("dve", lambda e, pd=pd, m=m, c0=c0, n=n: e.scalar_tensor_tensor(
                        out=x[:, m, c0:c0 + n], in0=pd[:, 0:n], scalar=0.5, in1=x[:, m, c0:c0 + n], op0=ALU.mult, op1=ALU.add),
                        reads=[("pd", di), ("x", m, bi)], writes=[("x", m, bi)])

    def pool_phase():
        ar = Arena(un, UW)
        xn = ar.bf16(8, NTOK)
        mixed = ar.bf16(8, NTOK)
        sq = [ar.bf16(512), ar.bf16(512)]
        rstd = ar.f32(512)
        ext_p = [ar.f32(528) for _ in range(2)]
        ext_s = [ar.f32(16, 23) for _ in range(2)]
        tp = [ar.f32(528) for _ in range(2)]
        ts_ = [ar.f32(16, 23) for _ in range(2)]
        fix = ar.f32(16)
        goff = C_MN + 8
        S.add("pool", lambda e: e.dma_start(out=wA, in_=pool_w_in.rearrange("(k p) n -> p k n", p=128)), writes=["wA"], dsem="wA")
        wo = wBf[:, 0:8192].rearrange("p (k n) -> p k n", k=8)
        S.add("pool", lambda e: e.dma_start(out=wo, in_=pool_w_out.rearrange("(k p) n -> p k n", p=128)),
              writes=[("wB", s_) for s_ in range(4)], dsem=("wB", 0))
        S.add("pool", lambda e: e.dma_start(out=wgrp.rearrange("p (g kk) n -> p g kk n", g=4), in_=pool_w_group.rearrange("g (kk p) n -> p g kk n", p=128)),
              writes=["wgrp"], dsem="wgrp")
        for bi in range(5):
            c0, n = BLOCKS[bi]
            rmsnorm(sq, rstd, goff, bi, lambda k, c0=c0, n=n: xn[:, k, c0:c0 + n], lambda k, bi=bi: [("xn", k, bi)], PS[6])
        ev = 0
        ectr = 0
        for m in range(8):
            widx = m // 2
            w = POOL_WINDOWS[widx]
            prev = None
            for bi in range(5):
                c0, n = BLOCKS[bi]
                pp = PS[ev % 2]
                pi = ev % 2
                ev += 1
                for k in range(8):
                    S.add("pe", lambda e, k=k, pp=pp, m=m, c0=c0, n=n: e.matmul(
                        pp[:, 0:n], lhsT=wA[:, k, m * 128:(m + 1) * 128], rhs=xn[:, k, c0:c0 + n], start=(k == 0), stop=(k == 7)),
                        reads=["wA", ("xn", k, bi)], writes=[("pg", pi)])
                es = ectr % 2
                ectr += 1
                if bi < 4:
                    ep = ext_p[es]
                    ekey = ("extp", es)
                    if bi == 0:
                        S.add("pool", lambda e, ep=ep: e.memset(ep[:, 0:15], 0.0), writes=[ekey])
                    else:
                        S.add("pool", lambda e, ep=ep, prev=prev: e.tensor_copy(out=ep[:, 0:15], in_=prev[:, 512:527]), reads=[("extp", 1 - es)], writes=[ekey])
                    S.add("act", lambda e, pp=pp, ep=ep: e.activation(out=ep[:, 15:527], in_=pp[:, 0:512], func=AF.Copy),
                          reads=[("pg", pi)], writes=[ekey])
                    prev = ep
                    if bi == 3:
                        S.add("sp", lambda e, m=m, ep=ep: e.dma_start(out=pool_p_out[m * 128:(m + 1) * 128, :], in_=ep[:, 512:527]),
                              reads=[ekey], dsem="pool_p_out")
                    src = ep
                    skey = ekey
                    hi = 527
                    sl = lambda a, lo_, hi_: a[:, lo_:hi_]
                    tmps = tp
                    tkey = "tp"
                else:
                    esm = ext_s[es]
                    ekey = ("exts", es)
                    for bq in range(2):
                        S.add("sp", lambda e, m=m, esm=esm, bq=bq: e.dma_start(out=esm[:, bq * 8:(bq + 1) * 8, 0:15], in_=poolT[m * 128:(m + 1) * 128, bq * 8:(bq + 1) * 8, :]),
                              writes=[ekey], dsem=("exts", es))
                    S.add("act", lambda e, pp=pp, esm=esm: e.activation(out=esm[:, :, 15:23], in_=pp[:, 0:128].rearrange("p (b t) -> p b t", b=16), func=AF.Copy),
                          reads=[("pg", pi)], writes=[ekey])
                    for bq in range(2):
                        S.add("sp", lambda e, m=m, esm=esm, bq=bq: e.dma_start(out=pool_s_out[m * 128:(m + 1) * 128, bq * 8:(bq + 1) * 8, :], in_=esm[:, bq * 8:(bq + 1) * 8, 8:23]),
                              reads=[ekey], dsem="pool_s_out")
                    src = esm
                    skey = ekey
                    hi = 23
                    sl = lambda a, lo_, hi_: a[:, :, lo_:hi_]
                    tmps = ts_
                    tkey = "ts"
                base = src
                lo = 0
                step = 1
                lvl = 0
                while step < w:
                    dst = tmps[lvl % 2]
                    nlo = lo + step
                    S.add("pool", lambda e, dst=dst, src=src, nlo=nlo, step=step, hi=hi, sl=sl: e.tensor_tensor(
                        out=sl(dst, nlo, hi), in0=sl(src, nlo, hi), in1=sl(src, nlo - step, hi - step), op=ALU.add),
                        reads=[skey], writes=[(tkey, lvl % 2)])
                    src = dst
                    skey = (tkey, lvl % 2)
                    lo = nlo
                    step *= 2
                    lvl += 1
                if bi < 4:
                    S.add("dve", lambda e, src=src, base=base, m=m, w=w, c0=c0: e.scalar_tensor_tensor(
                        out=mixed[:, m, c0:c0 + 512], in0=src[:, 15:527], scalar=1.0 / w, in1=base[:, 15:527], op0=ALU.mult, op1=ALU.subtract),
                        reads=[skey, ekey], writes=[("mixed", m, bi)])
                    if bi == 0:
                        S.add("dve", lambda e, src=src, widx=widx: e.tensor_tensor(
                            out=fix, in0=src[:, 15:31], in1=consts[:, K_INVC + widx * 16:K_INVC + widx * 16 + 16], op=ALU.mult),
                            reads=[skey, "consts"], writes=["fix"])
                        S.add("dve", lambda e, base=base, m=m: e.tensor_tensor(out=mixed[:, m, 0:16], in0=fix, in1=base[:, 15:31], op=ALU.subtract),
                              reads=["fix", ekey], writes=[("mixed", m, 0)])
                else:
                    S.add("dve", lambda e, src=src, base=base, m=m, w=w: e.scalar_tensor_tensor(
                        out=mixed[:, m, 2048:2176].rearrange("p (b t) -> p b t", b=16), in0=src[:, :, 15:23], scalar=1.0 / w, in1=base[:, :, 15:23],
                        op0=ALU.mult, op1=ALU.subtract),
                        reads=[skey, ekey], writes=[("mixed", m, 4)])
        wg4 = wgrp.rearrange("p (g kk) n -> p g kk n", g=4)
        mg = xn
        ev = 0
        for m in range(8):
            g, mm = m // 2, m % 2
            for bi in range(5):
                c0, n = BLOCKS[bi]
                pp = PS[ev % 2]
                pi = ev % 2
                ev += 1
                for kk in range(2):
                    S.add("pe", lambda e, kk=kk, pp=pp, g=g, mm=mm, c0=c0, n=n: e.matmul(
                        pp[:, 0:n], lhsT=wg4[:, g, kk, mm * 128:(mm + 1) * 128], rhs=mixed[:, 2 * g + kk, c0:c0 + n], start=(kk == 0), stop=(kk == 1)),
                        reads=["wgrp", ("mixed", 2 * g + kk, bi)], writes=[("pg", pi)])
                S.add("act", lambda e, pp=pp, m=m, c0=c0, n=n: e.activation(out=mg[:, m, c0:c0 + n], in_=pp[:, 0:n], func=AF.Identity, scale=col(C_PS, m)),
                      reads=[("pg", pi), "cols"] + [("xn", k, bi) for k in range(8)], writes=[("mg", m, bi), ("xn", m, bi)])
        ev = 0
        for bi in range(5):
            c0, n = BLOCKS[bi]
            for m in range(8):
                pd = PS[4 + ev % 2]
                di = ev % 2
                ev += 1
                for k in range(8):
                    S.add("pe", lambda e, k=k, pd=pd, m=m, c0=c0, n=n: e.matmul(
                        pd[:, 0:n], lhsT=wo[:, k, m * 128:(m + 1) * 128], rhs=mg[:, k, c0:c0 + n], start=(k == 0), stop=(k == 7)),
                        reads=[("wB", 0), ("mg", k, bi)], writes=[("pd", di)])
                S.add("dve", lambda e, pd=pd, m=m, c0=c0, n=n: e.tensor_tensor(out=x[:, m, c0:c0 + n], in0=x[:, m, c0:c0 + n], in1=pd[:, 0:n], op=ALU.add),
                      reads=[("pd", di), ("x", m, bi)], writes=[("x", m, bi)])

    def ssd_phase():
        SBLK = [(i * 256, 256, i // 2) for i in range(8)] + [(2048, 128, 4)]
        SM = PS[2]

        def pk(i):
            return ("ps", i)

        def common(ar):
            d = {}
            _sq = ar.bf16(256)
            d["sq"] = [_sq, _sq]
            d["rstd"] = ar.f32(256)
            for nm in ("dt", "a", "acum", "dend", "cd", "f2"):
                d[nm] = ar.f32(2, 32)
            d["dtx"] = d["dt"]
            d["seg"] = [ar.f32(4, 128), ar.f32(4, 128)]
            d["E"] = [ar.f32(4, 128), ar.f32(4, 128)]
            d["M"] = [ar.bf16(4, 128), ar.bf16(4, 128)]
            d["Cp"] = [ar.bf16(4, 128), ar.bf16(4, 128)]
            d["xdt"] = [ar.bf16(256), ar.bf16(256)]
            d["xdtd"] = [ar.bf16(256), ar.bf16(256)]
            d["y1"] = [ar.f32(2, 128), ar.f32(2, 128)]
            d["y2"] = [ar.f32(2, 128), ar.f32(2, 128)]
            d["ysq"] = [ar.bf16(2, 128), ar.bf16(2, 128)]
            d["rg"] = [ar.f32(128), ar.f32(128)]
            d["sttmp"] = [ar.f32(256), ar.f32(256)]
            return d

        def blockbufs(ar, n):
            nt = n // 128
            d = {}
            d["xnb"] = ar.bf16(8, n)
            d["sz"] = [ar.bf16(2, n) for _ in range(8)]
            d["xsT"] = [ar.bf16(2, n) for _ in range(8)]
            d["BT"] = [ar.bf16(n) for _ in range(8)]
            d["CT"] = [ar.bf16(n) for _ in range(8)]
            d["tok"] = [ar.bf16(nt, 384) for _ in range(8)]
            d["raw"] = [ar.f32(n + 8), ar.f32(n + 8)] if n == 256 else [ar.f32(16, 11), ar.f32(16, 11)]
            d["acc"] = [ar.f32(n), ar.f32(n)]
            d["yn"] = ar.bf16(16, n)
            return d

        arp = Arena(un, UW)
        cm = common(arp)
        bp = blockbufs(arp, 256)
        carry = arp.f32(32, 3)
        hstate = arp.f32(2048)
        hbf = arp.bf16(2048)

        ars = Arena(un, UW)
        common(ars)
        bs = blockbufs(ars, 128)
        convh = ars.f32(32, 48)
        cd_all = ars.f32(16, 32)
        h0g = [ars.f32(8, 256), ars.f32(8, 256)]
        h0bf = [ars.bf16(8, 256), ars.bf16(8, 256)]
        Bm = ars.bf16(16, 128)

        S.add("pool", lambda e: e.dma_start(out=wdt, in_=ssd_w_in[:, 6144:6176].rearrange("(k p) n -> p k n", p=128)), writes=["wdt"], dsem="wdt")
        S.add("act", lambda e: e.activation(out=a_row, in_=rows[:, 32:64], func=AF.Exp), reads=["rows"], writes=["a_row"])
        S.add("dve", lambda e: e.tensor_scalar_mul(out=a_row, in0=a_row, scalar1=-1.0), reads=["a_row"], writes=["a_row"])

        win_slots = [wBf[:, s_ * 6144:(s_ + 1) * 6144].rearrange("p (k n) -> p k n", k=8) for s_ in range(2)]
        win_slots.append(wA.rearrange("p k n -> p (k n)")[:, 0:6144].rearrange("p (k n) -> p k n", k=8))
        WIN_SLOT_OF_G = [0, 1, 2, 0, 1, 2, 0, 1]
        wo_slots = [wA[:, s_ * 4:(s_ + 1) * 4, :] for s_ in range(2)]
        gctr = [0]
        woctr = [0]
        rawctr = [0]
        tctr = [0]
        cctr = [0]
        sctr = [0]
        hctr = [0]

        win_loaded = {}

        def get_win(sbi_, g):
            if sbi_ > 8:
                return None
            if (sbi_, g) not in win_loaded:
                win_loaded[(sbi_, g)] = load_win(g)
            return win_loaded[(sbi_, g)]

        wo_loaded = {}

        def get_wo(sbi_, q):
            if sbi_ > 8:
                return None
            if (sbi_, q) not in wo_loaded:
                ws_i = woctr[0] % 2
                woctr[0] += 1
                wq = wo_slots[ws_i]
                S.add("pool", lambda e, q=q, wq=wq: e.dma_start(out=wq, in_=ssd_w_out[q * 512:(q + 1) * 512, :].rearrange("(k p) n -> p k n", p=128)),
                      writes=[("wo", ws_i), ("win", 2)], dsem=("wo", ws_i))
                wo_loaded[(sbi_, q)] = ws_i
            return wo_loaded[(sbi_, q)]

        def load_win(g):
            s_ = WIN_SLOT_OF_G[g]
            ws = win_slots[s_]
            wkeys = [("win", s_)] + ([("wo", 0), ("wo", 1)] if s_ == 2 else [])
            S.add("pool", lambda e, ws=ws: e.dma_start(out=ws, in_=ssd_w_in[:, g * 768:(g + 1) * 768].rearrange("(k p) n -> p k n", p=128)),
                  writes=wkeys, dsem=("win", s_))
            return s_

        def ssd_block(sbi):
            c0, n, kb = SBLK[sbi]
            nt = n // 128
            samp = (sbi == 8)
            first_blk = (sbi == 0)
            bb = bs if samp else bp
            xnb = bb["xnb"]
            yn = bb["yn"]
            UM = consts[:, K_US:K_US + 128] if samp else consts[:, K_UP:K_UP + 128]
            NEG = consts[:, K_NEGS:K_NEGS + 128] if samp else consts[:, K_NEGP:K_NEGP + 128]
            LAST = consts[:, K_BD:K_BD + 128] if samp else consts[:, K_ONES:K_ONES + 128]
            dtx, dt, a_, acum, dend, cd, f2 = (cm[k_] for k_ in ("dtx", "dt", "a", "acum", "dend", "cd", "f2"))
            rmsnorm(cm["sq"], cm["rstd"], C_MN, kb, lambda k: xnb[:, k, 0:n], lambda k: [("xnb", k)], SM, ps_keys=(pk(2),), span=(c0, n))
            for t in range(nt):
                for k in range(8):
                    S.add("pe", lambda e, t=t, k=k: e.matmul(SM[:, 128 + t * 32:128 + (t + 1) * 32], lhsT=xnb[:, k, t * 128:(t + 1) * 128], rhs=wdt[:, k, :],
                                                             start=(k == 0), stop=(k == 7)),
                          reads=[("xnb", k), "wdt"], writes=[pk(2)])
            smdt = SM[:, 128:128 + nt * 32].rearrange("p (t h) -> p t h", t=nt)
            S.add("dve", lambda e: e.tensor_tensor(out=dtx[:, 0:nt, :], in0=smdt, in1=bcast_mid(rows[:, 0:32], nt), op=ALU.add),
                  reads=[pk(2), "rows"], writes=["dt"])
            S.add("act", lambda e: e.activation(out=dtx[:, 0:nt, :], in_=dtx[:, 0:nt, :], func=AF.Exp), reads=["dt"], writes=["dt"])
            S.add("act", lambda e: e.activation(out=dt[:, 0:nt, :], in_=dtx[:, 0:nt, :], func=AF.Ln, bias=1.0), reads=["dt"], writes=["dt"])
            S.add("dve", lambda e: e.tensor_tensor(out=a_[:, 0:nt, :], in0=dt[:, 0:nt, :], in1=bcast_mid(a_row, nt), op=ALU.mult),
                  reads=["dt", "a_row"], writes=["a"])
            for t in range(nt):
                S.add("pe", lambda e, t=t: e.matmul(SM[:, 256:288], lhsT=UM, rhs=a_[:, t, :], start=True, stop=True),
                      reads=["a", "consts"], writes=[pk(2)])
                S.add("act", lambda e, t=t: e.activation(out=acum[:, t, :], in_=SM[:, 256:288], func=AF.Copy), reads=[pk(2)], writes=["acum"])
                S.add("pe", lambda e, t=t: e.matmul(SM[:, 288:320], lhsT=LAST, rhs=a_[:, t, :], start=True, stop=True),
                      reads=["a", "consts"], writes=[pk(2)])
                S.add("dve", lambda e, t=t: e.tensor_tensor(out=dend[:, t, :], in0=SM[:, 288:320], in1=acum[:, t, :], op=ALU.subtract),
                      reads=[pk(2), "acum"], writes=["dend"])
                S.add("act", lambda e, t=t: e.activation(out=cd[:, t, :], in_=SM[:, 288:320], func=AF.Exp), reads=[pk(2)], writes=["cd"])
            S.add("act", lambda e: e.activation(out=dend[:, 0:nt, :], in_=dend[:, 0:nt, :], func=AF.Exp), reads=["dend"], writes=["dend"])
            S.add("dve", lambda e: e.tensor_tensor(out=f2[:, 0:nt, :], in0=dt[:, 0:nt, :], in1=dend[:, 0:nt, :], op=ALU.mult),
                  reads=["dt", "dend"], writes=["f2"])
            if samp:
                for b in range(16):
                    S.add("pe", lambda e, b=b: e.matmul(SM[:, 0:512][:, b * 32:(b + 1) * 32], lhsT=bcast_col(consts[:, K_SEQM + b:K_SEQM + b + 1], 128), rhs=a_[:, 0, :],
                                                        start=True, stop=True),
                          reads=["a", "consts", "acum", "dend", "cd"], writes=[pk(2)])
                S.add("act", lambda e: e.activation(out=cd_all, in_=SM.rearrange("p (b h) -> p b h", b=16), func=AF.Exp), reads=[pk(2)], writes=["cd_all"])
                S.add("sp", lambda e: e.dma_start(out=convh.rearrange("p j c -> p (j c)"), in_=convT), writes=["convh"], dsem="convh")

            def inproj(g, ws_i):
                ws = win_slots[ws_i]
                sz = bb["sz"][g]; xsT = bb["xsT"][g]; BT = bb["BT"][g]; CT = bb["CT"][g]; tok = bb["tok"][g]
                pend = []
                for cc in range(6):
                    pai = cctr[0] % 2
                    cctr[0] += 1
                    pa = PS[pai]
                    for k in range(8):
                        S.add("pe", lambda e, k=k, pa=pa, cc=cc: e.matmul(pa[:, 0:n], lhsT=ws[:, k, cc * 128:(cc + 1) * 128], rhs=xnb[:, k, 0:n],
                                                                         start=(k == 0), stop=(k == 7)),
                              reads=[("win", ws_i), ("xnb", k)], writes=[pk(pai)])
                    if cc < 2:
                        S.add("act", lambda e, pa=pa, cc=cc: e.activation(out=sz[:, cc, 0:n], in_=pa[:, 0:n], func=AF.Silu),
                              reads=[pk(pai)], writes=[("sz", g)])
                        continue
                    j = (2 * g + cc - 2) if cc < 4 else ((16 + g) if cc == 4 else (24 + g))
                    ri = rawctr[0] % 2
                    rawctr[0] += 1
                    raw = bb["raw"][ri]
                    acc = bb["acc"][ri]
                    if samp:
                        rdat = raw[:, :, 3:11]
                        pav = pa[:, 0:128].rearrange("p (b t) -> p b t", b=16)
                        accv = acc[:, 0:128].rearrange("p (b t) -> p b t", b=16)
                        taps = [raw[:, :, kk:kk + 8] for kk in range(3)]
                        S.add("dve", lambda e, raw=raw, j=j: e.tensor_copy(out=raw[:, :, 0:3], in_=convh[:, j, :].rearrange("p (b k) -> p b k", b=16)),
                              reads=["convh"], writes=[("rawh", ri)])
                    else:
                        rdat = raw[:, 3:3 + n]
                        pav = pa[:, 0:n]
                        accv = acc[:, 0:n]
                        taps = [raw[:, kk:kk + n] for kk in range(3)]
                        if first_blk:
                            S.add("dve", lambda e, raw=raw: e.memset(raw[:, 0:3], 0.0), writes=[("rawh", ri)])
                        else:
                            S.add("dve", lambda e, raw=raw, j=j: e.tensor_copy(out=raw[:, 0:3], in_=carry[:, j, :]), reads=[("carry", j)], writes=[("rawh", ri)])
                    S.add("act", lambda e, rdat=rdat, pav=pav: e.activation(out=rdat, in_=pav, func=AF.Copy), reads=[pk(pai)], writes=[("raw", ri)])
                    S.add("act", lambda e, accv=accv, pav=pav, j=j: e.activation(out=accv, in_=pav, func=AF.Identity, scale=col(C_CW, 3 * 32 + j), bias=col(C_CB, j)),
                          reads=[pk(pai), "cols"], writes=[("acc", ri)])
                    while pend:
                        pend.pop(0)()
                    if samp:
                        S.add("dve", lambda e, raw=raw, j=j: e.tensor_copy(out=convh[:, j, :].rearrange("p (b k) -> p b k", b=16), in_=raw[:, :, 8:11]),
                              reads=[("raw", ri)], writes=["convh"])
                    else:
                        S.add("dve", lambda e, raw=raw, j=j: e.tensor_copy(out=carry[:, j, :], in_=raw[:, n:n + 3]), reads=[("raw", ri)], writes=[("carry", j)])
                    for kk in range(3):
                        S.add("dve", lambda e, accv=accv, tp_=taps[kk], kk=kk, j=j: e.scalar_tensor_tensor(
                            out=accv, in0=tp_, scalar=col(C_CW, kk * 32 + j), in1=accv, op0=ALU.mult, op1=ALU.add),
                            reads=[("raw", ri), ("rawh", ri), ("acc", ri), "cols"], writes=[("acc", ri)])
                    if cc < 4:
                        dst = xsT[:, cc - 2, 0:n]; dk = ("xsT", g)
                    elif cc == 4:
                        dst = BT[:, 0:n]; dk = ("BT", g)
                    else:
                        dst = CT[:, 0:n]; dk = ("CT", g)
                    pend.append(lambda dst=dst, acc=acc, ri=ri, dk=dk: S.add(
                        "act", lambda e: e.activation(out=dst, in_=acc[:, 0:n], func=AF.Silu), reads=[("acc", ri)], writes=[dk]))
                while pend:
                    pend.pop(0)()
            def inproj_tr(g):
                xsT = bb["xsT"][g]; BT = bb["BT"][g]; tok = bb["tok"][g]
                for t in range(nt):
                    th = tctr[0] % 2
                    tctr[0] += 1
                    tb_ = th * 512
                    for jj in range(2):
                        S.add("pe", lambda e, jj=jj, t=t, tb_=tb_: e.transpose(PT[:, tb_ + jj * 128:tb_ + (jj + 1) * 128], xsT[:, jj, t * 128:(t + 1) * 128], ident_bf),
                              reads=[("xsT", g), "ident"], writes=["pt"])
                    S.add("pe", lambda e, t=t, tb_=tb_: e.transpose(PT[:, tb_ + 256:tb_ + 384], BT[:, t * 128:(t + 1) * 128], ident_bf),
                          reads=[("BT", g), "ident"], writes=["pt"])
                    S.add("dve", lambda e, t=t, tb_=tb_: e.tensor_copy(out=tok[:, t, :], in_=PT[:, tb_:tb_ + 384]), reads=["pt"], writes=[("tok", g)])

            def _unpack(c):
                return (c[k_] for k_ in ("t", "g", "sz", "xsT", "BT", "CT", "tok", "si", "iAB", "iCB", "iYB", "iST", "AB", "CBG", "YB", "ST",
                                         "seg", "E", "M", "Cp", "xdt", "xdtd", "y1", "y2", "ysq", "rg", "sttmp", "first"))

            def scan_gen(t, g):
                sz = bb["sz"][g]; xsT = bb["xsT"][g]; BT = bb["BT"][g]; CT = bb["CT"][g]; tok = bb["tok"][g]
                si = sctr[0] % 2
                sctr[0] += 1
                iAB, iCB, iYB, iST = 0 + si, 2 + si, 4 + si, 6
                AB, CBG, YB, ST = PS[iAB], PS[iCB], PS[iYB], PS[iST]
                seg = cm["seg"][si]; E = cm["E"][si]; M = cm["M"][si]; Cp = cm["Cp"][si]
                xdt = cm["xdt"][si]; xdtd = cm["xdtd"][si]; y1 = cm["y1"][si]; y2 = cm["y2"][si]
                ysq = cm["ysq"][si]; rg = cm["rg"][si]; sttmp = cm["sttmp"][si]
                first = (first_blk and t == 0)
                S.add("pe", lambda e: e.matmul(CBG[:, 0:128], lhsT=BT[:, t * 128:(t + 1) * 128], rhs=CT[:, t * 128:(t + 1) * 128], start=True, stop=True),
                      reads=[("BT", g), ("CT", g)], writes=[pk(iCB)])
                for hh in range(4):
                    hd = 4 * g + hh
                    S.add("pe", lambda e, hh=hh, hd=hd: e.matmul(AB[:, hh * 128:(hh + 1) * 128], lhsT=bcast_col(a_[:, t, hd:hd + 1], 128), rhs=UM, start=True, stop=True),
                          reads=["a", "consts"], writes=[pk(iAB)])
                xs4 = tok[:, t, 0:256].rearrange("p (h q) -> p h q", h=4)
                S.add("pool", lambda e: e.tensor_tensor(out=xdt.rearrange("p (h q) -> p h q", h=4), in0=xs4, in1=bcast_last(dt[:, t, 4 * g:4 * g + 4], 64), op=ALU.mult),
                      reads=[("tok", g), "dt"], writes=[("xdt", si)])
                S.add("pool", lambda e: e.tensor_tensor(out=xdtd.rearrange("p (h q) -> p h q", h=4), in0=xs4, in1=bcast_last(f2[:, t, 4 * g:4 * g + 4], 64), op=ALU.mult),
                      reads=[("tok", g), "f2"], writes=[("xdtd", si)])
                if not samp and not first:
                    hsl0 = hstate[:, g * 256:(g + 1) * 256]
                    S.add("pool", lambda e: e.tensor_tensor(out=sttmp.rearrange("p (h q) -> p h q", h=4), in0=hsl0.rearrange("p (h q) -> p h q", h=4),
                                                           in1=bcast_last(cd[:, t, 4 * g:4 * g + 4], 64), op=ALU.mult),
                          reads=[("hst", g), "cd"], writes=[("sttmp", si)])
                yield
                for hh in range(4):
                    hd = 4 * g + hh
                    S.add("dve", lambda e, hh=hh, hd=hd: e.scalar_tensor_tensor(
                        out=seg[:, hh, :], in0=AB[:, hh * 128:(hh + 1) * 128], scalar=acum[:, t, hd:hd + 1], in1=NEG, op0=ALU.subtract, op1=ALU.add),
                        reads=[pk(iAB), "acum", "consts"], writes=[("seg", si)])
                S.add("act", lambda e: e.activation(out=seg, in_=seg, func=AF.Exp), reads=[("seg", si)], writes=[("seg", si)])
                S.add("act", lambda e: e.activation(out=E, in_=AB.rearrange("p (h l) -> p h l", h=4), func=AF.Exp), reads=[pk(iAB)], writes=[("E", si)])
                yield
                S.add("dve", lambda e: e.tensor_tensor(out=M, in0=seg, in1=bcast_mid(CBG[:, 0:128], 4), op=ALU.mult),
                      reads=[("seg", si), pk(iCB)], writes=[("M", si)])
                S.add("pool", lambda e: e.tensor_tensor(out=Cp, in0=E, in1=bcast_mid(CT[:, t * 128:(t + 1) * 128], 4), op=ALU.mult),
                      reads=[("E", si), ("CT", g)], writes=[("Cp", si)])
                yield
                if samp:
                    for hh in range(4):
                        jj, half = hh // 2, hh % 2
                        yo = YB[half * 64:(half + 1) * 64, jj * 128:(jj + 1) * 128]
                        if hh < 2:
                            S.add("pe", lambda e, yo=yo, hh=hh: e.matmul(yo, lhsT=xdt[:, hh * 64:(hh + 1) * 64], rhs=M[:, hh, :], start=True, stop=True),
                                  reads=[("xdt", si), ("M", si)], writes=[pk(iYB)])
                        else:
                            S.add("pe", lambda e, yo=yo, hh=hh: e.matmul(yo, lhsT=xdt[:, hh * 64:(hh + 1) * 64], rhs=M[:, hh, :], start=False, stop=True, skip_group_check=True),
                                  reads=[("xdt", si), ("M", si)], writes=[pk(iYB)])
                    S.add("pool", lambda e: e.tensor_tensor(out=Bm, in0=bcast_mid(tok[:, 0, 256:384], 16), in1=bcast_last(consts[:, K_SEQM:K_SEQM + 16], 128), op=ALU.mult),
                          reads=[("tok", g), "consts"], writes=["Bm"])
                    for hf in range(2):
                        hs_i = hctr[0] % 2
                        hctr[0] += 1
                        hg = h0g[hs_i]
                        hb_ = h0bf[hs_i]
                        S.add("sp", lambda e, hg=hg, hf=hf: e.dma_start(out=hg.rearrange("p b c -> p (b c)"), in_=ssmT[g][:, hf * 2048:(hf + 1) * 2048]),
                              writes=[("h0g", hs_i)], dsem=("h0g", hs_i))
                        S.add("act", lambda e, hg=hg, hb_=hb_: e.activation(out=hb_, in_=hg, func=AF.Copy), reads=[("h0g", hs_i)], writes=[("h0bf", hs_i)])
                        for hh in range(4):
                            jj, half = hh // 2, hh % 2
                            for bl in range(8):
                                b = hf * 8 + bl
                                S.add("pe", lambda e, b=b, bl=bl, hh=hh, half=half, jj=jj, hb_=hb_: e.matmul(
                                    YB[half * 64:(half + 1) * 64, jj * 128 + b * 8:jj * 128 + (b + 1) * 8], lhsT=hb_[:, bl, hh * 64:(hh + 1) * 64],
                                    rhs=Cp[:, hh, b * 8:(b + 1) * 8], start=False, stop=True, skip_group_check=True),
                                    reads=[("h0bf", hs_i), ("Cp", si)], writes=[pk(iYB)])
                        for bl in range(8):
                            b = hf * 8 + bl
                            pq = PS[6]
                            S.add("pe", lambda e, b=b, pq=pq: e.matmul(pq[:, 0:256], lhsT=Bm[:, b, :], rhs=xdtd, start=True, stop=True),
                                  reads=["Bm", ("xdtd", si)], writes=[pk(6)])
                            S.add("pool", lambda e, b=b, bl=bl, hg=hg: e.tensor_tensor(out=hg[:, bl, :].rearrange("p (h q) -> p h q", h=4), in0=hg[:, bl, :].rearrange("p (h q) -> p h q", h=4),
                                                                                   in1=bcast_last(cd_all[:, b, 4 * g:4 * g + 4], 64), op=ALU.mult),
                                  reads=[("h0g", hs_i), ("h0bf", hs_i), "cd_all"], writes=[("h0g", hs_i)])
                            S.add("dve", lambda e, bl=bl, pq=pq, hg=hg: e.tensor_tensor(out=hg[:, bl, :], in0=hg[:, bl, :], in1=pq[:, 0:256], op=ALU.add),
                                  reads=[("h0g", hs_i), pk(6)], writes=[("h0g", hs_i)])
                        S.add("sp", lambda e, hg=hg, hf=hf: e.dma_start(out=ssm_s_out[g][:, hf * 2048:(hf + 1) * 2048], in_=hg.rearrange("p b c -> p (b c)")),
                              reads=[("h0g", hs_i)], dsem="ssm_s_out")
                else:
                    for hh in range(4):
                        hd = 4 * g + hh
                        jj, half = hh // 2, hh % 2
                        yo = YB[half * 64:(half + 1) * 64, jj * 128:(jj + 1) * 128]
                        S.add("pe", lambda e, yo=yo, hh=hh: e.matmul(yo, lhsT=xdt[:, hh * 64:(hh + 1) * 64], rhs=M[:, hh, :], start=True, stop=first),
                              reads=[("xdt", si), ("M", si)], writes=[pk(iYB)])
                        if not first:
                            S.add("pe", lambda e, yo=yo, hd=hd, hh=hh: e.matmul(yo, lhsT=hbf[:, hd * 64:(hd + 1) * 64], rhs=Cp[:, hh, :], start=False, stop=True),
                                  reads=[("hbf", g), ("Cp", si)], writes=[pk(iYB)])
                    S.add("pe", lambda e: e.matmul(ST[:, 0:256], lhsT=tok[:, t, 256:384], rhs=xdtd, start=True, stop=True),
                          reads=[("tok", g), ("xdtd", si)], writes=[pk(iST)])
                    hsl = hstate[:, g * 256:(g + 1) * 256]
                    hbl = hbf[:, g * 256:(g + 1) * 256]
                    if first:
                        S.add("dve", lambda e: e.tensor_copy(out=hbl, in_=ST[:, 0:256]), reads=[pk(iST)], writes=[("hbf", g)])
                        S.add("act", lambda e: e.activation(out=hsl, in_=ST[:, 0:256], func=AF.Copy), reads=[pk(iST)], writes=[("hst", g)])
                    else:
                        S.add("dve", lambda e: e.tensor_tensor(out=hbl, in0=sttmp, in1=ST[:, 0:256], op=ALU.add),
                              reads=[("sttmp", si), pk(iST)], writes=[("hbf", g)])
                        S.add("dve", lambda e: e.tensor_tensor(out=hsl, in0=sttmp, in1=ST[:, 0:256], op=ALU.add),
                              reads=[("sttmp", si), pk(iST)], writes=[("hst", g)])

                yield
                for jj in range(2):
                    j = 2 * g + jj
                    S.add("dve", lambda e, jj=jj, j=j: e.scalar_tensor_tensor(
                        out=y1[:, jj, :], in0=xsT[:, jj, t * 128:(t + 1) * 128], scalar=col(C_DD, j), in1=YB[:, jj * 128:(jj + 1) * 128], op0=ALU.mult, op1=ALU.add),
                        reads=[("xsT", g), pk(iYB), "cols"], writes=[("y1", si)])
                S.add("pool", lambda e: e.tensor_tensor(out=y2, in0=y1, in1=sz[:, :, t * 128:(t + 1) * 128], op=ALU.mult),
                      reads=[("y1", si), ("sz", g)], writes=[("y2", si)])
                S.add("act", lambda e: e.activation(out=ysq, in_=y2, func=AF.Square), reads=[("y2", si)], writes=[("ysq", si)])
                yield
                for jj in range(2):
                    S.add("pe", lambda e, jj=jj: e.matmul(YB[:, 256:384], lhsT=ones_bf, rhs=ysq[:, jj, :], start=(jj == 0), stop=(jj == 1)),
                          reads=[("ysq", si), "ones"], writes=[pk(iYB)])
                S.add("act", lambda e: e.activation(out=rg, in_=YB[:, 256:384], func=AF.Ln, bias=EPS, scale=1.0 / 256.0), reads=[pk(iYB)], writes=[("rg", si)])
                S.add("act", lambda e: e.activation(out=rg, in_=rg, func=AF.Exp, scale=-0.5), reads=[("rg", si)], writes=[("rg", si)])
                yield
                for jj in range(2):
                    j = 2 * g + jj
                    S.add("dve", lambda e, jj=jj, j=j: e.scalar_tensor_tensor(
                        out=yn[:, j, t * 128:(t + 1) * 128], in0=y2[:, jj, :], scalar=col(C_SN, j), in1=rg, op0=ALU.mult, op1=ALU.mult),
                        reads=[("y2", si), ("rg", si), "cols"], writes=[("yn", j)])

            PH = int(_os.environ.get("SSD_PH", "3"))
            for g in range(8):
                ws_cur = get_win(sbi, g)
                if g + 1 < 8:
                    get_win(sbi, g + 1)
                if g + 2 < 8:
                    get_win(sbi, g + 2)
                inproj(g, ws_cur)
                if g >= 1:
                    inproj_tr(g - 1)
            inproj_tr(7)
            get_win(sbi + 1, 0)
            get_win(sbi + 1, 1)
            get_wo(sbi, 0)
            get_wo(sbi, 1)
            if PH < 2:
                return
            its = [(t, g) for t in range(nt) for g in range(8)]

            def step(gen_):
                try:
                    next(gen_)
                except StopIteration:
                    pass

            if samp:
                for (t_, g_) in its:
                    for _ in scan_gen(t_, g_):
                        pass
            else:
                gens = {}
                N_ = len(its)
                for i in range(N_ + 2):
                    if i < N_:
                        gens[i] = scan_gen(*its[i])
                    if 0 <= i - 1 < N_:
                        step(gens[i - 1])
                    if 0 <= i - 2 < N_:
                        step(gens[i - 2])
                    if i < N_:
                        step(gens[i])
                    if 0 <= i - 2 < N_:
                        step(gens[i - 2])
                    if i < N_:
                        step(gens[i])
                    if 0 <= i - 2 < N_:
                        step(gens[i - 2])
                        step(gens.pop(i - 2))
                    if i < N_:
                        step(gens[i])
            if PH < 3:
                return
            for q in range(4):
                ws_i = get_wo(sbi, q)
                wq = wo_slots[ws_i]
                for m in range(8):
                    po = PS[m % 2]
                    for kk in range(4):
                        S.add("pe", lambda e, kk=kk, m=m, po=po, wq=wq, q=q: e.matmul(po[:, 0:n], lhsT=wq[:, kk, m * 128:(m + 1) * 128], rhs=yn[:, q * 4 + kk, 0:n],
                                                                                    start=(kk == 0), stop=(kk == 3)),
                              reads=[("wo", ws_i), ("yn", q * 4 + kk)], writes=[pk(m % 2)])
                    S.add("dve", lambda e, m=m, po=po: e.tensor_tensor(out=x[:, m, c0:c0 + n], in0=x[:, m, c0:c0 + n], in1=po[:, 0:n], op=ALU.add),
                          reads=[pk(m % 2), ("x", m, kb)], writes=[("x", m, kb)])

        for sbi in range(int(_os.environ.get("SSD_NB", "8"))):
            ssd_block(sbi)
        S.add("sp", lambda e: e.dma_start(out=ssm_p_out, in_=hstate), reads=[("hst", g) for g in range(8)], dsem="ssm_p_out")
        S.add("sp", lambda e: e.dma_start(out=conv_p_out, in_=carry.rearrange("p j k -> p (j k)")), reads=[("carry", j) for j in range(32)], dsem="conv_p_out")
        S.barrier()
        if _os.environ.get("SSD_NOSAMP"):
            return
        ssd_block(8)
        S.add("sp", lambda e: e.dma_start(out=conv_s_out, in_=convh.rearrange("p j c -> p (j c)")), reads=["convh"], dsem="conv_s_out")

    if do_ffn:
        ffn_phase(0, 0)
    S.barrier()
    if do_ssd:
        ssd_phase()
        out_dsems += ["ssm_p_out", "conv_p_out", "ssm_s_out", "conv_s_out"]
    S.barrier()
    if do_ffn:
        ffn_phase(0, 1)
        ffn_phase(1, 0)
    S.barrier()
    if do_pool:
        pool_phase()
        out_dsems += ["pool_p_out", "pool_s_out"]
    S.barrier()
    if do_ffn:
        ffn_phase(1, 1)
    S.barrier()
    ar = Arena(un, UW)
    sq = [ar.bf16(512), ar.bf16(512)]
    rstd = ar.f32(512)
    yo = [ar.f32(8, 512), ar.f32(8, 512)]
    yTv = yT.rearrange("(k p) t -> p k t", p=128)
    for bi in range(5):
        c0, n = BLOCKS[bi]
        yb = yo[bi % 2]
        rmsnorm(sq, rstd, C_FIN, bi, lambda k, yb=yb, n=n: yb[:, k, 0:n], lambda k, bi=bi: [("yo", bi % 2)], PS[6])
        S.add("sp", lambda e, yb=yb, c0=c0, n=n: e.dma_start(out=yTv[:, :, c0:c0 + n], in_=yb[:, :, 0:n]), reads=[("yo", bi % 2)], dsem="yT")
    out_dsems.append("yT")
    S.emit(out_dsems=out_dsems)
    return nc


def _consts():
    c = np.zeros((128, NCONST), np.float32)
    s = np.arange(128)[:, None]
    l = np.arange(128)[None, :]
    same = (s // 8) == (l // 8)
    c[:, K_UP:K_UP + 128] = (s <= l)
    c[:, K_US:K_US + 128] = (s <= l) & same
    c[:, K_NEGP:K_NEGP + 128] = np.where(l >= s, 0.0, -30000.0)
    c[:, K_NEGS:K_NEGS + 128] = np.where((l >= s) & same, 0.0, -30000.0)
    c[:, K_ONES:K_ONES + 128] = 1.0
    c[:, K_BD:K_BD + 128] = same
    c[:, K_SEQM:K_SEQM + 16] = (np.arange(128)[:, None] // 8) == np.arange(16)[None, :]
    for wi, w in enumerate(POOL_WINDOWS):
        c[:, K_INVC + wi * 16:K_INVC + (wi + 1) * 16] = 1.0 / np.minimum(float(w), np.arange(16) + 1.0)[None, :]
    c[:, K_ID:K_ID + 128] = np.eye(128)
    return c


def _colmajor(v):
    v = np.asarray(v, np.float32)
    return np.ascontiguousarray(v.reshape(-1, 128).T)


_PROG = {}
_WIN_PERM = np.concatenate([np.r_[g * 256:(g + 1) * 256, 2048 + g * 256:2048 + (g + 1) * 256,
                                  4096 + g * 128:4096 + (g + 1) * 128, 5120 + g * 128:5120 + (g + 1) * 128] for g in range(8)]
                           + [np.arange(6144, 6176)])


def kernel(x_prompt, x_sample, state_ssm, state_conv, state_pool,
           ffn_norm, ffn_w_gate, ffn_w_up, ffn_w_down, mix_norm,
           ssd_w_in, ssd_conv_w, ssd_conv_b, ssd_dt_bias, ssd_a_log, ssd_d, ssd_norm, ssd_w_out,
           pool_w_in, pool_w_group, pool_scale, pool_w_out, final_norm, _flags=(True, True, True)):
    f = np.float32
    cols = np.zeros((128, NCOL), f)
    for i in range(2):
        for j in range(2):
            o = C_FN + (i * 2 + j) * 8
            cols[:, o:o + 8] = _colmajor(ffn_norm[i, j])
    for i in range(2):
        cols[:, C_MN + i * 8:C_MN + (i + 1) * 8] = _colmajor(mix_norm[i])
    cols[:, C_FIN:C_FIN + 8] = _colmajor(final_norm)
    for k in range(4):
        cols[:, C_CW + k * 32:C_CW + (k + 1) * 32] = _colmajor(ssd_conv_w[0, k])
    cols[:, C_CB:C_CB + 32] = _colmajor(ssd_conv_b[0])
    cols[:, C_DD:C_DD + 16] = _colmajor(np.repeat(np.asarray(ssd_d[0], f), 64))
    cols[:, C_SN:C_SN + 16] = _colmajor(ssd_norm[0])
    cols[:, C_PS:C_PS + 8] = _colmajor(pool_scale[0])
    rows = np.zeros((128, 64), f)
    rows[:, 0:32] = np.asarray(ssd_dt_bias[0], f)[None, :]
    rows[:, 32:64] = np.asarray(ssd_a_log[0], f)[None, :]
    consts = _consts()

    shared = {
        "cols": cols, "rows": rows, "consts": consts,
        "ffn_w_gate": np.ascontiguousarray(ffn_w_gate, f), "ffn_w_up": np.ascontiguousarray(ffn_w_up, f),
        "ffn_w_down": np.ascontiguousarray(ffn_w_down, f),
        "ssd_w_in": np.ascontiguousarray(np.asarray(ssd_w_in[0], f)[:, _WIN_PERM]), "ssd_w_out": np.ascontiguousarray(ssd_w_out[0], f),
        "pool_w_in": np.ascontiguousarray(pool_w_in[0], f), "pool_w_group": np.ascontiguousarray(pool_w_group[0], f),
        "pool_w_out": np.ascontiguousarray(pool_w_out[0], f),
    }
    in_maps = []
    for c in range(NCORES):
        sl = slice(c * 16, (c + 1) * 16)
        xs = np.asarray(x_sample[sl], f).reshape(128, 1024)
        xT = np.ascontiguousarray(np.concatenate([np.asarray(x_prompt[c], f), xs], axis=0).T)
        ssmT = np.ascontiguousarray(np.asarray(state_ssm[0, sl], f).reshape(16, 8, 256, 128).transpose(1, 3, 0, 2)).reshape(8, 128, 4096)
        convT = np.ascontiguousarray(np.asarray(state_conv[0, sl], f).reshape(16, 3, 32, 128).transpose(3, 2, 0, 1)).reshape(128, 32 * 48)
        poolT = np.ascontiguousarray(np.asarray(state_pool[0, sl], f).transpose(2, 0, 1))
        m = dict(shared)
        m.update({"xT": xT, "ssmT": ssmT, "convT": convT, "poolT": poolT})
        in_maps.append(m)

    key = tuple(_flags)
    if key not in _PROG:
        _PROG[key] = build_program(*_flags)
    nc = _PROG[key]
    res = run_bass_kernel_spmd(nc, in_maps, core_ids=list(range(NCORES)))
    R = res.results

    y_prompt = np.zeros((8, 2048, 1024), f)
    y_sample = np.zeros((128, 8, 1024), f)
    ssm_p = np.zeros((1, 8, 32, 64, 128), f)
    conv_p = np.zeros((1, 8, 3, 4096), f)
    pool_p = np.zeros((1, 8, 15, 1024), f)
    ssm_s = np.zeros((1, 128, 32, 64, 128), f)
    conv_s = np.zeros((1, 128, 3, 4096), f)
    pool_s = np.zeros((1, 128, 15, 1024), f)
    for c in range(NCORES):
        r = R[c]
        yT = np.asarray(r["yT"])
        y_prompt[c] = yT[:, :2048].T
        y_sample[c * 16:(c + 1) * 16] = yT[:, 2048:].T.reshape(16, 8, 1024)
        if "ssm_p_out" in r:
            ssm_p[0, c] = np.asarray(r["ssm_p_out"]).T.reshape(32, 64, 128)
            conv_p[0, c] = np.asarray(r["conv_p_out"]).reshape(128, 32, 3).transpose(2, 1, 0).reshape(3, 4096)
            ssm_s[0, c * 16:(c + 1) * 16] = np.asarray(r["ssm_s_out"]).reshape(8, 128, 16, 256).transpose(2, 0, 3, 1).reshape(16, 32, 64, 128)
            conv_s[0, c * 16:(c + 1) * 16] = np.asarray(r["conv_s_out"]).reshape(128, 32, 16, 3).transpose(2, 3, 1, 0).reshape(16, 3, 4096)
        if "pool_p_out" in r:
            pool_p[0, c] = np.asarray(r["pool_p_out"]).T
            pool_s[0, c * 16:(c + 1) * 16] = np.asarray(r["pool_s_out"]).transpose(1, 2, 0)
    return (y_prompt, y_sample, ssm_p, conv_p, pool_p, ssm_s, conv_s, pool_s)
```
